# Optimizing a Trainium2 kernel written in Bass

```python
import math
import jax, jax.numpy as jnp
from jax import lax
import numpy as np

D_MODEL = 1024
BATCH = 16
SEQ = 2048
DEPTH = 1

PLE_DIM = 256
RMS_EPS = 1e-6
N_BRANCH = 2
S5_WIDTH = 512
S5_GROUP = 16
S5_GROUPS = S5_WIDTH // S5_GROUP
S5_STATE = 64
SSD_WIDTH = 1536
SSD_HEADDIM = 64
SSD_HEADS = SSD_WIDTH // SSD_HEADDIM
SSD_GROUPS = 4
SSD_HPG = SSD_HEADS // SSD_GROUPS
SSD_STATE = 128
SSD_CONV = 4
SSD_CHUNK = 128
SSD_BC = SSD_GROUPS * SSD_STATE
SSD_CONV_DIM = SSD_WIDTH + 2 * SSD_BC
DT_MIN, DT_MAX = 1e-3, 1e-1
_SIZES = (S5_WIDTH, S5_WIDTH, SSD_WIDTH, SSD_CONV_DIM, SSD_HEADS, N_BRANCH * D_MODEL)
IN_PROJ_DIM = int(sum(_SIZES))
SPLITS = tuple(int(v) for v in np.cumsum(_SIZES)[:-1])

kernel_name = "hybrid_s5_ssd_gated_block"


def rms_norm(x, w):
    xf = x.astype(jnp.float32)
    y = xf * lax.rsqrt(jnp.mean(xf * xf, axis=-1, keepdims=True) + RMS_EPS)
    return (y * w.astype(jnp.float32)).astype(x.dtype)


def s5_mixer(u, a_re, a_im, b_re, b_im, c_re, c_im, d, log_step, w_glu, b_glu):
    f32 = jnp.float32
    bsz, L, _ = u.shape
    uf = u.astype(f32).reshape(bsz, L, S5_GROUPS, S5_GROUP)
    a_re = a_re.astype(f32); a_im = a_im.astype(f32)
    step = jnp.exp(log_step.astype(f32))[:, None]
    mag = jnp.exp(a_re * step)
    lb_re = mag * jnp.cos(a_im * step)
    lb_im = mag * jnp.sin(a_im * step)
    den = a_re * a_re + a_im * a_im
    n_re = lb_re - 1.0
    n_im = lb_im
    f_re = (n_re * a_re + n_im * a_im) / den
    f_im = (n_im * a_re - n_re * a_im) / den
    b_re = b_re.astype(f32); b_im = b_im.astype(f32)
    bb_re = f_re[..., None] * b_re - f_im[..., None] * b_im
    bb_im = f_re[..., None] * b_im + f_im[..., None] * b_re
    bu_re = jnp.einsum('gph,blgh->blgp', bb_re, uf)
    bu_im = jnp.einsum('gph,blgh->blgp', bb_im, uf)
    ar = jnp.broadcast_to(lb_re, (1, L, S5_GROUPS, S5_STATE))
    ai = jnp.broadcast_to(lb_im, (1, L, S5_GROUPS, S5_STATE))

    def combine(e1, e2):
        a1r, a1i, b1r, b1i = e1
        a2r, a2i, b2r, b2i = e2
        return (a2r * a1r - a2i * a1i,
                a2r * a1i + a2i * a1r,
                a2r * b1r - a2i * b1i + b2r,
                a2r * b1i + a2i * b1r + b2i)

    _, _, s_re, s_im = lax.associative_scan(combine, (ar, ai, bu_re, bu_im), axis=1)
    y = (jnp.einsum('ghp,blgp->blgh', c_re.astype(f32), s_re)
         - jnp.einsum('ghp,blgp->blgh', c_im.astype(f32), s_im))
    y = y.reshape(bsz, L, S5_WIDTH) + d.astype(f32) * u.astype(f32)
    y = jax.nn.gelu(y)
    y = y * jax.nn.sigmoid(y @ w_glu.astype(f32) + b_glu.astype(f32))
    return y.astype(u.dtype)


def causal_depthwise_conv(x, w, b):
    C = x.shape[-1]
    y = lax.conv_general_dilated(x, w[:, None, :], window_strides=(1,),
                                 padding=[(SSD_CONV - 1, 0)],
                                 dimension_numbers=('NWC', 'WIO', 'NWC'),
                                 feature_group_count=C)
    return y + b


def ssd_mixer(z, xbc, dt_raw, conv_w, conv_b, dt_bias, a_log, d_skip, norm_w):
    f32 = jnp.float32
    bsz, L, _ = xbc.shape
    nc = L // SSD_CHUNK
    G, J, P, N, CH = SSD_GROUPS, SSD_HPG, SSD_HEADDIM, SSD_STATE, SSD_CHUNK
    xbc = jax.nn.silu(causal_depthwise_conv(xbc.astype(f32), conv_w.astype(f32), conv_b.astype(f32)))
    xs, bm, cm = jnp.split(xbc, [SSD_WIDTH, SSD_WIDTH + SSD_BC], axis=-1)
    dt = jax.nn.softplus(dt_raw.astype(f32) + dt_bias.astype(f32))
    a = -jnp.exp(a_log.astype(f32))
    xs = xs.reshape(bsz, nc, CH, G, J, P)
    bm = bm.reshape(bsz, nc, CH, G, N)
    cm = cm.reshape(bsz, nc, CH, G, N)
    dtc = dt.reshape(bsz, nc, CH, G, J)
    xdt = xs * dtc[..., None]
    da = jnp.transpose((dt * a).reshape(bsz, nc, CH, G, J), (0, 1, 3, 4, 2))
    a_cum = jnp.cumsum(da, axis=-1)
    seg = a_cum[..., :, None] - a_cum[..., None, :]
    causal = jnp.tril(jnp.ones((CH, CH), dtype=bool))
    lmat = jnp.exp(jnp.where(causal, seg, -jnp.inf))
    scores = jnp.einsum('bclgn,bcsgn->bcgls', cm, bm)
    wts = scores[:, :, :, None] * lmat
    y_diag = jnp.einsum('bcgjls,bcsgjp->bclgjp', wts, xdt)
    decay_states = jnp.exp(a_cum[..., -1:] - a_cum)
    states = jnp.einsum('bclgn,bcgjl,bclgjp->bcgjpn', bm, decay_states, xdt)
    chunk_decay = jnp.exp(a_cum[..., -1])

    def step(carry, inp):
        dec, st = inp
        return carry * dec[..., None, None] + st, carry

    init = jnp.zeros((bsz, G, J, P, N), f32)
    _, prev = lax.scan(step, init, (jnp.moveaxis(chunk_decay, 1, 0), jnp.moveaxis(states, 1, 0)))
    prev = jnp.moveaxis(prev, 0, 1)
    y_off = jnp.einsum('bclgn,bcgjpn,bcgjl->bclgjp', cm, prev, jnp.exp(a_cum))
    y = y_diag + y_off + xs * d_skip.astype(f32).reshape(G, J)[:, :, None]
    y = y.reshape(bsz, L, SSD_WIDTH)
    yg = (y * jax.nn.silu(z.astype(f32))).reshape(bsz, L, G, SSD_WIDTH // G)
    yg = yg * lax.rsqrt(jnp.mean(yg * yg, axis=-1, keepdims=True) + RMS_EPS)
    y = yg.reshape(bsz, L, SSD_WIDTH) * norm_w.astype(f32)
    return y.astype(z.dtype)


def setup_inputs(seed: int = 0) -> dict:
    key = jax.random.key(seed)
    ks = iter(jax.random.split(key, 40))
    nrm = lambda shape, s: jax.random.normal(next(ks), shape, jnp.float32) * s
    D = D_MODEL
    x = jax.random.normal(next(ks), (BATCH, SEQ, D), jnp.float32)
    p = jax.random.normal(next(ks), (DEPTH, BATCH, SEQ, PLE_DIM), jnp.float32)
    norm_w = 1.0 + nrm((DEPTH, D), 0.02)
    w_in = nrm((DEPTH, D, IN_PROJ_DIM), D ** -0.5)
    n_idx = jnp.arange(S5_STATE, dtype=jnp.float32)
    s5_a_re = -0.5 + nrm((DEPTH, S5_GROUPS, S5_STATE), 0.01)
    s5_a_im = math.pi * n_idx + nrm((DEPTH, S5_GROUPS, S5_STATE), 0.01)
    s5_b_re = nrm((DEPTH, S5_GROUPS, S5_STATE, S5_GROUP), (2 * S5_GROUP) ** -0.5)
    s5_b_im = nrm((DEPTH, S5_GROUPS, S5_STATE, S5_GROUP), (2 * S5_GROUP) ** -0.5)
    s5_c_re = nrm((DEPTH, S5_GROUPS, S5_GROUP, S5_STATE), S5_STATE ** -0.5)
    s5_c_im = nrm((DEPTH, S5_GROUPS, S5_GROUP, S5_STATE), S5_STATE ** -0.5)
    s5_d = nrm((DEPTH, S5_WIDTH), 1.0)
    s5_log_step = jax.random.uniform(next(ks), (DEPTH, S5_GROUPS), jnp.float32,
                                     math.log(DT_MIN), math.log(DT_MAX))
    s5_w_glu = nrm((DEPTH, S5_WIDTH, S5_WIDTH), S5_WIDTH ** -0.5)
    s5_b_glu = nrm((DEPTH, S5_WIDTH), 0.01)
    ssd_conv_w = nrm((DEPTH, SSD_CONV, SSD_CONV_DIM), SSD_CONV ** -0.5)
    ssd_conv_b = nrm((DEPTH, SSD_CONV_DIM), 0.01)
    dt0 = jnp.exp(jax.random.uniform(next(ks), (DEPTH, SSD_HEADS), jnp.float32,
                                     math.log(DT_MIN), math.log(DT_MAX)))
    ssd_dt_bias = dt0 + jnp.log(-jnp.expm1(-dt0))
    ssd_a_log = jnp.log(jax.random.uniform(next(ks), (DEPTH, SSD_HEADS), jnp.float32, 1.0, 16.0))
    ssd_d = 1.0 + nrm((DEPTH, SSD_HEADS), 0.1)
    ssd_norm_w = 1.0 + nrm((DEPTH, SSD_WIDTH), 0.02)
    w_br_s5 = nrm((DEPTH, S5_WIDTH, D), S5_WIDTH ** -0.5)
    w_br_ssd = nrm((DEPTH, SSD_WIDTH, D), SSD_WIDTH ** -0.5)
    w_out = nrm((DEPTH, D, D), D ** -0.5)
    ple_norm_w = 1.0 + nrm((DEPTH, D), 0.02)
    w_ple_gate = nrm((DEPTH, D, D), D ** -0.5)
    w_ple_proj = nrm((DEPTH, PLE_DIM, D), PLE_DIM ** -0.5)
    final_norm_w = 1.0 + nrm((D,), 0.02)
    return {"x": x, "p": p, "norm_w": norm_w, "w_in": w_in,
            "s5_a_re": s5_a_re, "s5_a_im": s5_a_im, "s5_b_re": s5_b_re, "s5_b_im": s5_b_im,
            "s5_c_re": s5_c_re, "s5_c_im": s5_c_im, "s5_d": s5_d, "s5_log_step": s5_log_step,
            "s5_w_glu": s5_w_glu, "s5_b_glu": s5_b_glu,
            "ssd_conv_w": ssd_conv_w, "ssd_conv_b": ssd_conv_b, "ssd_dt_bias": ssd_dt_bias,
            "ssd_a_log": ssd_a_log, "ssd_d": ssd_d, "ssd_norm_w": ssd_norm_w,
            "w_br_s5": w_br_s5, "w_br_ssd": w_br_ssd, "w_out": w_out,
            "ple_norm_w": ple_norm_w, "w_ple_gate": w_ple_gate, "w_ple_proj": w_ple_proj,
            "final_norm_w": final_norm_w}


def reference(x, p, norm_w, w_in, s5_a_re, s5_a_im, s5_b_re, s5_b_im, s5_c_re, s5_c_im,
              s5_d, s5_log_step, s5_w_glu, s5_b_glu, ssd_conv_w, ssd_conv_b, ssd_dt_bias,
              ssd_a_log, ssd_d, ssd_norm_w, w_br_s5, w_br_ssd, w_out, ple_norm_w,
              w_ple_gate, w_ple_proj, final_norm_w):
    h = x
    for i in range(DEPTH):
        hn = rms_norm(h, norm_w[i])
        proj = hn @ w_in[i]
        s5_u, s5_z, ssd_z, ssd_xbc, ssd_dt, gate_logits = jnp.split(proj, SPLITS, axis=-1)
        y5 = s5_mixer(s5_u, s5_a_re[i], s5_a_im[i], s5_b_re[i], s5_b_im[i], s5_c_re[i],
                      s5_c_im[i], s5_d[i], s5_log_step[i], s5_w_glu[i], s5_b_glu[i])
        y5 = y5 * jax.nn.silu(s5_z)
        yss = ssd_mixer(ssd_z, ssd_xbc, ssd_dt, ssd_conv_w[i], ssd_conv_b[i], ssd_dt_bias[i],
                        ssd_a_log[i], ssd_d[i], ssd_norm_w[i])
        g5, gss = jnp.split(jax.nn.sigmoid(gate_logits), N_BRANCH, axis=-1)
        merged = g5 * (y5 @ w_br_s5[i]) + gss * (yss @ w_br_ssd[i])
        h = h + merged @ w_out[i]
        ple_gate = jax.nn.sigmoid(rms_norm(h, ple_norm_w[i]) @ w_ple_gate[i])
        h = h + ple_gate * (p[i] @ w_ple_proj[i])
    return rms_norm(h, final_norm_w)
```

```python
import math
from contextlib import ExitStack

import numpy as np
import ml_dtypes
import concourse.bass as bass
import concourse.mybir as mybir
from concourse.bass_utils import run_bass_kernel_spmd

F32 = mybir.dt.float32
BF16 = mybir.dt.bfloat16
I32 = mybir.dt.int32
AF = mybir.ActivationFunctionType
ALU = mybir.AluOpType

TB = 256
NWSL = 4
PI = math.pi
TWO_PI = 2.0 * math.pi


class _Op:
    __slots__ = ("eng", "fn", "reads", "writes", "deps", "sig", "dma_sem", "dma_val", "is_dma", "waits")

    def __init__(self, eng, fn, reads, writes, is_dma):
        self.eng = eng
        self.fn = fn
        self.reads = reads
        self.writes = writes
        self.deps = []
        self.sig = None
        self.is_dma = is_dma
        self.dma_sem = None
        self.dma_val = None
        self.waits = []


class Prog:
    ENGS = ("pe", "act", "dve", "pool", "sp")

    def __init__(self, nc, n_dma_sems=40, same_engine_sync=True):
        self.nc = nc
        self.ops = []
        self.last_writer = {}
        self.readers = {}
        self.n_dma_sems = n_dma_sems
        self.same_engine_sync = same_engine_sync
        self.alias = {}

    def _expand(self, keys):
        out = []
        for k in keys:
            a = self.alias.get(k)
            if a is None:
                out.append(k)
            else:
                out.extend(a)
        return tuple(out)

    def _add(self, op):
        op.reads = self._expand(op.reads)
        op.writes = self._expand(op.writes)
        deps = set()
        for r in op.reads:
            w = self.last_writer.get(r)
            if w is not None:
                deps.add(w)
        for r in op.writes:
            w = self.last_writer.get(r)
            if w is not None and (op.is_dma or self.ops[w].is_dma or self.ops[w].eng != op.eng):
                deps.add(w)
            for rd in self.readers.get(r, {}).values():
                if op.is_dma or self.ops[rd].is_dma or self.ops[rd].eng != op.eng:
                    deps.add(rd)
        idx = len(self.ops)
        op.deps = sorted(deps)
        self.ops.append(op)
        for r in op.writes:
            self.last_writer[r] = idx
            self.readers[r] = {}
        for r in op.reads:
            if r not in op.writes:
                k = ("dma", idx) if op.is_dma else op.eng
                self.readers.setdefault(r, {})[k] = idx
        return idx

    def op(self, eng, fn, reads=(), writes=()):
        return self._add(_Op(eng, fn, tuple(reads), tuple(writes), False))

    def dma(self, queue, out, in_, reads=(), writes=(), **kw):
        def fn(e, out=out, in_=in_, kw=kw):
            return e.dma_start(out=out, in_=in_, allow_slow_non_contiguous=True, **kw)
        return self._add(_Op(queue, fn, tuple(reads), tuple(writes), True))

    def emit(self, final_wait_ops=()):
        nc = self.nc
        ops = self.ops
        needed = [False] * len(ops)
        for o in ops:
            for d in o.deps:
                needed[d] = True
        for i in final_wait_ops:
            needed[i] = True
        sig_cnt = {e: 0 for e in self.ENGS}
        dma_cnt = [0] * self.n_dma_sems
        half = self.n_dma_sems // 2
        pools = {"sp": list(range(0, half)), "pool": list(range(half, self.n_dma_sems))}
        dma_rr = {"sp": 0, "pool": 0}
        for i, o in enumerate(ops):
            if o.is_dma:
                pl = pools[o.eng]
                o.dma_sem = pl[dma_rr[o.eng] % len(pl)]
                dma_rr[o.eng] += 1
                dma_cnt[o.dma_sem] += 1
                o.dma_val = 16 * dma_cnt[o.dma_sem]
            elif needed[i]:
                sig_cnt[o.eng] += 1
                o.sig = sig_cnt[o.eng]
        self.sig_cnt = sig_cnt
        waited = {e: {} for e in self.ENGS}
        for i, o in enumerate(ops):
            need = {}
            if o.is_dma and o.dma_val > 16:
                need[("dma", o.dma_sem)] = o.dma_val - 16
            for d in o.deps:
                p = ops[d]
                if p.is_dma:
                    k = ("dma", p.dma_sem)
                    v = p.dma_val
                else:
                    if p.eng == o.eng and (p.eng == "pe" or not self.same_engine_sync):
                        continue
                    k = ("eng", p.eng)
                    v = p.sig
                if need.get(k, 0) < v:
                    need[k] = v
            w = waited[o.eng]
            for k, v in need.items():
                if w.get(k, 0) < v:
                    w[k] = v
                    o.waits.append((k, v))
        final = []
        for i in final_wait_ops:
            p = ops[i]
            if p.is_dma:
                final.append((("dma", p.dma_sem), p.dma_val))
            else:
                final.append((("eng", p.eng), p.sig))
        with ExitStack() as st:
            sems = {}
            for e in self.ENGS:
                sems[("eng", e)] = st.enter_context(nc.semaphore("s_" + e))
            for j in range(self.n_dma_sems):
                sems[("dma", j)] = st.enter_context(nc.semaphore("s_dma%d" % j))
            block = st.enter_context(nc.Block())
            per = {e: [o for o in ops if o.eng == e] for e in self.ENGS}

            def run(e, engobj):
                for o in per[e]:
                    for k, v in o.waits:
                        engobj.wait_ge(sems[k], v)
                    ins = o.fn(engobj)
                    if o.is_dma:
                        ins.then_inc(sems[("dma", o.dma_sem)], 16)
                    elif o.sig is not None:
                        ins.then_inc(sems[("eng", e)], 1)
                if e == "sp":
                    for k, v in final:
                        engobj.wait_ge(sems[k], v)

            @block.tensor
            def _(eng):
                run("pe", eng)

            @block.scalar
            def _(eng):
                run("act", eng)

            @block.vector
            def _(eng):
                run("dve", eng)

            @block.gpsimd
            def _(eng):
                run("pool", eng)

            @block.sync
            def _(eng):
                run("sp", eng)


def _consts():
    bf = ml_dtypes.bfloat16
    c = {}
    c["c_identb"] = np.eye(128, dtype=np.float32).astype(bf)
    c["c_identf"] = np.eye(128, dtype=np.float32)
    c["c_onesb"] = np.ones((128, 128), np.float32).astype(bf)
    c["c_onesf"] = np.ones((128, 128), np.float32)
    sel = np.zeros((128, 64, 128), np.float32)
    selT = np.zeros((128, 64, 128), np.float32)
    for g8 in range(8):
        for j in range(8):
            for h in range(16):
                sel[g8 * 16 + h, g8 * 8 + j, j * 16 + h] = 1.0
                selT[j * 16 + h, j * 8 + g8, g8 * 16 + h] = 1.0
    c["c_sel"] = sel.reshape(128, 64 * 128).astype(bf)
    c["c_negbig"] = (-30000.0 * np.eye(128, dtype=np.float32)).astype(bf)
    k = np.arange(128)[:, None]
    l = np.arange(128)[None, :]
    gt = (k > l).astype(np.float32)
    c["c_gt"] = np.concatenate([gt, gt], axis=1).astype(bf)
    oh = np.zeros((24, 24, 128), np.float32)
    for hd in range(24):
        oh[hd, hd, :] = 1.0
    c["c_onehot"] = oh.reshape(24, 24 * 128).astype(bf)
    rm = np.ones((24, TB), np.float32)
    rm[:, 0::128] = 0.0
    c["c_rmask"] = rm
    jp = (np.arange(128) // 16)[:, None]
    jj = (np.arange(128) // 16)[None, :]
    c["c_bmask"] = (jj >= jp).astype(np.float32)
    kv = np.zeros((128, 16, 32), np.float32)
    for ki in range(16):
        kv[:, ki, :] = ki - 7
    c["c_kv"] = kv.reshape(128, 512)
    cv = np.zeros((128, 32, 32), np.float32)
    cpos = np.ones((128, 32, 32), np.float32)
    for cc in range(32):
        cv[:, :, cc] = 8.0 * cc
    cpos[:, :, 0] = 0.0
    c["c_cv"] = cv.reshape(128, 1024)
    c["c_cpos"] = cpos.reshape(128, 1024)
    sg = np.zeros((128, 2), np.float32)
    sg[:64, 0] = -1.0
    sg[64:, 0] = 1.0
    sg[:64, 1] = 1.0
    sg[64:, 1] = -1.0
    c["c_sg"] = sg
    return c


_CONST_SHAPES = None


def build(NSEQ, NBLK, stop=None, dbg=False):
    NB = NSEQ * NBLK
    NTOK = NB * TB
    nc = bass.Bass("TRN2", target_bir_lowering=False)

    def din(name, shape, dt=F32):
        return nc.dram_tensor(name, list(shape), dt, kind="ExternalInput").ap()

    x_d = din("x", [NTOK, 1024])
    p_d = din("p", [NTOK, 256])
    out_d = nc.dram_tensor("out", [NTOK, 1024], F32, kind="ExternalOutput").ap()
    norm_w_d = din("norm_w", [8, 128])
    w_in_d = din("w_in", [1024, 7192])
    a_re_d = din("s5_a_re", [32, 64])
    a_im_d = din("s5_a_im", [32, 64])
    b_re_d = din("s5_b_re", [32, 64, 16])
    b_im_d = din("s5_b_im", [32, 64, 16])
    c_re_d = din("s5_c_re", [512, 64])
    c_im_d = din("s5_c_im", [512, 64])
    s5_d_d = din("s5_d", [4, 128])
    lstep_d = din("s5_log_step", [32])
    w_glu_d = din("s5_w_glu", [512, 512])
    b_glu_d = din("s5_b_glu", [4, 128])
    conv_w_d = din("ssd_conv_w", [80, 128])
    conv_b_d = din("ssd_conv_b", [20, 128])
    dt_bias_d = din("ssd_dt_bias", [24, 1])
    a_log_d = din("ssd_a_log", [24, 1])
    ssd_d_d = din("ssd_d", [24])
    ssd_nw_d = din("ssd_norm_w", [12, 128])
    w_br5_d = din("w_br_s5", [512, 1024])
    w_brs_d = din("w_br_ssd", [1536, 1024])
    w_out_d = din("w_out", [1024, 1024])
    ple_nw_d = din("ple_norm_w", [8, 128])
    w_pg_d = din("w_ple_gate", [1024, 1024])
    w_pp_d = din("w_ple_proj", [256, 1024])
    fnw_d = din("final_norm_w", [1024])
    cd = {}
    for name, arr in _consts().items():
        cd[name] = din(name, arr.shape, BF16 if arr.dtype == ml_dtypes.bfloat16 else F32)

    scr = {
        "w_in": nc.dram_tensor("scr_w_in", [1024, 7192], BF16, kind="ExternalOutput").ap(),
        "w_brs": nc.dram_tensor("scr_w_brs", [1536, 1024], BF16, kind="ExternalOutput").ap(),
        "w_out": nc.dram_tensor("scr_w_out", [1024, 1024], BF16, kind="ExternalOutput").ap(),
        "w_pg": nc.dram_tensor("scr_w_pg", [1024, 1024], BF16, kind="ExternalOutput").ap(),
    }

    with ExitStack() as st:
        def sb(name, shape, dt=F32):
            return st.enter_context(nc.sbuf_tensor(name, list(shape), dt))

        def psum(name, shape, dt):
            return st.enter_context(nc.psum_tensor(name, list(shape), dt))

        P = Prog(nc)
        psf = [psum("psf%d" % i, [128, 512], F32) for i in range(4)]
        pst = [psum("pst%d" % i, [128, 1024], BF16) for i in range(2)]
        pscs = [psum("psc%d" % i, [128, 512], F32) for i in range(2)]
        st_ = {"f": 0, "t": 0, "w": 0, "cast": 0}

        def nbf():
            i = st_["f"]
            st_["f"] = (i + 1) % 4
            return psf[i], "psf%d" % i

        def nbt():
            i = st_["t"]
            st_["t"] = (i + 1) % 2
            return pst[i], "pst%d" % i

        def mm(out, lhsT, rhs, start, stop, reads, writes):
            P.op("pe", lambda e: e.matmul(out, lhsT=lhsT, rhs=rhs, start=start, stop=stop), reads, writes)

        def tr(out, in_, ident, reads, writes):
            P.op("pe", lambda e: e.transpose(out=out, in_=in_, identity=ident), reads, writes)

        def act(out, in_, func, reads, writes, bias=None, scale=None, accum=None):
            kw = {}
            if bias is not None:
                kw["bias"] = bias
            if scale is not None:
                kw["scale"] = scale
            if accum is not None:
                kw["accum_out"] = accum
            P.op("act", lambda e: e.activation(out=out, in_=in_, func=func, **kw), reads, writes)

        def eng_of(eng):
            return eng

        def tt(eng, out, in0, in1, op, reads, writes):
            P.op(eng, lambda e: e.tensor_tensor(out=out, in0=in0, in1=in1, op=op), reads, writes)

        def ts(eng, out, in0, s1, s2, op0, op1, reads, writes):
            if op1 is None:
                P.op(eng, lambda e: e.tensor_scalar(out=out, in0=in0, scalar1=s1, scalar2=None, op0=op0), reads, writes)
            else:
                P.op(eng, lambda e: e.tensor_scalar(out=out, in0=in0, scalar1=s1, scalar2=s2, op0=op0, op1=op1), reads, writes)

        def stt(eng, out, in0, scalar, in1, op0, op1, reads, writes):
            P.op(eng, lambda e: e.scalar_tensor_tensor(out=out, in0=in0, scalar=scalar, in1=in1, op0=op0, op1=op1), reads, writes)

        def cp(eng, out, in_, reads, writes):
            if eng == "act":
                P.op("act", lambda e: e.copy(out=out, in_=in_), reads, writes)
            else:
                P.op(eng, lambda e: e.tensor_copy(out=out, in_=in_), reads, writes)

        def mset(eng, out, val, writes):
            P.op(eng, lambda e: e.memset(out, val), (), writes)

        identb = sb("identb", [128, 128], BF16)
        identf = sb("identf", [128, 128], F32)
        onesb = sb("onesb", [128, 128], BF16)
        onesf = sb("onesf", [128, 128], F32)
        sel = sb("sel", [128, 64, 128], BF16)
        negbig = sb("negbig", [128, 128], BF16)
        gtm = sb("gtm", [128, 256], BF16)
        onehot = sb("onehot", [24, 24, 128], BF16)
        rmask = sb("rmask", [24, TB], F32)
        sgc = sb("sgc", [128, 2], F32)
        for t_, d_ in ((identb, "c_identb"), (identf, "c_identf"), (onesb, "c_onesb"), (onesf, "c_onesf"),
                       (negbig, "c_negbig"), (gtm, "c_gt"), (rmask, "c_rmask"), (sgc, "c_sg")):
            P.dma("sp", t_[:], cd[d_], writes=[d_])
        P.dma("sp", sel[:].rearrange("p a b -> p (a b)"), cd["c_sel"], writes=["c_sel"])
        P.dma("sp", onehot[:].rearrange("p a b -> p (a b)"), cd["c_onehot"], writes=["c_onehot"])

        xt = [sb("xt%d" % i, [128, 2, 1024], F32) for i in range(2)]
        xnb = [sb("xnb%d" % i, [128, 1024], BF16) for i in range(2)]
        hnT = sb("hnT", [128, 8, TB], BF16)
        wsl = [sb("wsl%d" % i, [128, 8, 256], BF16) for i in range(NWSL)]
        wglu = sb("wglu", [128, 4, 512], BF16)
        wbr5 = sb("wbr5", [128, 4, 1024], BF16)
        wpp = sb("wpp", [128, 2, 1024], BF16)
        vecT = sb("vecT", [128, 136], F32)
        fnw = sb("fnw", [128, 1024], F32)
        dtb = sb("dtb", [24, 1], F32)
        aneg = sb("aneg", [24, 1], F32)
        dch = sb("dch", [128, 12], F32)
        dsk = sb("dsk", [128, 12, 128], BF16)
        WA = sb("WA", [128, 32, 128], BF16)
        TOE = sb("TOE", [128, 32, 128], BF16)
        WC = sb("WC", [128, 32, 128], BF16)
        COSM = sb("COSM", [128, 32, 32], F32)
        SGNM = sb("SGNM", [128, 32, 32], F32)
        RHO0 = sb("RHO0", [128, 32, 32], F32)
        L8RE = sb("L8RE", [128, 32], F32)
        L8SG = sb("L8SG", [128, 32], F32)
        XCS = sb("XCS", [128, 32], F32)
        XCW = sb("XCW", [128, 32], F32)
        ssq = sb("ssq", [128, 8], F32)
        rst = sb("rst", [128, 8], F32)
        ARENA_W = 15 * 1024 + 512
        arena = sb("arena", [128, ARENA_W], F32)

        class Arena:
            def __init__(self):
                self.off = 0

            def reset(self):
                self.off = 0

            def get(self, shape, dt, name, parts=None):
                n = 1
                for s_ in shape[1:]:
                    n *= s_
                words = (n * (2 if dt == BF16 else 4) + 3) // 4
                assert self.off + words <= ARENA_W, (name, self.off, words)
                g0 = self.off // 256
                g1 = (self.off + words - 1) // 256
                P.alias[name] = [("ar", k) for k in range(g0, g1 + 1)]
                if parts:
                    for pi in range(parts):
                        w0 = self.off + (words * pi) // parts
                        w1 = self.off + (words * (pi + 1)) // parts - 1
                        P.alias[(name, pi)] = [("ar", k) for k in range(w0 // 256, w1 // 256 + 1)]
                a = arena[0:shape[0], self.off:self.off + words]
                if dt != F32:
                    a = a.bitcast(dt)
                self.off += words
                if len(shape) == 3:
                    a = a.rearrange("p (a b) -> p a b", a=shape[1])
                elif len(shape) == 4:
                    a = a.rearrange("p (a b c) -> p a b c", a=shape[1], b=shape[2])
                return a

        AR = Arena()

        vs0 = AR.get([128, 128], F32, "vs0")
        vs1 = AR.get([128, 128], F32, "vs1")
        mset("dve", vs0, 0.0, ["vs0"])
        mset("dve", vs1, 0.0, ["vs1"])
        P.dma("sp", vs0[0:80, :], conv_w_d, reads=["vs0"], writes=["vs0"])
        P.dma("sp", vs0[80:100, :], conv_b_d, reads=["vs0"], writes=["vs0"])
        P.dma("sp", vs0[100:112, :], ssd_nw_d, reads=["vs0"], writes=["vs0"])
        P.dma("sp", vs0[112:120, :], norm_w_d, reads=["vs0"], writes=["vs0"])
        P.dma("sp", vs0[120:128, :], ple_nw_d, reads=["vs0"], writes=["vs0"])
        P.dma("sp", vs1[0:4, :], s5_d_d, reads=["vs1"], writes=["vs1"])
        P.dma("sp", vs1[4:8, :], b_glu_d, reads=["vs1"], writes=["vs1"])
        bk, bkk = nbf()
        tr(bk[:, 0:128], vs0, identf[:], ["vs0", "c_identf"], [bkk])
        tr(bk[:, 128:136], vs1[0:8, :], identf[0:8, 0:8], ["vs1", "c_identf"], [bkk])
        cp("dve", vecT[:], bk[:, 0:136], [bkk], ["vecT"])
        CW = lambda i, k: vecT[:, k * 20 + i:k * 20 + i + 1]
        CB = lambda i: vecT[:, 80 + i:81 + i]
        SNW = lambda i: vecT[:, 100 + i:101 + i]
        NW = lambda kt: vecT[:, 112 + kt:113 + kt]
        PNW = lambda kt: vecT[:, 120 + kt:121 + kt]
        D5 = lambda ft: vecT[:, 128 + ft:129 + ft]
        BGL = lambda ft: vecT[:, 132 + ft:133 + ft]
        with nc.allow_non_contiguous_dma(reason="tiny param loads"):
            P.dma("sp", fnw[:], fnw_d.partition_broadcast(128), writes=["fnw"])
            P.dma("sp", dtb[:], dt_bias_d, writes=["dtb"])
            P.dma("sp", aneg[:], a_log_d, writes=["aneg"])
            sdv = ssd_d_d.rearrange("(i two) -> two i", two=2)
            P.dma("sp", dch[0:64, :], sdv[0].partition_broadcast(64), writes=["dch"])
            P.dma("sp", dch[64:128, :], sdv[1].partition_broadcast(64), reads=["dch"], writes=["dch"])
        act(aneg[:], aneg[:], AF.Exp, ["aneg"], ["aneg"])
        ts("dve", aneg[:], aneg[:], -1.0, None, ALU.mult, None, ["aneg"], ["aneg"])
        for i in range(12):
            ts("dve", dsk[:, i, :], identb[:], dch[:, i:i + 1], None, ALU.mult, None, ["c_identb", "dch"], ["dsk"])

        AR.reset()
        A2 = lambda n, s, d=F32, parts=None: AR.get(s, d, n, parts)
        SKIP_S5 = stop in ('S0',)
        SKIP_W = stop in ('S0', 'S1')
        bmaskt = sb("bmaskt", [128, 128], F32)
        P.dma("sp", bmaskt[:], cd["c_bmask"], writes=["bmask"])
        y5T = sb("y5T", [128, 4, TB], BF16)
        yssT = sb("yssT", [128, 12, TB], BF16)
        prevS = sb("prevS", [128, 1536], F32)
        hist = sb("hist", [128, 20, 4], BF16)

        def _s5_setup():
            aTre = A2("aTre", [128, 32]); aTim = A2("aTim", [128, 32]); stp = A2("stp", [128, 32])
            ars = A2("ars", [128, 32]); ais = A2("ais", [128, 32]); r8 = A2("r8", [128, 32])
            anat = A2("anat", [32, 2, 128])
            for h in range(2):
                P.dma("sp", anat[:, 0, 64 * h:64 * h + 64], a_re_d, reads=["anat"] if h else [], writes=["anat"])
                P.dma("sp", anat[:, 1, 64 * h:64 * h + 64], a_im_d, reads=["anat"], writes=["anat"])
            bka, bkak = nbf()
            tr(bka[:, 0:32], anat[:, 0, :], identf[0:32, 0:32], ["anat", "c_identf"], [bkak])
            tr(bka[:, 32:64], anat[:, 1, :], identf[0:32, 0:32], ["anat", "c_identf"], [bkak])
            cp("dve", aTre, bka[:, 0:32], [bkak], ["aTre"])
            cp("dve", aTim, bka[:, 32:64], [bkak], ["aTim"])
            P.dma("sp", stp, lstep_d.partition_broadcast(128), writes=["stp"])
            act(stp, stp, AF.Exp, ["stp"], ["stp"])
            tt("dve", ars, aTre, stp, ALU.mult, ["aTre", "stp"], ["ars"])
            tt("dve", ais, aTim, stp, ALU.mult, ["aTim", "stp"], ["ais"])

            def range_reduce(eng, xin, shape, key, tmpname):
                y = AR.get(shape, F32, tmpname + "y")
                yi = AR.get(shape, I32, tmpname + "i")
                ky, ki_ = tmpname + "y", tmpname + "i"
                ts(eng, y, xin, 1.0 / TWO_PI, None, ALU.mult, None, [key], [ky])
                cp(eng, yi, y, [ky], [ki_])
                cp(eng, y, yi, [ki_], [ky])
                stt(eng, xin, y, -TWO_PI, xin, ALU.mult, ALU.add, [ky, key], [key])
                ts(eng, y, xin, PI, None, ALU.is_gt, None, [key], [ky])
                stt(eng, xin, y, -TWO_PI, xin, ALU.mult, ALU.add, [ky, key], [key])
                ts(eng, y, xin, -PI, None, ALU.is_lt, None, [key], [ky])
                stt(eng, xin, y, TWO_PI, xin, ALU.mult, ALU.add, [ky, key], [key])
                ts(eng, xin, xin, -PI, PI, ALU.max, ALU.min, [key], [key])

            mark0 = AR.off
            cvt = A2("cvt", [128, 32, 32]); cpos = A2("cpos", [128, 32, 32])
            AM = A2("AM", [128, 32, 32]); AM2 = A2("AM2", [128, 32, 32])
            P.dma("sp", cvt.rearrange("p a b -> p (a b)"), cd["c_cv"], writes=["cvt"])
            P.dma("sp", cpos.rearrange("p a b -> p (a b)"), cd["c_cpos"], writes=["cpos"])
            bcc = lambda a: a.unsqueeze(2).to_broadcast([128, 32, 32])
            tt("dve", AM, cvt, bcc(ais), ALU.mult, ["cvt", "ais"], ["AM"])
            ts("dve", AM2, AM, PI / 2, None, ALU.add, None, ["AM"], ["AM2"])
            mk = AR.off
            range_reduce("dve", AM, [128, 32, 32], "AM", "rra")
            AR.off = mk
            range_reduce("dve", AM2, [128, 32, 32], "AM2", "rra")
            act(AM, AM, AF.Sin, ["AM"], ["AM"])
            act(COSM[:], AM2, AF.Sin, ["AM2"], ["COSM"])
            ts("dve", SGNM[:], AM, sgc[:, 1:2], None, ALU.mult, None, ["AM", "c_sg"], ["SGNM"])
            act(r8, ars, AF.Exp, ["ars"], ["r8"], scale=8.0)
            tt("dve", RHO0[:], cpos, bcc(r8), ALU.mult, ["cpos", "r8"], ["RHO0"])
            AR.off = mark0
            kvt = A2("kvt", [128, 16, 32])
            P.dma("sp", kvt.rearrange("p a b -> p (a b)"), cd["c_kv"], writes=["kvt"])
            MAG = A2("MAG", [128, 16, 32]); ANG = A2("ANG", [128, 16, 32]); ANG2 = A2("ANG2", [128, 16, 32])
            PRE = A2("PRE", [128, 16, 32]); PIM = A2("PIM", [128, 16, 32]); PIMS = A2("PIMS", [128, 16, 32])
            bc16 = lambda a: a.unsqueeze(1).to_broadcast([128, 16, 32])
            tt("dve", MAG, kvt, bc16(ars), ALU.mult, ["kvt", "ars"], ["MAG"])
            act(MAG, MAG, AF.Exp, ["MAG"], ["MAG"])
            tt("dve", ANG, kvt, bc16(ais), ALU.mult, ["kvt", "ais"], ["ANG"])
            ts("dve", ANG2, ANG, PI / 2, None, ALU.add, None, ["ANG"], ["ANG2"])
            mk = AR.off
            range_reduce("dve", ANG, [128, 16, 32], "ANG", "rrb")
            AR.off = mk
            range_reduce("dve", ANG2, [128, 16, 32], "ANG2", "rrb")
            AR.off = mk
            act(ANG, ANG, AF.Sin, ["ANG"], ["ANG"])
            act(ANG2, ANG2, AF.Sin, ["ANG2"], ["ANG2"])
            tt("dve", PRE, MAG, ANG2, ALU.mult, ["MAG", "ANG2"], ["PRE"])
            tt("dve", PIM, MAG, ANG, ALU.mult, ["MAG", "ANG"], ["PIM"])
            ts("dve", PIMS, PIM, sgc[:, 0:1], None, ALU.mult, None, ["PIM", "c_sg"], ["PIMS"])
            cp("dve", L8RE[:], PRE[:, 15, :], ["PRE"], ["L8RE"])
            cp("dve", L8SG[:], PIMS[:, 15, :], ["PIMS"], ["L8SG"])
            nre = A2("nre", [128, 32]); den = A2("den", [128, 32]); t0 = A2("t0", [128, 32]); t1 = A2("t1", [128, 32])
            fre = A2("fre", [128, 32]); fim = A2("fim", [128, 32])
            ts("dve", nre, PRE[:, 8, :], -1.0, None, ALU.add, None, ["PRE"], ["nre"])
            tt("dve", den, aTre, aTre, ALU.mult, ["aTre"], ["den"])
            tt("dve", t0, aTim, aTim, ALU.mult, ["aTim"], ["t0"])
            tt("dve", den, den, t0, ALU.add, ["den", "t0"], ["den"])
            P.op("dve", lambda e: e.reciprocal(out=den, in_=den), P._expand(["den"]), P._expand(["den"]))
            tt("dve", t0, nre, aTre, ALU.mult, ["nre", "aTre"], ["t0"])
            tt("dve", t1, PIM[:, 8, :], aTim, ALU.mult, ["PIM", "aTim"], ["t1"])
            tt("dve", t0, t0, t1, ALU.add, ["t0", "t1"], ["t0"])
            tt("dve", fre, t0, den, ALU.mult, ["t0", "den"], ["fre"])
            tt("dve", t0, PIM[:, 8, :], aTre, ALU.mult, ["PIM", "aTre"], ["t0"])
            tt("dve", t1, nre, aTim, ALU.mult, ["nre", "aTim"], ["t1"])
            tt("dve", t0, t0, t1, ALU.subtract, ["t0", "t1"], ["t0"])
            tt("dve", fim, t0, den, ALU.mult, ["t0", "den"], ["fim"])
            bre = A2("bre", [128, 32, 16]); bim = A2("bim", [128, 32, 16])
            BBa = A2("BBa", [128, 32, 16]); BBb = A2("BBb", [128, 32, 16])
            u0 = A2("u0", [128, 32, 16]); u1 = A2("u1", [128, 32, 16])
            with nc.allow_non_contiguous_dma(reason="param loads"):
                for h in range(2):
                    for g4 in range(4):
                        gsl = slice(g4 * 8, g4 * 8 + 8)
                        P.dma("sp", bre[64 * h:64 * h + 64, gsl, :], b_re_d[gsl].rearrange("g p h -> p g h"), reads=["bre"], writes=["bre"])
                        P.dma("sp", bim[64 * h:64 * h + 64, gsl, :], b_im_d[gsl].rearrange("g p h -> p g h"), reads=["bim"], writes=["bim"])
            bch = lambda a: a.unsqueeze(2).to_broadcast([128, 32, 16])
            tt("dve", u0, bre, bch(fre), ALU.mult, ["bre", "fre"], ["u0"])
            tt("dve", u1, bim, bch(fim), ALU.mult, ["bim", "fim"], ["u1"])
            tt("dve", u0, u0, u1, ALU.subtract, ["u0", "u1"], ["u0"])
            tt("dve", u1, bim, bch(fre), ALU.mult, ["bim", "fre"], ["u1"])
            tt("dve", BBb, bre, bch(fim), ALU.mult, ["bre", "fim"], ["BBb"])
            tt("dve", u1, u1, BBb, ALU.add, ["u1", "BBb"], ["u1"])
            cp("dve", BBa[0:64], u0[0:64], ["u0"], ["BBa"])
            cp("dve", BBa[64:128], u1[64:128], ["u1", "BBa"], ["BBa"])
            cp("dve", BBb[0:64], u1[0:64], ["u1", "BBb"], ["BBb"])
            cp("dve", BBb[64:128], u0[64:128], ["u0", "BBb"], ["BBb"])
            cnr = A2("cnr", [128, 4, 128]); cni = A2("cni", [128, 4, 128])
            Ca = A2("Ca", [128, 32, 16]); Cb = A2("Cb", [128, 32, 16])
            with nc.allow_non_contiguous_dma(reason="param loads"):
                for h in range(2):
                    P.dma("sp", cnr[:, :, 64 * h:64 * h + 64], c_re_d.rearrange("(t q) p -> q t p", q=128), reads=["cnr"] if h else [], writes=["cnr"])
                    P.dma("sp", cni[:, :, 64 * h:64 * h + 64], c_im_d.rearrange("(t q) p -> q t p", q=128), reads=["cni"] if h else [], writes=["cni"])
            bkr, bkrk = nbf()
            bki, bkik = nbf()
            for t in range(4):
                tr(bkr[:, t * 128:(t + 1) * 128], cnr[:, t, :], identf[:], ["cnr", "c_identf"], [bkrk])
                tr(bki[:, t * 128:(t + 1) * 128], cni[:, t, :], identf[:], ["cni", "c_identf"], [bkik])
            v3 = lambda a: a.rearrange("p (g h) -> p g h", h=16)
            cp("dve", Ca[0:64], v3(bkr[0:64, :]), [bkrk], ["Ca"])
            ts("dve", Ca[64:128], v3(bki[64:128, :]), -1.0, None, ALU.mult, None, [bkik, "Ca"], ["Ca"])
            cp("dve", Cb[0:64], v3(bki[0:64, :]), [bkik], ["Cb"])
            cp("dve", Cb[64:128], v3(bkr[64:128, :]), [bkrk, "Cb"], ["Cb"])
            T1 = xt[0][:].rearrange("p a b -> p (a b)").rearrange("p (g j h) -> p g j h", g=16, j=8)
            T2 = xt[1][:].rearrange("p a b -> p (a b)").rearrange("p (g j h) -> p g j h", g=16, j=8)
            bch16 = lambda a: a.unsqueeze(2).to_broadcast([128, 16, 16])

            def table(dst, dk, koff, X0, X1, IM, opx, gh):
                gs = slice(gh * 16, gh * 16 + 16)
                for j in range(8):
                    kA = koff(j) + 7
                    e1 = "dve" if (j % 2 == 0) else "pool"
                    a0 = u0[:, 0:16, :] if e1 == "dve" else u0[:, 16:32, :]
                    a1 = u1[:, 0:16, :] if e1 == "dve" else u1[:, 16:32, :]
                    k0 = "u0" + e1
                    k1 = "u1" + e1
                    tt(e1, a0, X0[:, gs, :], bch16(PRE[:, kA, gs]), ALU.mult, ["BBa", "Ca", "PRE", "u0"], ["u0", k0])
                    tt(e1, a1, X1[:, gs, :], bch16(IM[:, kA, gs]), ALU.mult, ["BBb", "Cb", "PIM", "PIMS", "u1"], ["u1", k1])
                    tt(e1, dst[:, :, j, :], a0, a1, opx, ["u0", "u1", k0, k1], [dk])

            for gh in range(2):
                table(T1, "xt0", lambda j: j + 1, Ca, Cb, PIM, ALU.subtract, gh)
                cp("dve", WC[:, gh * 16:gh * 16 + 16, :], T1.rearrange("p g j h -> p g (j h)"), ["xt0"], ["WC"])
                table(T2, "xt1", lambda j: 7 - j, BBa, BBb, PIMS, ALU.add, gh)
                for q4 in range(4):
                    bA, bAk = nbf()
                    for gl in range(4):
                        tr(bA[:, gl * 128:(gl + 1) * 128], T2[:, q4 * 4 + gl].rearrange("p j h -> p (j h)"), identf[:], ["xt1", "c_identf"], [bAk])
                    g0 = gh * 16 + q4 * 4
                    cp("dve", WA[:, g0:g0 + 4, :], bA[:].rearrange("p (g m) -> p g m", g=4), [bAk], ["WA"])
                table(T1, "xt0", lambda j: -j, BBa, BBb, PIMS, ALU.add, gh)
                table(T2, "xt1", lambda j: j, Ca, Cb, PIM, ALU.subtract, gh)
                for q4 in range(4):
                    bT, bTk = nbf()
                    for gl in range(4):
                        mm(bT[:, gl * 128:(gl + 1) * 128], T1[:, q4 * 4 + gl].rearrange("p j h -> p (j h)"),
                           T2[:, q4 * 4 + gl].rearrange("p j h -> p (j h)"), True, True, ["xt0", "xt1"], [bTk])
                    g0 = gh * 16 + q4 * 4
                    tt("dve", TOE[:, g0:g0 + 4, :], bT[:].rearrange("p (g m) -> p g m", g=4),
                       bmaskt[:].unsqueeze(1).to_broadcast([128, 4, 128]), ALU.mult, [bTk, "bmask"], ["TOE"])

        if not SKIP_S5:
            _s5_setup()
        NSTG = 4
        PW = 1024
        pend = []
        stg = [A2("stg%d" % i, [128, PW]) for i in range(NSTG)]
        stb = [A2("stb%d" % i, [128, PW], BF16) for i in range(NSTG)]

        def conv_piece(wd, kt, c0, cw, dst_scr=None, dst_sb=None, name=""):
            i = st_["cast"]
            st_["cast"] += 1
            s_ = i % NSTG
            P.dma("pool", stg[s_][:, 0:cw], wd[kt * 128:(kt + 1) * 128, c0:c0 + cw], writes=["stg%d" % s_])
            if dst_sb is not None:
                while pend:
                    d_, s2_, rk_, wk_ = pend.pop(0)
                    P.dma("pool", d_, s2_, reads=[rk_], writes=[wk_])
                cp("act", dst_sb[:, kt, c0:c0 + cw], stg[s_][:, 0:cw], ["stg%d" % s_], [name])
            else:
                cp("act", stb[s_][:, 0:cw], stg[s_][:, 0:cw], ["stg%d" % s_], ["stb%d" % s_])
                key = ("scr", name, kt, c0 // PW)
                pend.append((dst_scr[kt * 128:(kt + 1) * 128, c0:c0 + cw], stb[s_][:, 0:cw], "stb%d" % s_, key))
            while len(pend) > (NSTG - 2 if dst_sb is None else NSTG - 2):
                d_, s2_, rk_, wk_ = pend.pop(0)
                P.dma("pool", d_, s2_, reads=[rk_], writes=[wk_])

        def convert(wd, K, C, dst_scr=None, dst_sb=None, name=""):
            for c0 in range(0, C, PW):
                cw = min(PW, C - c0)
                for kt in range(K // 128):
                    conv_piece(wd, kt, c0, cw, dst_scr, dst_sb, name)

        if not SKIP_W:
            convert(w_in_d, 1024, 2048, dst_scr=scr["w_in"], name="w_in")
            convert(w_glu_d, 512, 512, dst_sb=wglu, name="wglu")
            for c0 in range(2048, 7192, PW):
                cw = min(PW, 7192 - c0)
                for kt in range(8):
                    conv_piece(w_in_d, kt, c0, cw, dst_scr=scr["w_in"], name="w_in")
            convert(w_br5_d, 512, 1024, dst_sb=wbr5, name="wbr5")
            convert(w_brs_d, 1536, 1024, dst_scr=scr["w_brs"], name="w_brs")
            convert(w_out_d, 1024, 1024, dst_scr=scr["w_out"], name="w_out")
            convert(w_pp_d, 256, 1024, dst_sb=wpp, name="wpp")
            convert(w_pg_d, 1024, 1024, dst_scr=scr["w_pg"], name="w_pg")
            while pend:
                d_, s2_, rk_, wk_ = pend.pop(0)
                P.dma("pool", d_, s2_, reads=[rk_], writes=[wk_])

        def wload(name, kt0, nkt, c0, ncol):
            i = st_["w"]
            st_["w"] = (i + 1) % NWSL
            keys = [("scr", name, kt, c // 1024) for kt in range(kt0, kt0 + nkt) for c in sorted({c0, c0 + ncol - 1})]
            src = scr[name][kt0 * 128:(kt0 + nkt) * 128, c0:c0 + ncol].rearrange("(kt p) c -> p kt c", p=128)
            with nc.allow_non_contiguous_dma(reason="weight stream"):
                P.dma("sp", wsl[i][:, 0:nkt, 0:ncol], src, reads=keys, writes=["wsl%d" % i])
            return wsl[i], "wsl%d" % i

        out_stores = []
        dbg_t = {}

        def dump(name, ap, shape, dt, keys, blk):
            if not dbg:
                return
            if name not in dbg_t:
                dbg_t[name] = nc.dram_tensor("dbg_" + name, [NB] + list(shape), dt, kind="ExternalOutput").ap()
            out_stores.append(P.dma("pool", dbg_t[name][blk], ap, reads=keys))

        def early(Xsrc, XKsrc, t0s=0):
            out_stores.append(P.dma("pool", out_d[t0s:t0s + TB, :].rearrange("(t p) d -> p t d", p=128), Xsrc[:], reads=[XKsrc]))
        for blk in range(NB):
            first = (blk % NBLK) == 0
            t0_ = blk * TB
            xs_ = blk % 2
            X = xt[xs_]
            XK = "xt%d" % xs_
            AR.reset()
            P.dma("sp", X[:], x_d[t0_:t0_ + TB, :].rearrange("(t p) d -> p t d", p=128), writes=[XK])

            def norm_T(src_key, wcol, col0):
                for t in range(2):
                    act(xnb[t][:], X[:, t, :], AF.Square, [XK], ["xnb%d" % t, "ssq"], accum=ssq[:, col0 + t:col0 + t + 1])
                ts("dve", rst[:, col0:col0 + 2], ssq[:, col0:col0 + 2], 1.0 / 1024, 1e-6, ALU.mult, ALU.add, ["xnb0", "xnb1", "ssq"], ["rst"])
                act(rst[:, col0:col0 + 2], rst[:, col0:col0 + 2], AF.Sqrt, ["rst"], ["rst"])
                P.op("dve", lambda e: e.reciprocal(out=rst[:, col0:col0 + 2], in_=rst[:, col0:col0 + 2]), P._expand(["rst"]), P._expand(["rst"]))
                if stop == "A1":
                    return
                for t in range(2):
                    act(xnb[t][:], X[:, t, :], AF.Copy, [XK, "rst"], ["xnb%d" % t], scale=rst[:, col0 + t:col0 + t + 1])
                if stop == "A2":
                    return
                for half in range(2):
                    bt, btk = nbt()
                    for kl in range(4):
                        for t in range(2):
                            kt = half * 4 + kl
                            tr(bt[:, (kl * 2 + t) * 128:(kl * 2 + t + 1) * 128], xnb[t][:, kt * 128:(kt + 1) * 128], identb[:],
                               ["xnb%d" % t, "c_identb"], [btk])
                    for kl in range(4):
                        if stop == "A3":
                            break
                        kt = half * 4 + kl
                        if True:
                            ts("dve", hnT[:, kt, :], bt[:, kl * 256:(kl + 1) * 256], wcol(kt), None, ALU.mult, None, [btk, "vecT"], [("hnT", kt)])
                        else:
                            act(hnT[:, kt, :], bt[:, kl * 256:(kl + 1) * 256], AF.Copy, [btk, "vecT"], [("hnT", kt)], scale=wcol(kt))

            if stop in ("load", "S0", "S1"):
                early(X, XK, t0_)
                continue
            norm_T(XK, NW, 0)
            if stop in ("A", "A1", "A2", "A3"):
                early(X, XK, t0_)
                continue

            def inproj_pair(c0, width, nt):
                wt, wk = wload("w_in", 0, 8, c0, nt * width)
                bk_, bkk_ = nbf()
                for ti in range(nt):
                    for kt in range(8):
                        mm(bk_[0:width, ti * 256:(ti + 1) * 256], wt[:, kt, ti * width:(ti + 1) * width], hnT[:, kt, :],
                           kt == 0, kt == 7, [wk, ("hnT", kt)], [bkk_])
                return bk_, bkk_

            dump("hnT", hnT[:], [128, 8, TB], BF16, [("hnT", kt) for kt in range(8)], blk)
            szT = A2("szT", [128, 12, TB], BF16, 6)
            cvT = A2("cvT", [128, 20, TB], BF16, 10)
            xraws = [A2("xraw%d" % q, [128, 2, 260], BF16) for q in range(2)]
            accs = [A2("acc%d" % q, [128, 2, TB], F32, 2) for q in range(2)]
            if first:
                mset("pool", hist[:], 0.0, ["hist"])
                mset("pool", prevS[:], 0.0, ["prevS"])

            def conv_pair(pr):
                q = pr % 2
                xraw = xraws[q]; acc = accs[q]; xk = "xraw%d" % q; ak = "acc%d" % q
                bk_, bkk_ = inproj_pair(2560 + pr * 256, 128, 2)
                cp("pool", xraw[:, :, 0:3], hist[:, pr * 2:pr * 2 + 2, 0:3], ["hist"], [xk])
                cp("act", xraw[:, :, 3:259], bk_[:].rearrange("p (a b) -> p a b", a=2), [bkk_, xk], [xk])
                cp("pool", hist[:, pr * 2:pr * 2 + 2, 0:3], xraw[:, :, 256:259], [xk, "hist"], ["hist"])
                for ti in range(2):
                    i = pr * 2 + ti
                    act(acc[:, ti, :], xraw[:, ti, 0:256], AF.Identity, [xk, "vecT"], [(ak, ti)], bias=CB(i), scale=CW(i, 0))
                for k in (1, 2, 3):
                    for ti in range(2):
                        i = pr * 2 + ti
                        stt("dve", acc[:, ti, :], xraw[:, ti, k:k + 256], CW(i, k), acc[:, ti, :], ALU.mult, ALU.add, [xk, "vecT", (ak, ti)], [(ak, ti)])

            def conv_silu(pr):
                qp = pr % 2
                act(cvT[:, pr * 2:pr * 2 + 2, :], accs[qp], AF.Silu, [("acc%d" % qp, 0), ("acc%d" % qp, 1)], [("cvT", pr)])

            conv_state = {"n": 0}

            def conv_next(k=1):
                for _ in range(k):
                    pr = conv_state["n"]
                    if pr >= 10:
                        return
                    conv_pair(pr)
                    if pr > 0:
                        conv_silu(pr - 1)
                    conv_state["n"] = pr + 1
                    if pr == 9:
                        conv_silu(9)

            uT = A2("uT", [128, 4, TB], BF16); sz5 = A2("sz5", [128, 4, TB], BF16)
            for pr in range(2):
                bk_, bkk_ = inproj_pair(pr * 256, 128, 2)
                cp("dve", uT[:, pr * 2:pr * 2 + 2, :], bk_[:].rearrange("p (a b) -> p a b", a=2), [bkk_], ["uT"])
            for pr in range(2):
                bk_, bkk_ = inproj_pair(512 + pr * 256, 128, 2)
                act(sz5[:, pr * 2:pr * 2 + 2, :], bk_[:].rearrange("p (a b) -> p a b", a=2), AF.Silu, [bkk_], ["sz5"])
            Ug = A2("Ug", [128, 32, 32], BF16)
            for ft in range(4):
                bk_, bkk_ = nbf()
                uv = uT[:, ft, :].rearrange("p (c j) -> p j c", j=8)
                for g8 in range(8):
                    for j in range(8):
                        mm(bk_[:, g8 * 32:(g8 + 1) * 32], sel[:, g8 * 8 + j, :], uv[:, j, :], j == 0, j == 7, ["c_sel", "uT"], [bkk_])
                cp("act" if ft % 2 else "dve", Ug[:, ft * 8:(ft + 1) * 8, :], bk_[:, 0:256].rearrange("p (a b) -> p a b", a=8), [bkk_], ["Ug"])
            St = A2("St", [128, 32, 32], F32, 2); Sw = A2("Sw", [128, 32, 32], F32, 2)
            m1 = A2("m1", [128, 16, 32]); m2 = A2("m2", [128, 16, 32])
            if first:
                mset("dve", XCS[:], 0.0, ["XCS"])
                mset("dve", XCW[:], 0.0, ["XCW"])
            for hb in range(2):
                bS, bSk = nbf()
                bW, bWk = nbf()
                for gl in range(16):
                    g = hb * 16 + gl
                    mm(bS[:, gl * 32:(gl + 1) * 32], WA[:, g, :], Ug[:, g, :], True, True, ["WA", "Ug"], [bSk])
                    mm(bW[0:64, gl * 32:(gl + 1) * 32], WA[:, g, 64:128], Ug[:, g, :], True, True, ["WA", "Ug"], [bWk])
                    mm(bW[64:128, gl * 32:(gl + 1) * 32], WA[:, g, 0:64], Ug[:, g, :], True, True, ["WA", "Ug"], [bWk])
                gs = slice(hb * 16, hb * 16 + 16)
                S3 = bS[:].rearrange("p (a b) -> p a b", a=16)
                W3 = bW[:].rearrange("p (a b) -> p a b", a=16)
                tt("dve", m1, S3, COSM[:, gs, :], ALU.mult, [bSk, "COSM"], ["m1"])
                tt("dve", m2, W3, SGNM[:, gs, :], ALU.mult, [bWk, "SGNM"], ["m2"])
                tt("pool", St[:, gs, :], m1, m2, ALU.add, ["m1", "m2"], [("St", hb)])
                tt("dve", m1, W3, COSM[:, gs, :], ALU.mult, [bWk, "COSM"], ["m1"])
                tt("dve", m2, S3, SGNM[:, gs, :], ALU.mult, [bSk, "SGNM"], ["m2"])
                tt("pool", Sw[:, gs, :], m1, m2, ALU.subtract, ["m1", "m2"], [("Sw", hb)])
            for pr in range(6):
                bk_, bkk_ = inproj_pair(1024 + pr * 256, 128, 2)
                act(szT[:, pr * 2:pr * 2 + 2, :], bk_[:].rearrange("p (a b) -> p a b", a=2), AF.Silu, [bkk_], [("szT", pr)])
            c1 = A2("c1", [128, 32]); c2 = A2("c2", [128, 32])
            tt("dve", c1, L8RE[:], XCS[:], ALU.mult, ["L8RE", "XCS"], ["c1"])
            tt("dve", c2, L8SG[:], XCW[:], ALU.mult, ["L8SG", "XCW"], ["c2"])
            tt("dve", c1, c1, c2, ALU.add, ["c1", "c2"], ["c1"])
            tt("dve", St[:, :, 0], St[:, :, 0], c1, ALU.add, [("St", 0), ("St", 1), "c1"], [("St", 0), ("St", 1)])
            tt("dve", c1, L8RE[:], XCW[:], ALU.mult, ["L8RE", "XCW"], ["c1"])
            tt("dve", c2, L8SG[:], XCS[:], ALU.mult, ["L8SG", "XCS"], ["c2"])
            tt("dve", c1, c1, c2, ALU.subtract, ["c1", "c2"], ["c1"])
            tt("dve", Sw[:, :, 0], Sw[:, :, 0], c1, ALU.add, [("Sw", 0), ("Sw", 1), "c1"], [("Sw", 0), ("Sw", 1)])
            Vt = A2("Vt", [128, 32, 32]); Vw = A2("Vw", [128, 32, 32])
            fl = lambda a: a.rearrange("p a b -> p (a b)")
            P.op("dve", lambda e: e.tensor_tensor_scan(out=fl(Vt), data0=fl(RHO0[:]), data1=fl(St), initial=0.0, op0=ALU.mult, op1=ALU.add),
                 P._expand(["RHO0", ("St", 0), ("St", 1)]), P._expand(["Vt"]))
            P.op("dve", lambda e: e.tensor_tensor_scan(out=fl(Vw), data0=fl(RHO0[:]), data1=fl(Sw), initial=0.0, op0=ALU.mult, op1=ALU.add),
                 P._expand(["RHO0", ("Sw", 0), ("Sw", 1)]), P._expand(["Vw"]))
            Xp = A2("Xp", [128, 32, 32], BF16)
            tt("dve", St, Vt, COSM[:], ALU.mult, ["Vt", "COSM", ("St", 0), ("St", 1)], [("St", 0), ("St", 1)])
            tt("pool", Sw, Vw, SGNM[:], ALU.mult, ["Vw", "SGNM", ("Sw", 0), ("Sw", 1)], [("Sw", 0), ("Sw", 1)])
            tt("dve", St, St, Sw, ALU.subtract, [("St", 0), ("St", 1), ("Sw", 0), ("Sw", 1)], [("St", 0), ("St", 1)])
            cp("act", Xp[:, :, 1:32], St[:, :, 0:31], [("St", 0), ("St", 1)], ["Xp"])
            cp("act", Xp[:, :, 0], XCS[:], ["XCS", "Xp"], ["Xp"])
            cp("dve", XCS[:], St[:, :, 31], [("St", 0), ("St", 1), "XCS"], ["XCS"])
            tt("dve", c1, Vw[:, :, 31], COSM[:, :, 31], ALU.mult, ["Vw", "COSM"], ["c1"])
            tt("dve", c2, Vt[:, :, 31], SGNM[:, :, 31], ALU.mult, ["Vt", "SGNM"], ["c2"])
            tt("dve", XCW[:], c1, c2, ALU.add, ["c1", "c2", "XCW"], ["XCW"])
            conv_next(3)
            Yg = A2("Yg", [128, 32, 32], BF16)
            for hb in range(2):
                bY, bYk = nbf()
                for gl in range(16):
                    g = hb * 16 + gl
                    mm(bY[:, gl * 32:(gl + 1) * 32], TOE[:, g, :], Ug[:, g, :], True, False, ["TOE", "Ug"], [bYk])
                    mm(bY[:, gl * 32:(gl + 1) * 32], WC[:, g, :], Xp[:, g, :], False, True, ["WC", "Xp"], [bYk])
                cp("act", Yg[:, hb * 16:hb * 16 + 16, :], bY[:].rearrange("p (a b) -> p a b", a=16), [bYk], ["Yg"])
            yT = A2("yT", [128, 4, TB], BF16, 4)
            ytmp = A2("ytmp", [128, TB])
            for ft in range(4):
                bk_, bkk_ = nbf()
                for j in range(8):
                    for g8 in range(8):
                        mm(bk_[:, j * 32:(j + 1) * 32], sel[:, j * 8 + g8, :], Yg[:, ft * 8 + g8, :], g8 == 0, g8 == 7, ["c_sel", "Yg"], [bkk_])
                stt("dve", ytmp.rearrange("p (c j) -> p j c", j=8), uT[:, ft, :].rearrange("p (c j) -> p j c", j=8), D5(ft),
                    bk_[:, 0:256].rearrange("p (j c) -> p j c", j=8), ALU.mult, ALU.add, ["uT", "vecT", bkk_], ["ytmp"])
                act(yT[:, ft, :], ytmp, AF.Gelu, ["ytmp"], [("yT", ft)])
                conv_next(1)
            dump("uT", uT, [128, 4, TB], BF16, ["uT"], blk)
            dump("Ug", Ug, [128, 32, 32], BF16, ["Ug"], blk)
            dump("Xp", Xp, [128, 32, 32], BF16, ["Xp"], blk)
            dump("Yg", Yg, [128, 32, 32], BF16, ["Yg"], blk)
            dump("yT", yT, [128, 4, TB], BF16, [("yT", f_) for f_ in range(4)], blk)
            sgl = A2("sgl", [128, TB], BF16)
            for ft in range(4):
                bk_, bkk_ = nbf()
                for kt in range(4):
                    mm(bk_[:, 0:256], wglu[:, kt, ft * 128:(ft + 1) * 128], yT[:, kt, :], kt == 0, kt == 3, ["wglu", ("yT", kt)], [bkk_])
                act(sgl, bk_[:, 0:256], AF.Sigmoid, [bkk_, "vecT"], ["sgl"], bias=BGL(ft))
                tt("dve", sgl, sgl, yT[:, ft, :], ALU.mult, ["sgl", ("yT", ft)], ["sgl"])
                tt("pool", y5T[:, ft, :], sgl, sz5[:, ft, :], ALU.mult, ["sgl", "sz5"], [("y5T", ft)])
                conv_next(1)
            conv_next(10)

            dump("y5T", y5T[:], [128, 4, TB], BF16, [("y5T", f_) for f_ in range(4)], blk)
            if stop == "C":
                early(X, XK, t0_)
                continue
            AR.reset()
            szT = A2("szT", [128, 12, TB], BF16, 6)
            cvT = A2("cvT", [128, 20, TB], BF16, 10)
            CV = lambda i: ("cvT", i // 2)
            dump("cvT", cvT, [128, 20, TB], BF16, [("cvT", f_) for f_ in range(10)], blk)
            wt, wk = wload("w_in", 0, 8, 5120, 24)
            bk_, bkk_ = nbf()
            for kt in range(8):
                mm(bk_[0:24, 0:256], wt[:, kt, 0:24], hnT[:, kt, :], kt == 0, kt == 7, [wk, ("hnT", kt)], [bkk_])
            dtT = A2("dtT", [24, TB]); daT = A2("daT", [24, TB]); AT = A2("AT", [24, TB])
            AThi = A2("AThi", [24, TB], BF16); ATlo = A2("ATlo", [24, TB], BF16)
            act(dtT, bk_[0:24, 0:256], AF.Exp, [bkk_, "dtb"], ["dtT"], bias=dtb[:, 0:1])
            act(dtT, dtT, AF.Ln, ["dtT"], ["dtT"], bias=1.0)
            ts("dve", daT, dtT, aneg[:, 0:1], None, ALU.mult, None, ["dtT", "aneg"], ["daT"])
            P.op("dve", lambda e: e.tensor_tensor_scan(out=AT, data0=rmask[:], data1=daT, initial=0.0, op0=ALU.mult, op1=ALU.add),
                 P._expand(["c_rmask", "daT"]), P._expand(["AT"]))
            cp("dve", AThi, AT, ["AT"], ["AThi"])
            tt("dve", ATlo, AT, AThi, ALU.subtract, ["AT", "AThi"], ["ATlo"])
            dump("dtT", dtT, [24, TB], F32, ["dtT"], blk)
            dump("AT", AT, [24, TB], F32, ["AT"], blk)
            dtk = A2("dtk", [128, 2, 24]); nAc = A2("nAc", [128, 2, 24])
            dec = A2("dec", [128, 2, 24]); cdec = A2("cdec", [128, 2, 24]); dtd = A2("dtd", [128, 2, 24])
            rhd = A2("rhd", [24, 2, 24])
            bk_, bkk_ = nbf()
            for ck in range(2):
                tr(bk_[:, ck * 24:(ck + 1) * 24], dtT[:, ck * 128:(ck + 1) * 128], identf[0:24, 0:24], ["dtT", "c_identf"], [bkk_])
                tr(bk_[:, 48 + ck * 24:48 + (ck + 1) * 24], AT[:, ck * 128:(ck + 1) * 128], identf[0:24, 0:24], ["AT", "c_identf"], [bkk_])
                ts("dve", rhd[:, ck, :], identf[0:24, 0:24], AT[:, ck * 128 + 127:ck * 128 + 128], None, ALU.mult, None, ["c_identf", "AT"], ["rhd"])
            mm(bk_[:, 96:144], onesf[0:24, :], rhd.rearrange("p a b -> p (a b)"), True, True, ["c_onesf", "rhd"], [bkk_])
            cp("dve", dtk, bk_[:, 0:48].rearrange("p (a b) -> p a b", a=2), [bkk_], ["dtk"])
            ts("dve", nAc, bk_[:, 48:96].rearrange("p (a b) -> p a b", a=2), -1.0, None, ALU.mult, None, [bkk_], ["nAc"])
            act(cdec, bk_[:, 96:144].rearrange("p (a b) -> p a b", a=2), AF.Exp, [bkk_], ["cdec"])
            tt("dve", dec, bk_[:, 96:144].rearrange("p (a b) -> p a b", a=2), nAc, ALU.add, [bkk_, "nAc"], ["dec"])
            act(dec, dec, AF.Exp, ["dec"], ["dec"])
            tt("dve", dtd, dtk, dec, ALU.mult, ["dtk", "dec"], ["dtd"])
            xdt = A2("xdt", [128, 2, 1536], BF16, 2); xdd = A2("xdd", [128, 2, 1536], BF16, 2)
            Btk = A2("Btk", [128, 2, 512], BF16, 2)
            for ck in range(2):
                cs = slice(ck * 128, (ck + 1) * 128)
                for (i0, n) in ((0, 8), (8, 4)):
                    bt, btk = nbt()
                    for ii in range(n):
                        tr(bt[:, ii * 128:(ii + 1) * 128], cvT[:, i0 + ii, cs], identb[:], [CV(i0 + ii), "c_identb"], [btk])
                    h0 = i0 * 2
                    nh = n * 2
                    src3 = bt[:, 0:n * 128].rearrange("p (h d) -> p h d", d=64)
                    tt("dve", xdt[:, ck, h0 * 64:(h0 + nh) * 64].rearrange("p (h d) -> p h d", d=64), src3,
                       dtk[:, ck, h0:h0 + nh].unsqueeze(2).to_broadcast([128, nh, 64]), ALU.mult, [btk, "dtk"], [("xdt", ck)])
                    tt("dve", xdd[:, ck, h0 * 64:(h0 + nh) * 64].rearrange("p (h d) -> p h d", d=64), src3,
                       dtd[:, ck, h0:h0 + nh].unsqueeze(2).to_broadcast([128, nh, 64]), ALU.mult, [btk, "dtd"], [("xdd", ck)])
                bt, btk = nbt()
                for gq in range(4):
                    tr(bt[:, gq * 128:(gq + 1) * 128], cvT[:, 12 + gq, cs], identb[:], [CV(12 + gq), "c_identb"], [btk])
                cp("act", Btk[:, ck, :], bt[:, 0:512], [btk], [("Btk", ck)])
            pvb = A2("pvb", [128, 2, 1536], BF16, 2)
            for ck in range(2):
                cp("act", pvb[:, ck, :], prevS[:], ["prevS"], [("pvb", ck)])
                tt("pool", prevS[:].rearrange("p (h d) -> p h d", d=64), prevS[:].rearrange("p (h d) -> p h d", d=64),
                   cdec[:, ck, :].unsqueeze(2).to_broadcast([128, 24, 64]), ALU.mult, ["prevS", "cdec"], ["prevS"])
                for gq in range(4):
                    bk_, bkk_ = nbf()
                    mm(bk_[:, 0:384], Btk[:, ck, gq * 128:(gq + 1) * 128], xdd[:, ck, gq * 384:(gq + 1) * 384], True, True,
                       [("Btk", ck), ("xdd", ck)], [bkk_])
                    tt("dve", prevS[:, gq * 384:(gq + 1) * 384], prevS[:, gq * 384:(gq + 1) * 384], bk_[:, 0:384], ALU.add, ["prevS", bkk_], ["prevS"])
            NBUF = 4
            ET = [A2("ET%d" % i, [128, TB], BF16, 2) for i in range(NBUF)]
            EA = [A2("EA%d" % i, [128, TB], BF16) for i in range(NBUF)]
            CD = [A2("CD%d" % i, [128, TB], BF16) for i in range(NBUF)]
            WT = [A2("WT%d" % i, [128, TB], BF16) for i in range(NBUF)]
            ygT = A2("ygT", [128, 3, TB], F32, 3); sq = A2("sq", [128, 3, TB], BF16, 3); rsr = A2("rsr", [128, TB])

            def head_decay(gq, hd, bSc, bSck):
                b_ = hd % NBUF
                bA_, bAk_ = nbf()
                mm(bA_[:, 0:256], onehot[:, hd, :], AThi, True, False, ["c_onehot", "AThi"], [bAk_])
                mm(bA_[:, 0:256], onehot[:, hd, :], ATlo, False, True, ["c_onehot", "ATlo"], [bAk_])
                mm(bA_[:, 256:512], onehot[:, hd, :], AThi, True, False, ["c_onehot", "AThi"], [bAk_])
                mm(bA_[:, 256:512], onehot[:, hd, :], ATlo, False, False, ["c_onehot", "ATlo"], [bAk_])
                mm(bA_[:, 256:512], negbig[:], gtm[:], False, True, ["c_negbig", "c_gt"], [bAk_])
                act(EA[b_], bA_[:, 0:256], AF.Exp, [bAk_], ["EA%d" % b_])
                for ck in range(2):
                    cs = slice(ck * 128, (ck + 1) * 128)
                    act(ET[b_][:, cs], bA_[:, 256 + ck * 128:256 + (ck + 1) * 128], AF.Exp, [bAk_, "nAc"], [("ET%d" % b_, ck)],
                        bias=nAc[:, ck, hd:hd + 1])
                tt("dve", WT[b_], bSc[:, 0:256], ET[b_], ALU.mult, [bSck, ("ET%d" % b_, 0), ("ET%d" % b_, 1)], ["WT%d" % b_])
                tt("pool", CD[b_], cvT[:, 16 + gq, :], EA[b_], ALU.mult, [CV(16 + gq), "EA%d" % b_], ["CD%d" % b_])

            def head_pair_y(gq, hd):
                i = hd // 2
                bY_, bYk_ = nbf()
                for ck in range(2):
                    cs = slice(ck * 128, (ck + 1) * 128)
                    mm(bY_[:, cs], dsk[:, i, :], cvT[:, i, cs], True, False, ["dsk", CV(i)], [bYk_])
                    for hh in range(2):
                        h2 = hd - 1 + hh
                        b2 = h2 % NBUF
                        mm(bY_[64 * hh:64 * hh + 64, cs], xdt[:, ck, h2 * 64:(h2 + 1) * 64], WT[b2][:, cs], False, False,
                           [("xdt", ck), "WT%d" % b2], [bYk_])
                        mm(bY_[64 * hh:64 * hh + 64, cs], pvb[:, ck, h2 * 64:(h2 + 1) * 64], CD[b2][:, cs], False, hh == 1,
                           [("pvb", ck), "CD%d" % b2], [bYk_])
                il = i % 3
                tt("dve", ygT[:, il, :], bY_[:, 0:256], szT[:, i, :], ALU.mult, [bYk_, ("szT", i // 2)], [("ygT", il)])
                tt("dve", sq[:, il, :], ygT[:, il, :], ygT[:, il, :], ALU.mult, [("ygT", il)], [("sq", il)])

            for gq in range(4):
                bSc, bSck = pscs[gq % 2], "psc%d" % (gq % 2)
                for ck in range(2):
                    cs = slice(ck * 128, (ck + 1) * 128)
                    mm(bSc[:, cs], cvT[:, 12 + gq, cs], cvT[:, 16 + gq, cs], True, True, [CV(12 + gq), CV(16 + gq)], [bSck])
                h0 = gq * 6
                head_decay(gq, h0, bSc, bSck)
                head_decay(gq, h0 + 1, bSc, bSck)
                head_decay(gq, h0 + 2, bSc, bSck)
                head_decay(gq, h0 + 3, bSc, bSck)
                head_pair_y(gq, h0 + 1)
                head_decay(gq, h0 + 4, bSc, bSck)
                head_decay(gq, h0 + 5, bSc, bSck)
                head_pair_y(gq, h0 + 3)
                head_pair_y(gq, h0 + 5)
                bR, bRk = nbf()
                for il in range(3):
                    mm(bR[:, 0:256], onesb[:], sq[:, il, :], il == 0, il == 2, ["c_onesb", ("sq", il)], [bRk])
                ts("dve", rsr, bR[:, 0:256], 1.0 / 384, 1e-6, ALU.mult, ALU.add, [bRk], ["rsr"])
                act(rsr, rsr, AF.Ln, ["rsr"], ["rsr"])
                act(rsr, rsr, AF.Exp, ["rsr"], ["rsr"], scale=-0.5)
                for il in range(3):
                    i = gq * 3 + il
                    stt("dve", yssT[:, i, :], ygT[:, il, :], SNW(i), rsr, ALU.mult, ALU.mult, [("ygT", il), "vecT", "rsr"], [("yssT", i)])

            dump("yssT", yssT[:], [128, 12, TB], BF16, [("yssT", f_) for f_ in range(12)], blk)
            dump("prevS", prevS[:], [128, 1536], F32, ["prevS"], blk)
            if stop == "D":
                early(X, XK, t0_)
                continue
            AR.reset()
            gT = A2("gT", [128, 16, TB], BF16, 8)
            mgT = A2("mgT", [128, 8, TB], BF16, 8)
            for pr in range(8):
                bk_, bkk_ = inproj_pair(5144 + pr * 256, 128, 2)
                act(gT[:, pr * 2:pr * 2 + 2, :], bk_[:].rearrange("p (a b) -> p a b", a=2), AF.Sigmoid, [bkk_], [("gT", pr)])
            e1t = A2("e1t", [128, TB]); e2t = A2("e2t", [128, TB])
            for dp in range(4):
                wa, wak = wload("w_brs", 0, 8, dp * 256, 256)
                wb, wbk = wload("w_brs", 8, 4, dp * 256, 256)
                for dl in range(2):
                    d_ = dp * 2 + dl
                    bk_, bkk_ = nbf()
                    for kt in range(4):
                        mm(bk_[:, 0:256], wbr5[:, kt, d_ * 128:(d_ + 1) * 128], y5T[:, kt, :], kt == 0, kt == 3, ["wbr5", ("y5T", kt)], [bkk_])
                    for kt in range(12):
                        w_, wk_ = (wa, wak) if kt < 8 else (wb, wbk)
                        mm(bk_[:, 256:512], w_[:, kt % 8, dl * 128:(dl + 1) * 128], yssT[:, kt, :], kt == 0, kt == 11, [wk_, ("yssT", kt)], [bkk_])
                    tt("dve", e1t, bk_[:, 0:256], gT[:, d_, :], ALU.mult, [bkk_, ("gT", d_ // 2)], ["e1t"])
                    tt("dve", e2t, bk_[:, 256:512], gT[:, 8 + d_, :], ALU.mult, [bkk_, ("gT", 4 + d_ // 2)], ["e2t"])
                    tt("pool", mgT[:, d_, :], e1t, e2t, ALU.add, ["e1t", "e2t"], [("mgT", d_)])
            MG = [("mgT", d_) for d_ in range(8)]
            dump("mgT", mgT, [128, 8, TB], BF16, MG, blk)
            dump("gT", gT, [128, 16, TB], BF16, [("gT", f_) for f_ in range(8)], blk)
            for dc in range(4):
                wt, wk = wload("w_out", 0, 8, dc * 256, 256)
                bk_, bkk_ = nbf()
                for t in range(2):
                    for kt in range(8):
                        mm(bk_[:, t * 256:(t + 1) * 256], mgT[:, kt, t * 128:(t + 1) * 128], wt[:, kt, :], kt == 0, kt == 7, [wk, ("mgT", kt)], [bkk_])
                tt("dve", X[:, :, dc * 256:(dc + 1) * 256], X[:, :, dc * 256:(dc + 1) * 256], bk_[:].rearrange("p (a b) -> p a b", a=2),
                   ALU.add, [XK, bkk_], [XK])
            norm_T(XK, PNW, 2)
            pf = A2("pf", [128, 2, 256]); pb = A2("pb", [128, 2, 256], BF16); pT = A2("pT", [128, 2, TB], BF16)
            P.dma("pool", pf, p_d[t0_:t0_ + TB, :].rearrange("(t p) d -> p t d", p=128), writes=["pf"])
            cp("act", pb, pf, ["pf"], ["pb"])
            bt, btk = nbt()
            for kt in range(2):
                for t in range(2):
                    tr(bt[:, (kt * 2 + t) * 128:(kt * 2 + t + 1) * 128], pb[:, t, kt * 128:(kt + 1) * 128], identb[:], ["pb", "c_identb"], [btk])
            cp("dve", pT, bt[:, 0:512].rearrange("p (a b) -> p a b", a=2), [btk], ["pT"])
            sgp = A2("sgp", [128, 2, 256]); e3t = A2("e3t", [128, 2, 256])
            for dc in range(4):
                wt, wk = wload("w_pg", 0, 8, dc * 256, 256)
                bG, bGk = nbf()
                bP, bPk = nbf()
                for t in range(2):
                    for kt in range(8):
                        mm(bG[:, t * 256:(t + 1) * 256], hnT[:, kt, t * 128:(t + 1) * 128], wt[:, kt, :], kt == 0, kt == 7, [wk, ("hnT", kt)], [bGk])
                    for kt in range(2):
                        mm(bP[:, t * 256:(t + 1) * 256], pT[:, kt, t * 128:(t + 1) * 128], wpp[:, kt, dc * 256:(dc + 1) * 256], kt == 0, kt == 1,
                           ["wpp", "pT"], [bPk])
                act(sgp, bG[:].rearrange("p (a b) -> p a b", a=2), AF.Sigmoid, [bGk], ["sgp"])
                tt("dve", e3t, bP[:].rearrange("p (a b) -> p a b", a=2), sgp, ALU.mult, [bPk, "sgp"], ["e3t"])
                tt("pool", X[:, :, dc * 256:(dc + 1) * 256], X[:, :, dc * 256:(dc + 1) * 256], e3t, ALU.add, [XK, "e3t"], [XK])
            ob = A2("ob", [128, 2, 1024], F32, 2)
            for t in range(2):
                act(xnb[t][:], X[:, t, :], AF.Square, [XK], ["xnb%d" % t, "ssq"], accum=ssq[:, 4 + t:5 + t])
            ts("dve", rst[:, 4:6], ssq[:, 4:6], 1.0 / 1024, 1e-6, ALU.mult, ALU.add, ["xnb0", "xnb1", "ssq"], ["rst"])
            act(rst[:, 4:6], rst[:, 4:6], AF.Sqrt, ["rst"], ["rst"])
            P.op("dve", lambda e: e.reciprocal(out=rst[:, 4:6], in_=rst[:, 4:6]), P._expand(["rst"]), P._expand(["rst"]))
            for t in range(2):
                act(ob[:, t, :], X[:, t, :], AF.Copy, [XK, "rst"], [("ob", t)], scale=rst[:, 4 + t:5 + t])
                tt("dve" if t == 0 else "pool", ob[:, t, :], ob[:, t, :], fnw[:], ALU.mult, [("ob", t), "fnw"], [("ob", t)])
            out_stores.append(P.dma("pool", out_d[t0_:t0_ + TB, :].rearrange("(t p) d -> p t d", p=128), ob, reads=[("ob", 0), ("ob", 1)]))

        P.emit(final_wait_ops=out_stores)
    return nc


_NC_CACHE = {}


def _prep_shared(inp):
    f = lambda a: np.ascontiguousarray(np.asarray(a, dtype=np.float32))
    sh = {
        "norm_w": f(inp["norm_w"]).reshape(8, 128),
        "w_in": f(inp["w_in"]).reshape(1024, 7192),
        "s5_a_re": f(inp["s5_a_re"]).reshape(32, 64),
        "s5_a_im": f(inp["s5_a_im"]).reshape(32, 64),
        "s5_b_re": f(inp["s5_b_re"]).reshape(32, 64, 16),
        "s5_b_im": f(inp["s5_b_im"]).reshape(32, 64, 16),
        "s5_c_re": f(inp["s5_c_re"]).reshape(512, 64),
        "s5_c_im": f(inp["s5_c_im"]).reshape(512, 64),
        "s5_d": f(inp["s5_d"]).reshape(4, 128),
        "s5_log_step": f(inp["s5_log_step"]).reshape(32),
        "s5_w_glu": f(inp["s5_w_glu"]).reshape(512, 512),
        "s5_b_glu": f(inp["s5_b_glu"]).reshape(4, 128),
        "ssd_conv_w": f(inp["ssd_conv_w"]).reshape(80, 128),
        "ssd_conv_b": f(inp["ssd_conv_b"]).reshape(20, 128),
        "ssd_dt_bias": f(inp["ssd_dt_bias"]).reshape(24, 1),
        "ssd_a_log": f(inp["ssd_a_log"]).reshape(24, 1),
        "ssd_d": f(inp["ssd_d"]).reshape(24),
        "ssd_norm_w": f(inp["ssd_norm_w"]).reshape(12, 128),
        "w_br_s5": f(inp["w_br_s5"]).reshape(512, 1024),
        "w_br_ssd": f(inp["w_br_ssd"]).reshape(1536, 1024),
        "w_out": f(inp["w_out"]).reshape(1024, 1024),
        "ple_norm_w": f(inp["ple_norm_w"]).reshape(8, 128),
        "w_ple_gate": f(inp["w_ple_gate"]).reshape(1024, 1024),
        "w_ple_proj": f(inp["w_ple_proj"]).reshape(256, 1024),
        "final_norm_w": f(inp["final_norm_w"]).reshape(1024),
    }
    sh.update(_consts())
    return sh


def run(inputs, n_cores=8):
    x = np.asarray(inputs["x"], dtype=np.float32)
    p = np.asarray(inputs["p"], dtype=np.float32)
    B, L, Dm = x.shape
    assert B % n_cores == 0 and L % TB == 0
    NSEQ = B // n_cores
    NBLK = L // TB
    key = (NSEQ, NBLK)
    import os
    if key not in _NC_CACHE:
        _NC_CACHE[key] = build(NSEQ, NBLK, stop=os.environ.get("KSTOP"), dbg=bool(os.environ.get("KDBG")))
    nc = _NC_CACHE[key]
    sh = _prep_shared(inputs)
    xs = x.reshape(n_cores, NSEQ * L, Dm)
    ps = p.reshape(n_cores, NSEQ * L, 256)
    in_maps = []
    for c in range(n_cores):
        m = dict(sh)
        m["x"] = np.ascontiguousarray(xs[c])
        m["p"] = np.ascontiguousarray(ps[c])
        in_maps.append(m)
    res = run_bass_kernel_spmd(nc, in_maps, core_ids=list(range(n_cores)))
    global LAST_RES
    LAST_RES = res.results
    out = np.stack([np.asarray(r["out"], dtype=np.float32) for r in res.results], axis=0)
    return out.reshape(B, L, Dm)


def kernel(**inputs):
    return run(inputs, 8)
```

```python
import math
from contextlib import ExitStack

import numpy as np
import ml_dtypes
import concourse.bass as bass
import concourse.mybir as mybir
from concourse.bass_utils import run_bass_kernel_spmd

F32 = mybir.dt.float32
BF16 = mybir.dt.bfloat16
I32 = mybir.dt.int32
AF = mybir.ActivationFunctionType
ALU = mybir.AluOpType

TB = 256
NWSL = 4
PI = math.pi
TWO_PI = 2.0 * math.pi


class _Op:
    __slots__ = ("eng", "fn", "reads", "writes", "deps", "sig", "dma_sem", "dma_val", "is_dma", "waits")

    def __init__(self, eng, fn, reads, writes, is_dma):
        self.eng = eng
        self.fn = fn
        self.reads = reads
        self.writes = writes
        self.deps = []
        self.sig = None
        self.is_dma = is_dma
        self.dma_sem = None
        self.dma_val = None
        self.waits = []


class Prog:
    ENGS = ("pe", "act", "dve", "pool", "sp")

    def __init__(self, nc, n_dma_sems=40, same_engine_sync=True):
        self.nc = nc
        self.ops = []
        self.last_writer = {}
        self.readers = {}
        self.n_dma_sems = n_dma_sems
        self.same_engine_sync = same_engine_sync
        self.alias = {}

    def _expand(self, keys):
        out = []
        for k in keys:
            a = self.alias.get(k)
            if a is None:
                out.append(k)
            else:
                out.extend(a)
        return tuple(out)

    def _add(self, op):
        op.reads = self._expand(op.reads)
        op.writes = self._expand(op.writes)
        deps = set()
        for r in op.reads:
            w = self.last_writer.get(r)
            if w is not None:
                deps.add(w)
        for r in op.writes:
            w = self.last_writer.get(r)
            if w is not None and (op.is_dma or self.ops[w].is_dma or self.ops[w].eng != op.eng):
                deps.add(w)
            for rd in self.readers.get(r, {}).values():
                if op.is_dma or self.ops[rd].is_dma or self.ops[rd].eng != op.eng:
                    deps.add(rd)
        idx = len(self.ops)
        op.deps = sorted(deps)
        self.ops.append(op)
        for r in op.writes:
            self.last_writer[r] = idx
            self.readers[r] = {}
        for r in op.reads:
            if r not in op.writes:
                k = ("dma", idx) if op.is_dma else op.eng
                self.readers.setdefault(r, {})[k] = idx
        return idx

    def op(self, eng, fn, reads=(), writes=()):
        return self._add(_Op(eng, fn, tuple(reads), tuple(writes), False))

    def dma(self, queue, out, in_, reads=(), writes=(), **kw):
        def fn(e, out=out, in_=in_, kw=kw):
            return e.dma_start(out=out, in_=in_, allow_slow_non_contiguous=True, **kw)
        return self._add(_Op(queue, fn, tuple(reads), tuple(writes), True))

    def emit(self, final_wait_ops=()):
        nc = self.nc
        ops = self.ops
        needed = [False] * len(ops)
        for o in ops:
            for d in o.deps:
                needed[d] = True
        for i in final_wait_ops:
            needed[i] = True
        sig_cnt = {e: 0 for e in self.ENGS}
        dma_cnt = [0] * self.n_dma_sems
        half = self.n_dma_sems // 2
        pools = {"sp": list(range(0, half)), "pool": list(range(half, self.n_dma_sems))}
        dma_rr = {"sp": 0, "pool": 0}
        for i, o in enumerate(ops):
            if o.is_dma:
                pl = pools[o.eng]
                o.dma_sem = pl[dma_rr[o.eng] % len(pl)]
                dma_rr[o.eng] += 1
                dma_cnt[o.dma_sem] += 1
                o.dma_val = 16 * dma_cnt[o.dma_sem]
            elif needed[i]:
                sig_cnt[o.eng] += 1
                o.sig = sig_cnt[o.eng]
        self.sig_cnt = sig_cnt
        waited = {e: {} for e in self.ENGS}
        for i, o in enumerate(ops):
            need = {}
            if o.is_dma and o.dma_val > 16:
                need[("dma", o.dma_sem)] = o.dma_val - 16
            for d in o.deps:
                p = ops[d]
                if p.is_dma:
                    k = ("dma", p.dma_sem)
                    v = p.dma_val
                else:
                    if p.eng == o.eng and (p.eng == "pe" or not self.same_engine_sync):
                        continue
                    k = ("eng", p.eng)
                    v = p.sig
                if need.get(k, 0) < v:
                    need[k] = v
            w = waited[o.eng]
            for k, v in need.items():
                if w.get(k, 0) < v:
                    w[k] = v
                    o.waits.append((k, v))
        final = []
        for i in final_wait_ops:
            p = ops[i]
            if p.is_dma:
                final.append((("dma", p.dma_sem), p.dma_val))
            else:
                final.append((("eng", p.eng), p.sig))
        with ExitStack() as st:
            sems = {}
            for e in self.ENGS:
                sems[("eng", e)] = st.enter_context(nc.semaphore("s_" + e))
            for j in range(self.n_dma_sems):
                sems[("dma", j)] = st.enter_context(nc.semaphore("s_dma%d" % j))
            block = st.enter_context(nc.Block())
            per = {e: [o for o in ops if o.eng == e] for e in self.ENGS}

            def run(e, engobj):
                for o in per[e]:
                    for k, v in o.waits:
                        engobj.wait_ge(sems[k], v)
                    ins = o.fn(engobj)
                    if o.is_dma:
                        ins.then_inc(sems[("dma", o.dma_sem)], 16)
                    elif o.sig is not None:
                        ins.then_inc(sems[("eng", e)], 1)
                if e == "sp":
                    for k, v in final:
                        engobj.wait_ge(sems[k], v)

            @block.tensor
            def _(eng):
                run("pe", eng)

            @block.scalar
            def _(eng):
                run("act", eng)

            @block.vector
            def _(eng):
                run("dve", eng)

            @block.gpsimd
            def _(eng):
                run("pool", eng)

            @block.sync
            def _(eng):
                run("sp", eng)


def _consts():
    bf = ml_dtypes.bfloat16
    c = {}
    c["c_identb"] = np.eye(128, dtype=np.float32).astype(bf)
    c["c_identf"] = np.eye(128, dtype=np.float32)
    c["c_onesb"] = np.ones((128, 128), np.float32).astype(bf)
    c["c_onesf"] = np.ones((128, 128), np.float32)
    sel = np.zeros((128, 64, 128), np.float32)
    selT = np.zeros((128, 64, 128), np.float32)
    for g8 in range(8):
        for j in range(8):
            for h in range(16):
                sel[g8 * 16 + h, g8 * 8 + j, j * 16 + h] = 1.0
                selT[j * 16 + h, j * 8 + g8, g8 * 16 + h] = 1.0
    c["c_sel"] = sel.reshape(128, 64 * 128).astype(bf)
    c["c_negbig"] = (-30000.0 * np.eye(128, dtype=np.float32)).astype(bf)
    k = np.arange(128)[:, None]
    l = np.arange(128)[None, :]
    gt = (k > l).astype(np.float32)
    c["c_gt"] = np.concatenate([gt, gt], axis=1).astype(bf)
    oh = np.zeros((24, 24, 128), np.float32)
    for hd in range(24):
        oh[hd, hd, :] = 1.0
    rm = np.ones((24, TB), np.float32)
    rm[:, 0::128] = 0.0
    c["c_rmask"] = rm
    jp = (np.arange(128) // 16)[:, None]
    jj = (np.arange(128) // 16)[None, :]
    c["c_bmask"] = (jj >= jp).astype(np.float32)
    kv = np.zeros((128, 16, 32), np.float32)
    for ki in range(16):
        kv[:, ki, :] = ki - 7
    c["c_kv"] = kv.reshape(128, 512)
    cv = np.zeros((128, 32, 32), np.float32)
    cpos = np.ones((128, 32, 32), np.float32)
    for cc in range(32):
        cv[:, :, cc] = 8.0 * cc
    cpos[:, :, 0] = 0.0
    c["c_cv"] = cv.reshape(128, 1024)
    c["c_cpos"] = cpos.reshape(128, 1024)
    sg = np.zeros((128, 2), np.float32)
    sg[:64, 0] = -1.0
    sg[64:, 0] = 1.0
    sg[:64, 1] = 1.0
    sg[64:, 1] = -1.0
    c["c_sg"] = sg
    return c


_CONST_SHAPES = None


def build(NSEQ, NBLK, stop=None, dbg=False):
    NB = NSEQ * NBLK
    NTOK = NB * TB
    nc = bass.Bass("TRN2", target_bir_lowering=False)

    def din(name, shape, dt=F32):
        return nc.dram_tensor(name, list(shape), dt, kind="ExternalInput").ap()

    x_d = din("x", [NTOK, 1024])
    p_d = din("p", [NTOK, 256])
    out_d = nc.dram_tensor("out", [NTOK, 1024], F32, kind="ExternalOutput").ap()
    norm_w_d = din("norm_w", [8, 128])
    w_in_d = din("w_in", [1024, 7192])
    a_re_d = din("s5_a_re", [32, 64])
    a_im_d = din("s5_a_im", [32, 64])
    b_re_d = din("s5_b_re", [32, 64, 16])
    b_im_d = din("s5_b_im", [32, 64, 16])
    c_re_d = din("s5_c_re", [512, 64])
    c_im_d = din("s5_c_im", [512, 64])
    s5_d_d = din("s5_d", [4, 128])
    lstep_d = din("s5_log_step", [32])
    w_glu_d = din("s5_w_glu", [512, 512])
    b_glu_d = din("s5_b_glu", [4, 128])
    conv_w_d = din("ssd_conv_w", [80, 128])
    conv_b_d = din("ssd_conv_b", [20, 128])
    dt_bias_d = din("ssd_dt_bias", [24, 1])
    a_log_d = din("ssd_a_log", [24, 1])
    ssd_d_d = din("ssd_d", [24])
    ssd_nw_d = din("ssd_norm_w", [12, 128])
    w_br5_d = din("w_br_s5", [512, 1024])
    w_brs_d = din("w_br_ssd", [1536, 1024])
    w_out_d = din("w_out", [1024, 1024])
    ple_nw_d = din("ple_norm_w", [8, 128])
    w_pg_d = din("w_ple_gate", [1024, 1024])
    w_pp_d = din("w_ple_proj", [256, 1024])
    fnw_d = din("final_norm_w", [1024])
    cd = {}
    for name, arr in _consts().items():
        cd[name] = din(name, arr.shape, BF16 if arr.dtype == ml_dtypes.bfloat16 else F32)

    scr = {
        "w_in": nc.dram_tensor("scr_w_in", [1024, 7192], BF16, kind="ExternalOutput").ap(),
        "w_brs": nc.dram_tensor("scr_w_brs", [1536, 1024], BF16, kind="ExternalOutput").ap(),
        "w_out": nc.dram_tensor("scr_w_out", [1024, 1024], BF16, kind="ExternalOutput").ap(),
        "w_pg": nc.dram_tensor("scr_w_pg", [1024, 1024], BF16, kind="ExternalOutput").ap(),
    }

    with ExitStack() as st:
        def sb(name, shape, dt=F32):
            return st.enter_context(nc.sbuf_tensor(name, list(shape), dt))

        def psum(name, shape, dt):
            return st.enter_context(nc.psum_tensor(name, list(shape), dt))

        P = Prog(nc)
        psf = [psum("psf%d" % i, [128, 512], F32) for i in range(4)]
        pst = [psum("pst%d" % i, [128, 1024], BF16) for i in range(2)]
        pscs = [psum("psc%d" % i, [128, 512], F32) for i in range(2)]
        st_ = {"f": 0, "t": 0, "w": 0, "cast": 0}

        def nbf():
            i = st_["f"]
            st_["f"] = (i + 1) % 4
            return psf[i], "psf%d" % i

        def nbt():
            i = st_["t"]
            st_["t"] = (i + 1) % 2
            return pst[i], "pst%d" % i

        def mm(out, lhsT, rhs, start, stop, reads, writes):
            P.op("pe", lambda e: e.matmul(out, lhsT=lhsT, rhs=rhs, start=start, stop=stop), reads, writes)

        def tr(out, in_, ident, reads, writes):
            P.op("pe", lambda e: e.transpose(out=out, in_=in_, identity=ident), reads, writes)

        def act(out, in_, func, reads, writes, bias=None, scale=None, accum=None):
            kw = {}
            if bias is not None:
                kw["bias"] = bias
            if scale is not None:
                kw["scale"] = scale
            if accum is not None:
                kw["accum_out"] = accum
            P.op("act", lambda e: e.activation(out=out, in_=in_, func=func, **kw), reads, writes)

        def eng_of(eng):
            return eng

        def tt(eng, out, in0, in1, op, reads, writes):
            P.op(eng, lambda e: e.tensor_tensor(out=out, in0=in0, in1=in1, op=op), reads, writes)

        def ts(eng, out, in0, s1, s2, op0, op1, reads, writes):
            if op1 is None:
                P.op(eng, lambda e: e.tensor_scalar(out=out, in0=in0, scalar1=s1, scalar2=None, op0=op0), reads, writes)
            else:
                P.op(eng, lambda e: e.tensor_scalar(out=out, in0=in0, scalar1=s1, scalar2=s2, op0=op0, op1=op1), reads, writes)

        def stt(eng, out, in0, scalar, in1, op0, op1, reads, writes):
            P.op(eng, lambda e: e.scalar_tensor_tensor(out=out, in0=in0, scalar=scalar, in1=in1, op0=op0, op1=op1), reads, writes)

        def cp(eng, out, in_, reads, writes):
            if eng == "act":
                P.op("act", lambda e: e.copy(out=out, in_=in_), reads, writes)
            else:
                P.op(eng, lambda e: e.tensor_copy(out=out, in_=in_), reads, writes)

        def mset(eng, out, val, writes):
            P.op(eng, lambda e: e.memset(out, val), (), writes)

        identb = sb("identb", [128, 128], BF16)
        identf = sb("identf", [128, 128], F32)
        onesb = sb("onesb", [128, 128], BF16)
        onesf = sb("onesf", [128, 128], F32)
        sel = sb("sel", [128, 64, 128], BF16)
        negbig = sb("negbig", [128, 128], BF16)
        gtm = sb("gtm", [128, 256], BF16)
        rmask = sb("rmask", [24, TB], F32)
        sgc = sb("sgc", [128, 2], F32)
        for t_, d_ in ((identb, "c_identb"), (identf, "c_identf"), (onesb, "c_onesb"), (onesf, "c_onesf"),
                       (negbig, "c_negbig"), (gtm, "c_gt"), (rmask, "c_rmask"), (sgc, "c_sg")):
            P.dma("sp", t_[:], cd[d_], writes=[d_])
        P.dma("sp", sel[:].rearrange("p a b -> p (a b)"), cd["c_sel"], writes=["c_sel"])

        xt = [sb("xt%d" % i, [128, 2, 1024], F32) for i in range(2)]
        xnb = [sb("xnb%d" % i, [128, 1024], BF16) for i in range(2)]
        hnTs = [sb("hnT%d" % i, [128, 8, TB], BF16) for i in range(2)]
        wsl = [sb("wsl%d" % i, [128, 8, 256], BF16) for i in range(NWSL)]
        wglu = sb("wglu", [128, 4, 512], BF16)
        wbr5 = sb("wbr5", [128, 4, 1024], BF16)
        wpp = sb("wpp", [128, 2, 1024], BF16)
        vecT = sb("vecT", [128, 136], F32)
        fnw = sb("fnw", [128, 1024], F32)
        dtb = sb("dtb", [24, 1], F32)
        aneg = sb("aneg", [24, 1], F32)
        dch = sb("dch", [128, 12], F32)
        dsk = sb("dsk", [128, 12, 128], BF16)
        WA = sb("WA", [128, 32, 128], BF16)
        TOE = sb("TOE", [128, 32, 128], BF16)
        WC = sb("WC", [128, 32, 128], BF16)
        COSM = sb("COSM", [128, 32, 32], F32)
        SGNM = sb("SGNM", [128, 32, 32], F32)
        RHO0 = sb("RHO0", [128, 32, 32], F32)
        L8RE = sb("L8RE", [128, 32], F32)
        L8SG = sb("L8SG", [128, 32], F32)
        XCS = sb("XCS", [128, 32], F32)
        XCW = sb("XCW", [128, 32], F32)
        ssq = sb("ssq", [128, 8], F32)
        rst = sb("rst", [128, 8], F32)
        ARENA_W = 15 * 1024 + 512
        arena = sb("arena", [128, ARENA_W], F32)

        class Arena:
            def __init__(self):
                self.off = 0

            def reset(self):
                self.off = 0

            def get(self, shape, dt, name, parts=None):
                n = 1
                for s_ in shape[1:]:
                    n *= s_
                words = (n * (2 if dt == BF16 else 4) + 3) // 4
                assert self.off + words <= ARENA_W, (name, self.off, words)
                g0 = self.off // 256
                g1 = (self.off + words - 1) // 256
                P.alias[name] = [("ar", k) for k in range(g0, g1 + 1)]
                if parts:
                    for pi in range(parts):
                        w0 = self.off + (words * pi) // parts
                        w1 = self.off + (words * (pi + 1)) // parts - 1
                        P.alias[(name, pi)] = [("ar", k) for k in range(w0 // 256, w1 // 256 + 1)]
                a = arena[0:shape[0], self.off:self.off + words]
                if dt != F32:
                    a = a.bitcast(dt)
                self.off += words
                if len(shape) == 3:
                    a = a.rearrange("p (a b) -> p a b", a=shape[1])
                elif len(shape) == 4:
                    a = a.rearrange("p (a b c) -> p a b c", a=shape[1], b=shape[2])
                return a

        AR = Arena()

        vs0 = AR.get([128, 128], F32, "vs0")
        vs1 = AR.get([128, 128], F32, "vs1")
        mset("dve", vs0, 0.0, ["vs0"])
        mset("dve", vs1, 0.0, ["vs1"])
        P.dma("sp", vs0[0:80, :], conv_w_d, reads=["vs0"], writes=["vs0"])
        P.dma("sp", vs0[80:100, :], conv_b_d, reads=["vs0"], writes=["vs0"])
        P.dma("sp", vs0[100:112, :], ssd_nw_d, reads=["vs0"], writes=["vs0"])
        P.dma("sp", vs0[112:120, :], norm_w_d, reads=["vs0"], writes=["vs0"])
        P.dma("sp", vs0[120:128, :], ple_nw_d, reads=["vs0"], writes=["vs0"])
        P.dma("sp", vs1[0:4, :], s5_d_d, reads=["vs1"], writes=["vs1"])
        P.dma("sp", vs1[4:8, :], b_glu_d, reads=["vs1"], writes=["vs1"])
        bk, bkk = nbf()
        tr(bk[:, 0:128], vs0, identf[:], ["vs0", "c_identf"], [bkk])
        tr(bk[:, 128:136], vs1[0:8, :], identf[0:8, 0:8], ["vs1", "c_identf"], [bkk])
        cp("dve", vecT[:], bk[:, 0:136], [bkk], ["vecT"])
        CW = lambda i, k: vecT[:, k * 20 + i:k * 20 + i + 1]
        CB = lambda i: vecT[:, 80 + i:81 + i]
        SNW = lambda i: vecT[:, 100 + i:101 + i]
        NW = lambda kt: vecT[:, 112 + kt:113 + kt]
        PNW = lambda kt: vecT[:, 120 + kt:121 + kt]
        D5 = lambda ft: vecT[:, 128 + ft:129 + ft]
        BGL = lambda ft: vecT[:, 132 + ft:133 + ft]
        with nc.allow_non_contiguous_dma(reason="tiny param loads"):
            P.dma("sp", fnw[:], fnw_d.partition_broadcast(128), writes=["fnw"])
            P.dma("sp", dtb[:], dt_bias_d, writes=["dtb"])
            P.dma("sp", aneg[:], a_log_d, writes=["aneg"])
            sdv = ssd_d_d.rearrange("(i two) -> two i", two=2)
            P.dma("sp", dch[0:64, :], sdv[0].partition_broadcast(64), writes=["dch"])
            P.dma("sp", dch[64:128, :], sdv[1].partition_broadcast(64), reads=["dch"], writes=["dch"])
        act(aneg[:], aneg[:], AF.Exp, ["aneg"], ["aneg"])
        ts("dve", aneg[:], aneg[:], -1.0, None, ALU.mult, None, ["aneg"], ["aneg"])
        for i in range(12):
            ts("dve", dsk[:, i, :], identb[:], dch[:, i:i + 1], None, ALU.mult, None, ["c_identb", "dch"], ["dsk"])

        AR.reset()
        A2 = lambda n, s, d=F32, parts=None: AR.get(s, d, n, parts)
        SKIP_S5 = stop in ('S0',)
        SKIP_W = stop in ('S0', 'S1')
        bmaskt = sb("bmaskt", [128, 128], F32)
        P.dma("sp", bmaskt[:], cd["c_bmask"], writes=["bmask"])
        y5T = sb("y5T", [128, 4, TB], BF16)
        yssT = sb("yssT", [128, 12, TB], BF16)
        prevS = sb("prevS", [128, 1536], F32)
        hist = sb("hist", [128, 20, 4], BF16)

        def _s5_setup():
            aTre = A2("aTre", [128, 32]); aTim = A2("aTim", [128, 32]); stp = A2("stp", [128, 32])
            ars = A2("ars", [128, 32]); ais = A2("ais", [128, 32]); r8 = A2("r8", [128, 32])
            anat = A2("anat", [32, 2, 128])
            for h in range(2):
                P.dma("sp", anat[:, 0, 64 * h:64 * h + 64], a_re_d, reads=["anat"] if h else [], writes=["anat"])
                P.dma("sp", anat[:, 1, 64 * h:64 * h + 64], a_im_d, reads=["anat"], writes=["anat"])
            bka, bkak = nbf()
            tr(bka[:, 0:32], anat[:, 0, :], identf[0:32, 0:32], ["anat", "c_identf"], [bkak])
            tr(bka[:, 32:64], anat[:, 1, :], identf[0:32, 0:32], ["anat", "c_identf"], [bkak])
            cp("dve", aTre, bka[:, 0:32], [bkak], ["aTre"])
            cp("dve", aTim, bka[:, 32:64], [bkak], ["aTim"])
            P.dma("sp", stp, lstep_d.partition_broadcast(128), writes=["stp"])
            act(stp, stp, AF.Exp, ["stp"], ["stp"])
            tt("dve", ars, aTre, stp, ALU.mult, ["aTre", "stp"], ["ars"])
            tt("dve", ais, aTim, stp, ALU.mult, ["aTim", "stp"], ["ais"])

            def range_reduce(eng, xin, shape, key, tmpname):
                y = AR.get(shape, F32, tmpname + "y")
                yi = AR.get(shape, I32, tmpname + "i")
                ky, ki_ = tmpname + "y", tmpname + "i"
                ts(eng, y, xin, 1.0 / TWO_PI, None, ALU.mult, None, [key], [ky])
                cp(eng, yi, y, [ky], [ki_])
                cp(eng, y, yi, [ki_], [ky])
                stt(eng, xin, y, -TWO_PI, xin, ALU.mult, ALU.add, [ky, key], [key])
                ts(eng, y, xin, PI, None, ALU.is_gt, None, [key], [ky])
                stt(eng, xin, y, -TWO_PI, xin, ALU.mult, ALU.add, [ky, key], [key])
                ts(eng, y, xin, -PI, None, ALU.is_lt, None, [key], [ky])
                stt(eng, xin, y, TWO_PI, xin, ALU.mult, ALU.add, [ky, key], [key])
                ts(eng, xin, xin, -PI, PI, ALU.max, ALU.min, [key], [key])

            mark0 = AR.off
            cvt = A2("cvt", [128, 32, 32]); cpos = A2("cpos", [128, 32, 32])
            AM = A2("AM", [128, 32, 32]); AM2 = A2("AM2", [128, 32, 32])
            P.dma("sp", cvt.rearrange("p a b -> p (a b)"), cd["c_cv"], writes=["cvt"])
            P.dma("sp", cpos.rearrange("p a b -> p (a b)"), cd["c_cpos"], writes=["cpos"])
            bcc = lambda a: a.unsqueeze(2).to_broadcast([128, 32, 32])
            tt("dve", AM, cvt, bcc(ais), ALU.mult, ["cvt", "ais"], ["AM"])
            ts("dve", AM2, AM, PI / 2, None, ALU.add, None, ["AM"], ["AM2"])
            mk = AR.off
            range_reduce("dve", AM, [128, 32, 32], "AM", "rra")
            AR.off = mk
            range_reduce("dve", AM2, [128, 32, 32], "AM2", "rra")
            act(AM, AM, AF.Sin, ["AM"], ["AM"])
            act(COSM[:], AM2, AF.Sin, ["AM2"], ["COSM"])
            ts("dve", SGNM[:], AM, sgc[:, 1:2], None, ALU.mult, None, ["AM", "c_sg"], ["SGNM"])
            act(r8, ars, AF.Exp, ["ars"], ["r8"], scale=8.0)
            tt("dve", RHO0[:], cpos, bcc(r8), ALU.mult, ["cpos", "r8"], ["RHO0"])
            AR.off = mark0
            kvt = A2("kvt", [128, 16, 32])
            P.dma("sp", kvt.rearrange("p a b -> p (a b)"), cd["c_kv"], writes=["kvt"])
            MAG = A2("MAG", [128, 16, 32]); ANG = A2("ANG", [128, 16, 32]); ANG2 = A2("ANG2", [128, 16, 32])
            PRE = A2("PRE", [128, 16, 32]); PIM = A2("PIM", [128, 16, 32]); PIMS = A2("PIMS", [128, 16, 32])
            bc16 = lambda a: a.unsqueeze(1).to_broadcast([128, 16, 32])
            tt("dve", MAG, kvt, bc16(ars), ALU.mult, ["kvt", "ars"], ["MAG"])
            act(MAG, MAG, AF.Exp, ["MAG"], ["MAG"])
            tt("dve", ANG, kvt, bc16(ais), ALU.mult, ["kvt", "ais"], ["ANG"])
            ts("dve", ANG2, ANG, PI / 2, None, ALU.add, None, ["ANG"], ["ANG2"])
            mk = AR.off
            range_reduce("dve", ANG, [128, 16, 32], "ANG", "rrb")
            AR.off = mk
            range_reduce("dve", ANG2, [128, 16, 32], "ANG2", "rrb")
            AR.off = mk
            act(ANG, ANG, AF.Sin, ["ANG"], ["ANG"])
            act(ANG2, ANG2, AF.Sin, ["ANG2"], ["ANG2"])
            tt("dve", PRE, MAG, ANG2, ALU.mult, ["MAG", "ANG2"], ["PRE"])
            tt("dve", PIM, MAG, ANG, ALU.mult, ["MAG", "ANG"], ["PIM"])
            ts("dve", PIMS, PIM, sgc[:, 0:1], None, ALU.mult, None, ["PIM", "c_sg"], ["PIMS"])
            cp("dve", L8RE[:], PRE[:, 15, :], ["PRE"], ["L8RE"])
            cp("dve", L8SG[:], PIMS[:, 15, :], ["PIMS"], ["L8SG"])
            nre = A2("nre", [128, 32]); den = A2("den", [128, 32]); t0 = A2("t0", [128, 32]); t1 = A2("t1", [128, 32])
            fre = A2("fre", [128, 32]); fim = A2("fim", [128, 32])
            ts("dve", nre, PRE[:, 8, :], -1.0, None, ALU.add, None, ["PRE"], ["nre"])
            tt("dve", den, aTre, aTre, ALU.mult, ["aTre"], ["den"])
            tt("dve", t0, aTim, aTim, ALU.mult, ["aTim"], ["t0"])
            tt("dve", den, den, t0, ALU.add, ["den", "t0"], ["den"])
            P.op("dve", lambda e: e.reciprocal(out=den, in_=den), P._expand(["den"]), P._expand(["den"]))
            tt("dve", t0, nre, aTre, ALU.mult, ["nre", "aTre"], ["t0"])
            tt("dve", t1, PIM[:, 8, :], aTim, ALU.mult, ["PIM", "aTim"], ["t1"])
            tt("dve", t0, t0, t1, ALU.add, ["t0", "t1"], ["t0"])
            tt("dve", fre, t0, den, ALU.mult, ["t0", "den"], ["fre"])
            tt("dve", t0, PIM[:, 8, :], aTre, ALU.mult, ["PIM", "aTre"], ["t0"])
            tt("dve", t1, nre, aTim, ALU.mult, ["nre", "aTim"], ["t1"])
            tt("dve", t0, t0, t1, ALU.subtract, ["t0", "t1"], ["t0"])
            tt("dve", fim, t0, den, ALU.mult, ["t0", "den"], ["fim"])
            bre = A2("bre", [128, 32, 16]); bim = A2("bim", [128, 32, 16])
            BBa = A2("BBa", [128, 32, 16]); BBb = A2("BBb", [128, 32, 16])
            u0 = A2("u0", [128, 32, 16]); u1 = A2("u1", [128, 32, 16])
            with nc.allow_non_contiguous_dma(reason="param loads"):
                for h in range(2):
                    for g4 in range(4):
                        gsl = slice(g4 * 8, g4 * 8 + 8)
                        P.dma("sp", bre[64 * h:64 * h + 64, gsl, :], b_re_d[gsl].rearrange("g p h -> p g h"), reads=["bre"], writes=["bre"])
                        P.dma("sp", bim[64 * h:64 * h + 64, gsl, :], b_im_d[gsl].rearrange("g p h -> p g h"), reads=["bim"], writes=["bim"])
            bch = lambda a: a.unsqueeze(2).to_broadcast([128, 32, 16])
            tt("dve", u0, bre, bch(fre), ALU.mult, ["bre", "fre"], ["u0"])
            tt("dve", u1, bim, bch(fim), ALU.mult, ["bim", "fim"], ["u1"])
            tt("dve", u0, u0, u1, ALU.subtract, ["u0", "u1"], ["u0"])
            tt("dve", u1, bim, bch(fre), ALU.mult, ["bim", "fre"], ["u1"])
            tt("dve", BBb, bre, bch(fim), ALU.mult, ["bre", "fim"], ["BBb"])
            tt("dve", u1, u1, BBb, ALU.add, ["u1", "BBb"], ["u1"])
            cp("dve", BBa[0:64], u0[0:64], ["u0"], ["BBa"])
            cp("dve", BBa[64:128], u1[64:128], ["u1", "BBa"], ["BBa"])
            cp("dve", BBb[0:64], u1[0:64], ["u1", "BBb"], ["BBb"])
            cp("dve", BBb[64:128], u0[64:128], ["u0", "BBb"], ["BBb"])
            cnr = A2("cnr", [128, 4, 128]); cni = A2("cni", [128, 4, 128])
            Ca = A2("Ca", [128, 32, 16]); Cb = A2("Cb", [128, 32, 16])
            with nc.allow_non_contiguous_dma(reason="param loads"):
                for h in range(2):
                    P.dma("sp", cnr[:, :, 64 * h:64 * h + 64], c_re_d.rearrange("(t q) p -> q t p", q=128), reads=["cnr"] if h else [], writes=["cnr"])
                    P.dma("sp", cni[:, :, 64 * h:64 * h + 64], c_im_d.rearrange("(t q) p -> q t p", q=128), reads=["cni"] if h else [], writes=["cni"])
            bkr, bkrk = nbf()
            bki, bkik = nbf()
            for t in range(4):
                tr(bkr[:, t * 128:(t + 1) * 128], cnr[:, t, :], identf[:], ["cnr", "c_identf"], [bkrk])
                tr(bki[:, t * 128:(t + 1) * 128], cni[:, t, :], identf[:], ["cni", "c_identf"], [bkik])
            v3 = lambda a: a.rearrange("p (g h) -> p g h", h=16)
            cp("dve", Ca[0:64], v3(bkr[0:64, :]), [bkrk], ["Ca"])
            ts("dve", Ca[64:128], v3(bki[64:128, :]), -1.0, None, ALU.mult, None, [bkik, "Ca"], ["Ca"])
            cp("dve", Cb[0:64], v3(bki[0:64, :]), [bkik], ["Cb"])
            cp("dve", Cb[64:128], v3(bkr[64:128, :]), [bkrk, "Cb"], ["Cb"])
            T1 = xt[0][:].rearrange("p a b -> p (a b)").rearrange("p (g j h) -> p g j h", g=16, j=8)
            T2 = xt[1][:].rearrange("p a b -> p (a b)").rearrange("p (g j h) -> p g j h", g=16, j=8)
            bch16 = lambda a: a.unsqueeze(2).to_broadcast([128, 16, 16])

            def table(dst, dk, koff, X0, X1, IM, opx, gh):
                gs = slice(gh * 16, gh * 16 + 16)
                for j in range(8):
                    kA = koff(j) + 7
                    e1 = "dve" if (j % 2 == 0) else "pool"
                    a0 = u0[:, 0:16, :] if e1 == "dve" else u0[:, 16:32, :]
                    a1 = u1[:, 0:16, :] if e1 == "dve" else u1[:, 16:32, :]
                    k0 = "u0" + e1
                    k1 = "u1" + e1
                    tt(e1, a0, X0[:, gs, :], bch16(PRE[:, kA, gs]), ALU.mult, ["BBa", "Ca", "PRE", "u0"], ["u0", k0])
                    tt(e1, a1, X1[:, gs, :], bch16(IM[:, kA, gs]), ALU.mult, ["BBb", "Cb", "PIM", "PIMS", "u1"], ["u1", k1])
                    tt(e1, dst[:, :, j, :], a0, a1, opx, ["u0", "u1", k0, k1], [dk])

            for gh in range(2):
                table(T1, "xt0", lambda j: j + 1, Ca, Cb, PIM, ALU.subtract, gh)
                cp("dve", WC[:, gh * 16:gh * 16 + 16, :], T1.rearrange("p g j h -> p g (j h)"), ["xt0"], ["WC"])
                table(T2, "xt1", lambda j: 7 - j, BBa, BBb, PIMS, ALU.add, gh)
                for q4 in range(4):
                    bA, bAk = nbf()
                    for gl in range(4):
                        tr(bA[:, gl * 128:(gl + 1) * 128], T2[:, q4 * 4 + gl].rearrange("p j h -> p (j h)"), identf[:], ["xt1", "c_identf"], [bAk])
                    g0 = gh * 16 + q4 * 4
                    cp("dve", WA[:, g0:g0 + 4, :], bA[:].rearrange("p (g m) -> p g m", g=4), [bAk], ["WA"])
                table(T1, "xt0", lambda j: -j, BBa, BBb, PIMS, ALU.add, gh)
                table(T2, "xt1", lambda j: j, Ca, Cb, PIM, ALU.subtract, gh)
                for q4 in range(4):
                    bT, bTk = nbf()
                    for gl in range(4):
                        mm(bT[:, gl * 128:(gl + 1) * 128], T1[:, q4 * 4 + gl].rearrange("p j h -> p (j h)"),
                           T2[:, q4 * 4 + gl].rearrange("p j h -> p (j h)"), True, True, ["xt0", "xt1"], [bTk])
                    g0 = gh * 16 + q4 * 4
                    tt("dve", TOE[:, g0:g0 + 4, :], bT[:].rearrange("p (g m) -> p g m", g=4),
                       bmaskt[:].unsqueeze(1).to_broadcast([128, 4, 128]), ALU.mult, [bTk, "bmask"], ["TOE"])

        if not SKIP_S5:
            _s5_setup()
        NSTG = 4
        PW = 1024
        pend = []
        stg = [A2("stg%d" % i, [128, PW]) for i in range(NSTG)]
        stb = [A2("stb%d" % i, [128, PW], BF16) for i in range(NSTG)]

        def conv_piece(wd, kt, c0, cw, dst_scr=None, dst_sb=None, name=""):
            i = st_["cast"]
            st_["cast"] += 1
            s_ = i % NSTG
            P.dma("pool", stg[s_][:, 0:cw], wd[kt * 128:(kt + 1) * 128, c0:c0 + cw], writes=["stg%d" % s_])
            if dst_sb is not None:
                while pend:
                    d_, s2_, rk_, wk_ = pend.pop(0)
                    P.dma("pool", d_, s2_, reads=[rk_], writes=[wk_])
                cp("act", dst_sb[:, kt, c0:c0 + cw], stg[s_][:, 0:cw], ["stg%d" % s_], [name])
            else:
                cp("act", stb[s_][:, 0:cw], stg[s_][:, 0:cw], ["stg%d" % s_], ["stb%d" % s_])
                key = ("scr", name, kt, c0 // PW)
                pend.append((dst_scr[kt * 128:(kt + 1) * 128, c0:c0 + cw], stb[s_][:, 0:cw], "stb%d" % s_, key))
            while len(pend) > (NSTG - 2 if dst_sb is None else NSTG - 2):
                d_, s2_, rk_, wk_ = pend.pop(0)
                P.dma("pool", d_, s2_, reads=[rk_], writes=[wk_])

        def convert(wd, K, C, dst_scr=None, dst_sb=None, name=""):
            for c0 in range(0, C, PW):
                cw = min(PW, C - c0)
                for kt in range(K // 128):
                    conv_piece(wd, kt, c0, cw, dst_scr, dst_sb, name)

        if not SKIP_W:
            convert(w_in_d, 1024, 2048, dst_scr=scr["w_in"], name="w_in")
            convert(w_glu_d, 512, 512, dst_sb=wglu, name="wglu")
            for c0 in range(2048, 7192, PW):
                cw = min(PW, 7192 - c0)
                for kt in range(8):
                    conv_piece(w_in_d, kt, c0, cw, dst_scr=scr["w_in"], name="w_in")
            convert(w_br5_d, 512, 1024, dst_sb=wbr5, name="wbr5")
            convert(w_brs_d, 1536, 1024, dst_scr=scr["w_brs"], name="w_brs")
            convert(w_out_d, 1024, 1024, dst_scr=scr["w_out"], name="w_out")
            convert(w_pp_d, 256, 1024, dst_sb=wpp, name="wpp")
            convert(w_pg_d, 1024, 1024, dst_scr=scr["w_pg"], name="w_pg")
            while pend:
                d_, s2_, rk_, wk_ = pend.pop(0)
                P.dma("pool", d_, s2_, reads=[rk_], writes=[wk_])

        def wload(name, kt0, nkt, c0, ncol):
            i = st_["w"]
            st_["w"] = (i + 1) % NWSL
            keys = [("scr", name, kt, c // 1024) for kt in range(kt0, kt0 + nkt) for c in sorted({c0, c0 + ncol - 1})]
            src = scr[name][kt0 * 128:(kt0 + nkt) * 128, c0:c0 + ncol].rearrange("(kt p) c -> p kt c", p=128)
            with nc.allow_non_contiguous_dma(reason="weight stream"):
                P.dma("sp", wsl[i][:, 0:nkt, 0:ncol], src, reads=keys, writes=["wsl%d" % i])
            return wsl[i], "wsl%d" % i

        out_stores = []
        dbg_t = {}

        def dump(name, ap, shape, dt, keys, blk):
            if not dbg:
                return
            if name not in dbg_t:
                dbg_t[name] = nc.dram_tensor("dbg_" + name, [NB] + list(shape), dt, kind="ExternalOutput").ap()
            out_stores.append(P.dma("pool", dbg_t[name][blk], ap, reads=keys))

        def early(Xsrc, XKsrc, t0s=0):
            out_stores.append(P.dma("pool", out_d[t0s:t0s + TB, :].rearrange("(t p) d -> p t d", p=128), Xsrc[:], reads=[XKsrc]))
        for blk in range(NB):
            first = (blk % NBLK) == 0
            t0_ = blk * TB
            xs_ = blk % 2
            X = xt[xs_]
            XK = "xt%d" % xs_
            AR.reset()
            H = hnTs[blk % 2]
            hp = blk % 2

            def norm_T(Xs, XKs, wcol, col0, Hd, hpd):
                for t in range(2):
                    act(xnb[t][:], Xs[:, t, :], AF.Square, [XKs], ["xnb%d" % t, "ssq"], accum=ssq[:, col0 + t:col0 + t + 1])
                ts("dve", rst[:, col0:col0 + 2], ssq[:, col0:col0 + 2], 1.0 / 1024, 1e-6, ALU.mult, ALU.add, ["xnb0", "xnb1", "ssq"], ["rst"])
                act(rst[:, col0:col0 + 2], rst[:, col0:col0 + 2], AF.Sqrt, ["rst"], ["rst"])
                P.op("dve", lambda e: e.reciprocal(out=rst[:, col0:col0 + 2], in_=rst[:, col0:col0 + 2]), P._expand(["rst"]), P._expand(["rst"]))
                for t in range(2):
                    act(xnb[t][:], Xs[:, t, :], AF.Copy, [XKs, "rst"], ["xnb%d" % t], scale=rst[:, col0 + t:col0 + t + 1])
                for half in range(2):
                    bt, btk = nbt()
                    for kl in range(4):
                        for t in range(2):
                            kt = half * 4 + kl
                            tr(bt[:, (kl * 2 + t) * 128:(kl * 2 + t + 1) * 128], xnb[t][:, kt * 128:(kt + 1) * 128], identb[:],
                               ["xnb%d" % t, "c_identb"], [btk])
                    for kl in range(4):
                        kt = half * 4 + kl
                        ts("dve", Hd[:, kt, :], bt[:, kl * 256:(kl + 1) * 256], wcol(kt), None, ALU.mult, None, [btk, "vecT"], [("hnT", hpd, kt)])

            def stage_A(b):
                xs2 = b % 2
                P.dma("sp", xt[xs2][:], x_d[b * TB:(b + 1) * TB, :].rearrange("(t p) d -> p t d", p=128), writes=["xt%d" % xs2])
                norm_T(xt[xs2], "xt%d" % xs2, NW, 0, hnTs[xs2], xs2)

            if stop in ("load", "S0", "S1"):
                P.dma("sp", X[:], x_d[t0_:t0_ + TB, :].rearrange("(t p) d -> p t d", p=128), writes=[XK])
                early(X, XK, t0_)
                continue
            if blk == 0:
                stage_A(0)

            def inproj_pair(c0, width, nt):
                wt, wk = wload("w_in", 0, 8, c0, nt * width)
                bk_, bkk_ = nbf()
                for ti in range(nt):
                    for kt in range(8):
                        mm(bk_[0:width, ti * 256:(ti + 1) * 256], wt[:, kt, ti * width:(ti + 1) * width], H[:, kt, :],
                           kt == 0, kt == 7, [wk, ("hnT", hp, kt)], [bkk_])
                return bk_, bkk_

            dump("hnT", H[:], [128, 8, TB], BF16, [("hnT", hp, kt) for kt in range(8)], blk)
            szT = A2("szT", [128, 12, TB], BF16, 6)
            uT = A2("uT", [128, 4, TB], BF16); sz5 = A2("sz5", [128, 4, TB], BF16)
            for pr in range(2):
                bk_, bkk_ = inproj_pair(pr * 256, 128, 2)
                cp("dve", uT[:, pr * 2:pr * 2 + 2, :], bk_[:].rearrange("p (a b) -> p a b", a=2), [bkk_], ["uT"])
            for pr in range(2):
                bk_, bkk_ = inproj_pair(512 + pr * 256, 128, 2)
                act(sz5[:, pr * 2:pr * 2 + 2, :], bk_[:].rearrange("p (a b) -> p a b", a=2), AF.Silu, [bkk_], ["sz5"])
            Ug = A2("Ug", [128, 32, 32], BF16)
            for ft in range(4):
                bk_, bkk_ = nbf()
                uv = uT[:, ft, :].rearrange("p (c j) -> p j c", j=8)
                for g8 in range(8):
                    for j in range(8):
                        mm(bk_[:, g8 * 32:(g8 + 1) * 32], sel[:, g8 * 8 + j, :], uv[:, j, :], j == 0, j == 7, ["c_sel", "uT"], [bkk_])
                cp("act" if ft % 2 else "dve", Ug[:, ft * 8:(ft + 1) * 8, :], bk_[:, 0:256].rearrange("p (a b) -> p a b", a=8), [bkk_], ["Ug"])
            St = A2("St", [128, 32, 32], F32, 2); Sw = A2("Sw", [128, 32, 32], F32, 2)
            m1 = A2("m1", [128, 16, 32]); m2 = A2("m2", [128, 16, 32])
            if first:
                mset("dve", XCS[:], 0.0, ["XCS"])
                mset("dve", XCW[:], 0.0, ["XCW"])
            for hb in range(2):
                bS, bSk = nbf()
                bW, bWk = nbf()
                for gl in range(16):
                    g = hb * 16 + gl
                    mm(bS[:, gl * 32:(gl + 1) * 32], WA[:, g, :], Ug[:, g, :], True, True, ["WA", "Ug"], [bSk])
                    mm(bW[0:64, gl * 32:(gl + 1) * 32], WA[:, g, 64:128], Ug[:, g, :], True, True, ["WA", "Ug"], [bWk])
                    mm(bW[64:128, gl * 32:(gl + 1) * 32], WA[:, g, 0:64], Ug[:, g, :], True, True, ["WA", "Ug"], [bWk])
                gs = slice(hb * 16, hb * 16 + 16)
                S3 = bS[:].rearrange("p (a b) -> p a b", a=16)
                W3 = bW[:].rearrange("p (a b) -> p a b", a=16)
                tt("dve", m1, S3, COSM[:, gs, :], ALU.mult, [bSk, "COSM"], ["m1"])
                tt("dve", m2, W3, SGNM[:, gs, :], ALU.mult, [bWk, "SGNM"], ["m2"])
                tt("pool", St[:, gs, :], m1, m2, ALU.add, ["m1", "m2"], [("St", hb)])
                tt("dve", m1, W3, COSM[:, gs, :], ALU.mult, [bWk, "COSM"], ["m1"])
                tt("dve", m2, S3, SGNM[:, gs, :], ALU.mult, [bSk, "SGNM"], ["m2"])
                tt("pool", Sw[:, gs, :], m1, m2, ALU.subtract, ["m1", "m2"], [("Sw", hb)])
            for pr in range(6):
                bk_, bkk_ = inproj_pair(1024 + pr * 256, 128, 2)
                act(szT[:, pr * 2:pr * 2 + 2, :], bk_[:].rearrange("p (a b) -> p a b", a=2), AF.Silu, [bkk_], [("szT", pr)])
            c1 = A2("c1", [128, 32]); c2 = A2("c2", [128, 32])
            tt("dve", c1, L8RE[:], XCS[:], ALU.mult, ["L8RE", "XCS"], ["c1"])
            tt("dve", c2, L8SG[:], XCW[:], ALU.mult, ["L8SG", "XCW"], ["c2"])
            tt("dve", c1, c1, c2, ALU.add, ["c1", "c2"], ["c1"])
            tt("dve", St[:, :, 0], St[:, :, 0], c1, ALU.add, [("St", 0), ("St", 1), "c1"], [("St", 0), ("St", 1)])
            tt("dve", c1, L8RE[:], XCW[:], ALU.mult, ["L8RE", "XCW"], ["c1"])
            tt("dve", c2, L8SG[:], XCS[:], ALU.mult, ["L8SG", "XCS"], ["c2"])
            tt("dve", c1, c1, c2, ALU.subtract, ["c1", "c2"], ["c1"])
            tt("dve", Sw[:, :, 0], Sw[:, :, 0], c1, ALU.add, [("Sw", 0), ("Sw", 1), "c1"], [("Sw", 0), ("Sw", 1)])
            Vt = A2("Vt", [128, 32, 32]); Vw = A2("Vw", [128, 32, 32])
            fl = lambda a: a.rearrange("p a b -> p (a b)")
            P.op("dve", lambda e: e.tensor_tensor_scan(out=fl(Vt), data0=fl(RHO0[:]), data1=fl(St), initial=0.0, op0=ALU.mult, op1=ALU.add),
                 P._expand(["RHO0", ("St", 0), ("St", 1)]), P._expand(["Vt"]))
            P.op("dve", lambda e: e.tensor_tensor_scan(out=fl(Vw), data0=fl(RHO0[:]), data1=fl(Sw), initial=0.0, op0=ALU.mult, op1=ALU.add),
                 P._expand(["RHO0", ("Sw", 0), ("Sw", 1)]), P._expand(["Vw"]))
            Xp = A2("Xp", [128, 32, 32], BF16)
            tt("dve", St, Vt, COSM[:], ALU.mult, ["Vt", "COSM", ("St", 0), ("St", 1)], [("St", 0), ("St", 1)])
            tt("pool", Sw, Vw, SGNM[:], ALU.mult, ["Vw", "SGNM", ("Sw", 0), ("Sw", 1)], [("Sw", 0), ("Sw", 1)])
            tt("dve", St, St, Sw, ALU.subtract, [("St", 0), ("St", 1), ("Sw", 0), ("Sw", 1)], [("St", 0), ("St", 1)])
            cp("act", Xp[:, :, 1:32], St[:, :, 0:31], [("St", 0), ("St", 1)], ["Xp"])
            cp("act", Xp[:, :, 0], XCS[:], ["XCS", "Xp"], ["Xp"])
            cp("dve", XCS[:], St[:, :, 31], [("St", 0), ("St", 1), "XCS"], ["XCS"])
            tt("dve", c1, Vw[:, :, 31], COSM[:, :, 31], ALU.mult, ["Vw", "COSM"], ["c1"])
            tt("dve", c2, Vt[:, :, 31], SGNM[:, :, 31], ALU.mult, ["Vt", "SGNM"], ["c2"])
            tt("dve", XCW[:], c1, c2, ALU.add, ["c1", "c2", "XCW"], ["XCW"])
            Yg = A2("Yg", [128, 32, 32], BF16)
            for hb in range(2):
                bY, bYk = nbf()
                for gl in range(16):
                    g = hb * 16 + gl
                    mm(bY[:, gl * 32:(gl + 1) * 32], TOE[:, g, :], Ug[:, g, :], True, False, ["TOE", "Ug"], [bYk])
                    mm(bY[:, gl * 32:(gl + 1) * 32], WC[:, g, :], Xp[:, g, :], False, True, ["WC", "Xp"], [bYk])
                cp("act", Yg[:, hb * 16:hb * 16 + 16, :], bY[:].rearrange("p (a b) -> p a b", a=16), [bYk], ["Yg"])
            yT = A2("yT", [128, 4, TB], BF16, 4)
            ytmp = A2("ytmp", [128, TB])
            for ft in range(4):
                bk_, bkk_ = nbf()
                for j in range(8):
                    for g8 in range(8):
                        mm(bk_[:, j * 32:(j + 1) * 32], sel[:, j * 8 + g8, :], Yg[:, ft * 8 + g8, :], g8 == 0, g8 == 7, ["c_sel", "Yg"], [bkk_])
                stt("dve", ytmp.rearrange("p (c j) -> p j c", j=8), uT[:, ft, :].rearrange("p (c j) -> p j c", j=8), D5(ft),
                    bk_[:, 0:256].rearrange("p (j c) -> p j c", j=8), ALU.mult, ALU.add, ["uT", "vecT", bkk_], ["ytmp"])
                act(yT[:, ft, :], ytmp, AF.Gelu, ["ytmp"], [("yT", ft)])
            dump("uT", uT, [128, 4, TB], BF16, ["uT"], blk)
            dump("Ug", Ug, [128, 32, 32], BF16, ["Ug"], blk)
            dump("Xp", Xp, [128, 32, 32], BF16, ["Xp"], blk)
            dump("Yg", Yg, [128, 32, 32], BF16, ["Yg"], blk)
            dump("yT", yT, [128, 4, TB], BF16, [("yT", f_) for f_ in range(4)], blk)
            sgl = A2("sgl", [128, TB], BF16)
            for ft in range(4):
                bk_, bkk_ = nbf()
                for kt in range(4):
                    mm(bk_[:, 0:256], wglu[:, kt, ft * 128:(ft + 1) * 128], yT[:, kt, :], kt == 0, kt == 3, ["wglu", ("yT", kt)], [bkk_])
                act(sgl, bk_[:, 0:256], AF.Sigmoid, [bkk_, "vecT"], ["sgl"], bias=BGL(ft))
                tt("dve", sgl, sgl, yT[:, ft, :], ALU.mult, ["sgl", ("yT", ft)], ["sgl"])
                tt("pool", y5T[:, ft, :], sgl, sz5[:, ft, :], ALU.mult, ["sgl", "sz5"], [("y5T", ft)])

            dump("y5T", y5T[:], [128, 4, TB], BF16, [("y5T", f_) for f_ in range(4)], blk)
            if stop == "C":
                early(X, XK, t0_)
                continue
            AR.reset()
            szT = A2("szT", [128, 12, TB], BF16, 6)
            cvT = A2("cvT", [128, 20, TB], BF16, 10)
            xraws = [A2("xraw%d" % q, [128, 2, 260], BF16) for q in range(2)]
            accs = [A2("acc%d" % q, [128, 2, TB], F32, 2) for q in range(2)]
            if first:
                mset("pool", hist[:], 0.0, ["hist"])
                mset("pool", prevS[:], 0.0, ["prevS"])
            for pr in range(10):
                q = pr % 2
                xraw = xraws[q]; acc = accs[q]; xk = "xraw%d" % q; ak = "acc%d" % q
                bk_, bkk_ = inproj_pair(2560 + pr * 256, 128, 2)
                cp("pool", xraw[:, :, 0:3], hist[:, pr * 2:pr * 2 + 2, 0:3], ["hist"], [xk])
                cp("act", xraw[:, :, 3:259], bk_[:].rearrange("p (a b) -> p a b", a=2), [bkk_, xk], [xk])
                cp("pool", hist[:, pr * 2:pr * 2 + 2, 0:3], xraw[:, :, 256:259], [xk, "hist"], ["hist"])
                for ti in range(2):
                    i = pr * 2 + ti
                    act(acc[:, ti, :], xraw[:, ti, 0:256], AF.Identity, [xk, "vecT"], [(ak, ti)], bias=CB(i), scale=CW(i, 0))
                for k in (1, 2, 3):
                    for ti in range(2):
                        i = pr * 2 + ti
                        stt("dve", acc[:, ti, :], xraw[:, ti, k:k + 256], CW(i, k), acc[:, ti, :], ALU.mult, ALU.add, [xk, "vecT", (ak, ti)], [(ak, ti)])
                if pr > 0:
                    qp = (pr - 1) % 2
                    act(cvT[:, (pr - 1) * 2:(pr - 1) * 2 + 2, :], accs[qp], AF.Silu, [("acc%d" % qp, 0), ("acc%d" % qp, 1)], [("cvT", pr - 1)])
            act(cvT[:, 18:20, :], accs[1], AF.Silu, [("acc1", 0), ("acc1", 1)], [("cvT", 9)])
            CV = lambda i: ("cvT", i // 2)
            dump("cvT", cvT, [128, 20, TB], BF16, [("cvT", f_) for f_ in range(10)], blk)
            wt, wk = wload("w_in", 0, 8, 5120, 24)
            bk_, bkk_ = nbf()
            for kt in range(8):
                mm(bk_[0:24, 0:256], wt[:, kt, 0:24], H[:, kt, :], kt == 0, kt == 7, [wk, ("hnT", hp, kt)], [bkk_])
            dtT = A2("dtT", [24, TB]); daT = A2("daT", [24, TB]); AT = A2("AT", [24, TB])
            AThi = A2("AThi", [24, TB], BF16); ATlo = A2("ATlo", [24, TB], BF16)
            act(dtT, bk_[0:24, 0:256], AF.Exp, [bkk_, "dtb"], ["dtT"], bias=dtb[:, 0:1])
            act(dtT, dtT, AF.Ln, ["dtT"], ["dtT"], bias=1.0)
            ts("dve", daT, dtT, aneg[:, 0:1], None, ALU.mult, None, ["dtT", "aneg"], ["daT"])
            P.op("dve", lambda e: e.tensor_tensor_scan(out=AT, data0=rmask[:], data1=daT, initial=0.0, op0=ALU.mult, op1=ALU.add),
                 P._expand(["c_rmask", "daT"]), P._expand(["AT"]))
            cp("dve", AThi, AT, ["AT"], ["AThi"])
            tt("dve", ATlo, AT, AThi, ALU.subtract, ["AT", "AThi"], ["ATlo"])
            dump("dtT", dtT, [24, TB], F32, ["dtT"], blk)
            dump("AT", AT, [24, TB], F32, ["AT"], blk)
            dtk = A2("dtk", [128, 2, 24]); nAc = A2("nAc", [128, 2, 24])
            dec = A2("dec", [128, 2, 24]); cdec = A2("cdec", [128, 2, 24]); dtd = A2("dtd", [128, 2, 24])
            rhd = A2("rhd", [24, 2, 24])
            bk_, bkk_ = nbf()
            for ck in range(2):
                tr(bk_[:, ck * 24:(ck + 1) * 24], dtT[:, ck * 128:(ck + 1) * 128], identf[0:24, 0:24], ["dtT", "c_identf"], [bkk_])
                tr(bk_[:, 48 + ck * 24:48 + (ck + 1) * 24], AT[:, ck * 128:(ck + 1) * 128], identf[0:24, 0:24], ["AT", "c_identf"], [bkk_])
                ts("dve", rhd[:, ck, :], identf[0:24, 0:24], AT[:, ck * 128 + 127:ck * 128 + 128], None, ALU.mult, None, ["c_identf", "AT"], ["rhd"])
            mm(bk_[:, 96:144], onesf[0:24, :], rhd.rearrange("p a b -> p (a b)"), True, True, ["c_onesf", "rhd"], [bkk_])
            cp("dve", dtk, bk_[:, 0:48].rearrange("p (a b) -> p a b", a=2), [bkk_], ["dtk"])
            ts("dve", nAc, bk_[:, 48:96].rearrange("p (a b) -> p a b", a=2), -1.0, None, ALU.mult, None, [bkk_], ["nAc"])
            act(cdec, bk_[:, 96:144].rearrange("p (a b) -> p a b", a=2), AF.Exp, [bkk_], ["cdec"])
            tt("dve", dec, bk_[:, 96:144].rearrange("p (a b) -> p a b", a=2), nAc, ALU.add, [bkk_, "nAc"], ["dec"])
            act(dec, dec, AF.Exp, ["dec"], ["dec"])
            tt("dve", dtd, dtk, dec, ALU.mult, ["dtk", "dec"], ["dtd"])
            xdt = A2("xdt", [128, 2, 1536], BF16, 2); xdd = A2("xdd", [128, 2, 1536], BF16, 2)
            Btk = A2("Btk", [128, 2, 512], BF16, 2)
            for ck in range(2):
                cs = slice(ck * 128, (ck + 1) * 128)
                for (i0, n) in ((0, 8), (8, 4)):
                    bt, btk = nbt()
                    for ii in range(n):
                        tr(bt[:, ii * 128:(ii + 1) * 128], cvT[:, i0 + ii, cs], identb[:], [CV(i0 + ii), "c_identb"], [btk])
                    h0 = i0 * 2
                    nh = n * 2
                    src3 = bt[:, 0:n * 128].rearrange("p (h d) -> p h d", d=64)
                    tt("dve", xdt[:, ck, h0 * 64:(h0 + nh) * 64].rearrange("p (h d) -> p h d", d=64), src3,
                       dtk[:, ck, h0:h0 + nh].unsqueeze(2).to_broadcast([128, nh, 64]), ALU.mult, [btk, "dtk"], [("xdt", ck)])
                    tt("dve", xdd[:, ck, h0 * 64:(h0 + nh) * 64].rearrange("p (h d) -> p h d", d=64), src3,
                       dtd[:, ck, h0:h0 + nh].unsqueeze(2).to_broadcast([128, nh, 64]), ALU.mult, [btk, "dtd"], [("xdd", ck)])
                bt, btk = nbt()
                for gq in range(4):
                    tr(bt[:, gq * 128:(gq + 1) * 128], cvT[:, 12 + gq, cs], identb[:], [CV(12 + gq), "c_identb"], [btk])
                cp("act", Btk[:, ck, :], bt[:, 0:512], [btk], [("Btk", ck)])
            pvb = A2("pvb", [128, 2, 1536], BF16, 2)
            for ck in range(2):
                cp("act", pvb[:, ck, :], prevS[:], ["prevS"], [("pvb", ck)])
                tt("pool", prevS[:].rearrange("p (h d) -> p h d", d=64), prevS[:].rearrange("p (h d) -> p h d", d=64),
                   cdec[:, ck, :].unsqueeze(2).to_broadcast([128, 24, 64]), ALU.mult, ["prevS", "cdec"], ["prevS"])
                for gq in range(4):
                    bk_, bkk_ = nbf()
                    mm(bk_[:, 0:384], Btk[:, ck, gq * 128:(gq + 1) * 128], xdd[:, ck, gq * 384:(gq + 1) * 384], True, True,
                       [("Btk", ck), ("xdd", ck)], [bkk_])
                    tt("dve", prevS[:, gq * 384:(gq + 1) * 384], prevS[:, gq * 384:(gq + 1) * 384], bk_[:, 0:384], ALU.add, ["prevS", bkk_], ["prevS"])
            NBUF = 4
            ET = [A2("ET%d" % i, [128, TB], BF16, 2) for i in range(NBUF)]
            EA = [A2("EA%d" % i, [128, TB], BF16) for i in range(NBUF)]
            CD = [A2("CD%d" % i, [128, TB], BF16) for i in range(NBUF)]
            WT = [A2("WT%d" % i, [128, TB], BF16) for i in range(NBUF)]
            ygT = A2("ygT", [128, 3, TB], F32, 3); sq = A2("sq", [128, 3, TB], BF16, 3); rsr = A2("rsr", [128, TB])

            def head_decay(gq, hd, bSc, bSck):
                b_ = hd % NBUF
                bA_, bAk_ = nbf()
                mm(bA_[:, 0:256], identb[0:24, hd:hd + 1].to_broadcast([24, 128]), AThi, True, False, ["c_identb", "AThi"], [bAk_])
                mm(bA_[:, 0:256], identb[0:24, hd:hd + 1].to_broadcast([24, 128]), ATlo, False, True, ["c_identb", "ATlo"], [bAk_])
                mm(bA_[:, 256:512], identb[0:24, hd:hd + 1].to_broadcast([24, 128]), AThi, True, False, ["c_identb", "AThi"], [bAk_])
                mm(bA_[:, 256:512], identb[0:24, hd:hd + 1].to_broadcast([24, 128]), ATlo, False, False, ["c_identb", "ATlo"], [bAk_])
                mm(bA_[:, 256:512], negbig[:], gtm[:], False, True, ["c_negbig", "c_gt"], [bAk_])
                act(EA[b_], bA_[:, 0:256], AF.Exp, [bAk_], ["EA%d" % b_])
                for ck in range(2):
                    cs = slice(ck * 128, (ck + 1) * 128)
                    act(ET[b_][:, cs], bA_[:, 256 + ck * 128:256 + (ck + 1) * 128], AF.Exp, [bAk_, "nAc"], [("ET%d" % b_, ck)],
                        bias=nAc[:, ck, hd:hd + 1])
                tt("dve", WT[b_], bSc[:, 0:256], ET[b_], ALU.mult, [bSck, ("ET%d" % b_, 0), ("ET%d" % b_, 1)], ["WT%d" % b_])
                tt("pool", CD[b_], cvT[:, 16 + gq, :], EA[b_], ALU.mult, [CV(16 + gq), "EA%d" % b_], ["CD%d" % b_])

            def head_pair_y(gq, hd):
                i = hd // 2
                bY_, bYk_ = nbf()
                for ck in range(2):
                    cs = slice(ck * 128, (ck + 1) * 128)
                    mm(bY_[:, cs], dsk[:, i, :], cvT[:, i, cs], True, False, ["dsk", CV(i)], [bYk_])
                    for hh in range(2):
                        h2 = hd - 1 + hh
                        b2 = h2 % NBUF
                        mm(bY_[64 * hh:64 * hh + 64, cs], xdt[:, ck, h2 * 64:(h2 + 1) * 64], WT[b2][:, cs], False, False,
                           [("xdt", ck), "WT%d" % b2], [bYk_])
                        mm(bY_[64 * hh:64 * hh + 64, cs], pvb[:, ck, h2 * 64:(h2 + 1) * 64], CD[b2][:, cs], False, hh == 1,
                           [("pvb", ck), "CD%d" % b2], [bYk_])
                il = i % 3
                tt("dve", ygT[:, il, :], bY_[:, 0:256], szT[:, i, :], ALU.mult, [bYk_, ("szT", i // 2)], [("ygT", il)])
                tt("dve", sq[:, il, :], ygT[:, il, :], ygT[:, il, :], ALU.mult, [("ygT", il)], [("sq", il)])

            for gq in range(4):
                bSc, bSck = pscs[gq % 2], "psc%d" % (gq % 2)
                for ck in range(2):
                    cs = slice(ck * 128, (ck + 1) * 128)
                    mm(bSc[:, cs], cvT[:, 12 + gq, cs], cvT[:, 16 + gq, cs], True, True, [CV(12 + gq), CV(16 + gq)], [bSck])
                h0 = gq * 6
                head_decay(gq, h0, bSc, bSck)
                head_decay(gq, h0 + 1, bSc, bSck)
                head_decay(gq, h0 + 2, bSc, bSck)
                head_decay(gq, h0 + 3, bSc, bSck)
                head_pair_y(gq, h0 + 1)
                head_decay(gq, h0 + 4, bSc, bSck)
                head_decay(gq, h0 + 5, bSc, bSck)
                head_pair_y(gq, h0 + 3)
                head_pair_y(gq, h0 + 5)
                bR, bRk = nbf()
                for il in range(3):
                    mm(bR[:, 0:256], onesb[:], sq[:, il, :], il == 0, il == 2, ["c_onesb", ("sq", il)], [bRk])
                ts("dve", rsr, bR[:, 0:256], 1.0 / 384, 1e-6, ALU.mult, ALU.add, [bRk], ["rsr"])
                act(rsr, rsr, AF.Ln, ["rsr"], ["rsr"])
                act(rsr, rsr, AF.Exp, ["rsr"], ["rsr"], scale=-0.5)
                for il in range(3):
                    i = gq * 3 + il
                    stt("dve", yssT[:, i, :], ygT[:, il, :], SNW(i), rsr, ALU.mult, ALU.mult, [("ygT", il), "vecT", "rsr"], [("yssT", i)])

            dump("yssT", yssT[:], [128, 12, TB], BF16, [("yssT", f_) for f_ in range(12)], blk)
            dump("prevS", prevS[:], [128, 1536], F32, ["prevS"], blk)
            if stop == "D":
                early(X, XK, t0_)
                continue
            AR.reset()
            gT = A2("gT", [128, 16, TB], BF16, 8)
            mgT = A2("mgT", [128, 8, TB], BF16, 8)
            for pr in range(8):
                bk_, bkk_ = inproj_pair(5144 + pr * 256, 128, 2)
                act(gT[:, pr * 2:pr * 2 + 2, :], bk_[:].rearrange("p (a b) -> p a b", a=2), AF.Sigmoid, [bkk_], [("gT", pr)])
            if blk + 1 < NB:
                stage_A(blk + 1)
            e1t = A2("e1t", [128, TB]); e2t = A2("e2t", [128, TB])
            for dp in range(4):
                wa, wak = wload("w_brs", 0, 8, dp * 256, 256)
                wb, wbk = wload("w_brs", 8, 4, dp * 256, 256)
                for dl in range(2):
                    d_ = dp * 2 + dl
                    bk_, bkk_ = nbf()
                    for kt in range(4):
                        mm(bk_[:, 0:256], wbr5[:, kt, d_ * 128:(d_ + 1) * 128], y5T[:, kt, :], kt == 0, kt == 3, ["wbr5", ("y5T", kt)], [bkk_])
                    for kt in range(12):
                        w_, wk_ = (wa, wak) if kt < 8 else (wb, wbk)
                        mm(bk_[:, 256:512], w_[:, kt % 8, dl * 128:(dl + 1) * 128], yssT[:, kt, :], kt == 0, kt == 11, [wk_, ("yssT", kt)], [bkk_])
                    tt("dve", e1t, bk_[:, 0:256], gT[:, d_, :], ALU.mult, [bkk_, ("gT", d_ // 2)], ["e1t"])
                    tt("dve", e2t, bk_[:, 256:512], gT[:, 8 + d_, :], ALU.mult, [bkk_, ("gT", 4 + d_ // 2)], ["e2t"])
                    tt("pool", mgT[:, d_, :], e1t, e2t, ALU.add, ["e1t", "e2t"], [("mgT", d_)])
            MG = [("mgT", d_) for d_ in range(8)]
            dump("mgT", mgT, [128, 8, TB], BF16, MG, blk)
            dump("gT", gT, [128, 16, TB], BF16, [("gT", f_) for f_ in range(8)], blk)
            for dc in range(4):
                wt, wk = wload("w_out", 0, 8, dc * 256, 256)
                bk_, bkk_ = nbf()
                for t in range(2):
                    for kt in range(8):
                        mm(bk_[:, t * 256:(t + 1) * 256], mgT[:, kt, t * 128:(t + 1) * 128], wt[:, kt, :], kt == 0, kt == 7, [wk, ("mgT", kt)], [bkk_])
                tt("dve", X[:, :, dc * 256:(dc + 1) * 256], X[:, :, dc * 256:(dc + 1) * 256], bk_[:].rearrange("p (a b) -> p a b", a=2),
                   ALU.add, [XK, bkk_], [XK])
            norm_T(X, XK, PNW, 2, H, hp)
            pf = A2("pf", [128, 2, 256]); pb = A2("pb", [128, 2, 256], BF16); pT = A2("pT", [128, 2, TB], BF16)
            P.dma("pool", pf, p_d[t0_:t0_ + TB, :].rearrange("(t p) d -> p t d", p=128), writes=["pf"])
            cp("act", pb, pf, ["pf"], ["pb"])
            bt, btk = nbt()
            for kt in range(2):
                for t in range(2):
                    tr(bt[:, (kt * 2 + t) * 128:(kt * 2 + t + 1) * 128], pb[:, t, kt * 128:(kt + 1) * 128], identb[:], ["pb", "c_identb"], [btk])
            cp("dve", pT, bt[:, 0:512].rearrange("p (a b) -> p a b", a=2), [btk], ["pT"])
            sgp = A2("sgp", [128, 2, 256]); e3t = A2("e3t", [128, 2, 256])
            for dc in range(4):
                wt, wk = wload("w_pg", 0, 8, dc * 256, 256)
                bG, bGk = nbf()
                bP, bPk = nbf()
                for t in range(2):
                    for kt in range(8):
                        mm(bG[:, t * 256:(t + 1) * 256], H[:, kt, t * 128:(t + 1) * 128], wt[:, kt, :], kt == 0, kt == 7, [wk, ("hnT", hp, kt)], [bGk])
                    for kt in range(2):
                        mm(bP[:, t * 256:(t + 1) * 256], pT[:, kt, t * 128:(t + 1) * 128], wpp[:, kt, dc * 256:(dc + 1) * 256], kt == 0, kt == 1,
                           ["wpp", "pT"], [bPk])
                act(sgp, bG[:].rearrange("p (a b) -> p a b", a=2), AF.Sigmoid, [bGk], ["sgp"])
                tt("dve", e3t, bP[:].rearrange("p (a b) -> p a b", a=2), sgp, ALU.mult, [bPk, "sgp"], ["e3t"])
                tt("pool", X[:, :, dc * 256:(dc + 1) * 256], X[:, :, dc * 256:(dc + 1) * 256], e3t, ALU.add, [XK, "e3t"], [XK])
            ob = A2("ob", [128, 2, 1024], F32, 2)
            for t in range(2):
                act(xnb[t][:], X[:, t, :], AF.Square, [XK], ["xnb%d" % t, "ssq"], accum=ssq[:, 4 + t:5 + t])
            ts("dve", rst[:, 4:6], ssq[:, 4:6], 1.0 / 1024, 1e-6, ALU.mult, ALU.add, ["xnb0", "xnb1", "ssq"], ["rst"])
            act(rst[:, 4:6], rst[:, 4:6], AF.Sqrt, ["rst"], ["rst"])
            P.op("dve", lambda e: e.reciprocal(out=rst[:, 4:6], in_=rst[:, 4:6]), P._expand(["rst"]), P._expand(["rst"]))
            for t in range(2):
                act(ob[:, t, :], X[:, t, :], AF.Copy, [XK, "rst"], [("ob", t)], scale=rst[:, 4 + t:5 + t])
                tt("dve" if t == 0 else "pool", ob[:, t, :], ob[:, t, :], fnw[:], ALU.mult, [("ob", t), "fnw"], [("ob", t)])
            out_stores.append(P.dma("pool", out_d[t0_:t0_ + TB, :].rearrange("(t p) d -> p t d", p=128), ob, reads=[("ob", 0), ("ob", 1)]))

        P.emit(final_wait_ops=out_stores)
    return nc


_NC_CACHE = {}


def _prep_shared(inp):
    f = lambda a: np.ascontiguousarray(np.asarray(a, dtype=np.float32))
    sh = {
        "norm_w": f(inp["norm_w"]).reshape(8, 128),
        "w_in": f(inp["w_in"]).reshape(1024, 7192),
        "s5_a_re": f(inp["s5_a_re"]).reshape(32, 64),
        "s5_a_im": f(inp["s5_a_im"]).reshape(32, 64),
        "s5_b_re": f(inp["s5_b_re"]).reshape(32, 64, 16),
        "s5_b_im": f(inp["s5_b_im"]).reshape(32, 64, 16),
        "s5_c_re": f(inp["s5_c_re"]).reshape(512, 64),
        "s5_c_im": f(inp["s5_c_im"]).reshape(512, 64),
        "s5_d": f(inp["s5_d"]).reshape(4, 128),
        "s5_log_step": f(inp["s5_log_step"]).reshape(32),
        "s5_w_glu": f(inp["s5_w_glu"]).reshape(512, 512),
        "s5_b_glu": f(inp["s5_b_glu"]).reshape(4, 128),
        "ssd_conv_w": f(inp["ssd_conv_w"]).reshape(80, 128),
        "ssd_conv_b": f(inp["ssd_conv_b"]).reshape(20, 128),
        "ssd_dt_bias": f(inp["ssd_dt_bias"]).reshape(24, 1),
        "ssd_a_log": f(inp["ssd_a_log"]).reshape(24, 1),
        "ssd_d": f(inp["ssd_d"]).reshape(24),
        "ssd_norm_w": f(inp["ssd_norm_w"]).reshape(12, 128),
        "w_br_s5": f(inp["w_br_s5"]).reshape(512, 1024),
        "w_br_ssd": f(inp["w_br_ssd"]).reshape(1536, 1024),
        "w_out": f(inp["w_out"]).reshape(1024, 1024),
        "ple_norm_w": f(inp["ple_norm_w"]).reshape(8, 128),
        "w_ple_gate": f(inp["w_ple_gate"]).reshape(1024, 1024),
        "w_ple_proj": f(inp["w_ple_proj"]).reshape(256, 1024),
        "final_norm_w": f(inp["final_norm_w"]).reshape(1024),
    }
    sh.update(_consts())
    return sh


def run(inputs, n_cores=8):
    x = np.asarray(inputs["x"], dtype=np.float32)
    p = np.asarray(inputs["p"], dtype=np.float32)
    B, L, Dm = x.shape
    assert B % n_cores == 0 and L % TB == 0
    NSEQ = B // n_cores
    NBLK = L // TB
    key = (NSEQ, NBLK)
    import os
    if key not in _NC_CACHE:
        _NC_CACHE[key] = build(NSEQ, NBLK, stop=os.environ.get("KSTOP"), dbg=bool(os.environ.get("KDBG")))
    nc = _NC_CACHE[key]
    sh = _prep_shared(inputs)
    xs = x.reshape(n_cores, NSEQ * L, Dm)
    ps = p.reshape(n_cores, NSEQ * L, 256)
    in_maps = []
    for c in range(n_cores):
        m = dict(sh)
        m["x"] = np.ascontiguousarray(xs[c])
        m["p"] = np.ascontiguousarray(ps[c])
        in_maps.append(m)
    res = run_bass_kernel_spmd(nc, in_maps, core_ids=list(range(n_cores)))
    global LAST_RES
    LAST_RES = res.results
    out = np.stack([np.asarray(r["out"], dtype=np.float32) for r in res.results], axis=0)
    return out.reshape(B, L, Dm)


def kernel(**inputs):
    return run(inputs, 8)
```

```python
import math
from contextlib import ExitStack

import numpy as np
import ml_dtypes
import concourse.bass as bass
import concourse.mybir as mybir
from concourse.bass_utils import run_bass_kernel_spmd

F32 = mybir.dt.float32
BF16 = mybir.dt.bfloat16
I32 = mybir.dt.int32
AF = mybir.ActivationFunctionType
ALU = mybir.AluOpType

TB = 256
NWSL = 4
PI = math.pi
TWO_PI = 2.0 * math.pi


class _Op:
    __slots__ = ("eng", "fn", "reads", "writes", "deps", "sig", "dma_sem", "dma_val", "is_dma", "waits")

    def __init__(self, eng, fn, reads, writes, is_dma):
        self.eng = eng
        self.fn = fn
        self.reads = reads
        self.writes = writes
        self.deps = []
        self.sig = None
        self.is_dma = is_dma
        self.dma_sem = None
        self.dma_val = None
        self.waits = []


class Prog:
    ENGS = ("pe", "act", "dve", "pool", "sp")

    def __init__(self, nc, n_dma_sems=40, same_engine_sync=True):
        self.nc = nc
        self.ops = []
        self.last_writer = {}
        self.readers = {}
        self.n_dma_sems = n_dma_sems
        self.same_engine_sync = same_engine_sync
        self.alias = {}

    def _expand(self, keys):
        out = []
        for k in keys:
            a = self.alias.get(k)
            if a is None:
                out.append(k)
            else:
                out.extend(a)
        return tuple(out)

    def _add(self, op):
        op.reads = self._expand(op.reads)
        op.writes = self._expand(op.writes)
        deps = set()
        for r in op.reads:
            w = self.last_writer.get(r)
            if w is not None:
                deps.add(w)
        for r in op.writes:
            w = self.last_writer.get(r)
            if w is not None and (op.is_dma or self.ops[w].is_dma or self.ops[w].eng != op.eng):
                deps.add(w)
            for rd in self.readers.get(r, {}).values():
                if op.is_dma or self.ops[rd].is_dma or self.ops[rd].eng != op.eng:
                    deps.add(rd)
        idx = len(self.ops)
        op.deps = sorted(deps)
        self.ops.append(op)
        for r in op.writes:
            self.last_writer[r] = idx
            self.readers[r] = {}
        for r in op.reads:
            if r not in op.writes:
                k = ("dma", idx) if op.is_dma else op.eng
                self.readers.setdefault(r, {})[k] = idx
        return idx

    def op(self, eng, fn, reads=(), writes=()):
        return self._add(_Op(eng, fn, tuple(reads), tuple(writes), False))

    def dma(self, queue, out, in_, reads=(), writes=(), **kw):
        def fn(e, out=out, in_=in_, kw=kw):
            return e.dma_start(out=out, in_=in_, allow_slow_non_contiguous=True, **kw)
        return self._add(_Op(queue, fn, tuple(reads), tuple(writes), True))

    def emit(self, final_wait_ops=()):
        nc = self.nc
        ops = self.ops
        needed = [False] * len(ops)
        for o in ops:
            for d in o.deps:
                needed[d] = True
        for i in final_wait_ops:
            needed[i] = True
        sig_cnt = {e: 0 for e in self.ENGS}
        dma_cnt = [0] * self.n_dma_sems
        half = self.n_dma_sems // 2
        pools = {"sp": list(range(0, half)), "pool": list(range(half, self.n_dma_sems))}
        dma_rr = {"sp": 0, "pool": 0}
        for i, o in enumerate(ops):
            if o.is_dma:
                pl = pools[o.eng]
                o.dma_sem = pl[dma_rr[o.eng] % len(pl)]
                dma_rr[o.eng] += 1
                dma_cnt[o.dma_sem] += 1
                o.dma_val = 16 * dma_cnt[o.dma_sem]
            elif needed[i]:
                sig_cnt[o.eng] += 1
                o.sig = sig_cnt[o.eng]
        self.sig_cnt = sig_cnt
        waited = {e: {} for e in self.ENGS}
        for i, o in enumerate(ops):
            need = {}
            if o.is_dma and o.dma_val > 16:
                need[("dma", o.dma_sem)] = o.dma_val - 16
            for d in o.deps:
                p = ops[d]
                if p.is_dma:
                    k = ("dma", p.dma_sem)
                    v = p.dma_val
                else:
                    if p.eng == o.eng and (p.eng == "pe" or not self.same_engine_sync):
                        continue
                    k = ("eng", p.eng)
                    v = p.sig
                if need.get(k, 0) < v:
                    need[k] = v
            w = waited[o.eng]
            for k, v in need.items():
                if w.get(k, 0) < v:
                    w[k] = v
                    o.waits.append((k, v))
        final = []
        for i in final_wait_ops:
            p = ops[i]
            if p.is_dma:
                final.append((("dma", p.dma_sem), p.dma_val))
            else:
                final.append((("eng", p.eng), p.sig))
        with ExitStack() as st:
            sems = {}
            for e in self.ENGS:
                sems[("eng", e)] = st.enter_context(nc.semaphore("s_" + e))
            for j in range(self.n_dma_sems):
                sems[("dma", j)] = st.enter_context(nc.semaphore("s_dma%d" % j))
            block = st.enter_context(nc.Block())
            per = {e: [o for o in ops if o.eng == e] for e in self.ENGS}

            def run(e, engobj):
                for o in per[e]:
                    for k, v in o.waits:
                        engobj.wait_ge(sems[k], v)
                    ins = o.fn(engobj)
                    if o.is_dma:
                        ins.then_inc(sems[("dma", o.dma_sem)], 16)
                    elif o.sig is not None:
                        ins.then_inc(sems[("eng", e)], 1)
                if e == "sp":
                    for k, v in final:
                        engobj.wait_ge(sems[k], v)

            @block.tensor
            def _(eng):
                run("pe", eng)

            @block.scalar
            def _(eng):
                run("act", eng)

            @block.vector
            def _(eng):
                run("dve", eng)

            @block.gpsimd
            def _(eng):
                run("pool", eng)

            @block.sync
            def _(eng):
                run("sp", eng)


def _consts():
    bf = ml_dtypes.bfloat16
    c = {}
    c["c_identb"] = np.eye(128, dtype=np.float32).astype(bf)
    c["c_identf"] = np.eye(128, dtype=np.float32)
    c["c_onesb"] = np.ones((128, 128), np.float32).astype(bf)
    c["c_onesf"] = np.ones((128, 128), np.float32)
    sel = np.zeros((128, 64, 128), np.float32)
    selT = np.zeros((128, 64, 128), np.float32)
    for g8 in range(8):
        for j in range(8):
            for h in range(16):
                sel[g8 * 16 + h, g8 * 8 + j, j * 16 + h] = 1.0
                selT[j * 16 + h, j * 8 + g8, g8 * 16 + h] = 1.0
    c["c_sel"] = sel.reshape(128, 64 * 128).astype(bf)
    c["c_negbig"] = (-30000.0 * np.eye(128, dtype=np.float32)).astype(bf)
    k = np.arange(128)[:, None]
    l = np.arange(128)[None, :]
    gt = (k > l).astype(np.float32)
    c["c_gt"] = np.concatenate([gt, gt], axis=1).astype(bf)
    oh = np.zeros((24, 24, 128), np.float32)
    for hd in range(24):
        oh[hd, hd, :] = 1.0
    rm = np.ones((24, TB), np.float32)
    rm[:, 0::128] = 0.0
    c["c_rmask"] = rm
    jp = (np.arange(128) // 16)[:, None]
    jj = (np.arange(128) // 16)[None, :]
    c["c_bmask"] = (jj >= jp).astype(np.float32)
    kv = np.zeros((128, 16, 32), np.float32)
    for ki in range(16):
        kv[:, ki, :] = ki - 7
    c["c_kv"] = kv.reshape(128, 512)
    cv = np.zeros((128, 32, 32), np.float32)
    cpos = np.ones((128, 32, 32), np.float32)
    for cc in range(32):
        cv[:, :, cc] = 8.0 * cc
    cpos[:, :, 0] = 0.0
    c["c_cv"] = cv.reshape(128, 1024)
    c["c_cpos"] = cpos.reshape(128, 1024)
    sg = np.zeros((128, 2), np.float32)
    sg[:64, 0] = -1.0
    sg[64:, 0] = 1.0
    sg[:64, 1] = 1.0
    sg[64:, 1] = -1.0
    c["c_sg"] = sg
    return c


_CONST_SHAPES = None


def build(NSEQ, NBLK, stop=None, dbg=False):
    NB = NSEQ * NBLK
    NTOK = NB * TB
    nc = bass.Bass("TRN2", target_bir_lowering=False)

    def din(name, shape, dt=F32):
        return nc.dram_tensor(name, list(shape), dt, kind="ExternalInput").ap()

    x_d = din("x", [NTOK, 1024])
    p_d = din("p", [NTOK, 256])
    out_d = nc.dram_tensor("out", [NTOK, 1024], F32, kind="ExternalOutput").ap()
    norm_w_d = din("norm_w", [8, 128])
    w_in_d = din("w_in", [1024, 7192])
    a_re_d = din("s5_a_re", [32, 64])
    a_im_d = din("s5_a_im", [32, 64])
    b_re_d = din("s5_b_re", [32, 64, 16])
    b_im_d = din("s5_b_im", [32, 64, 16])
    c_re_d = din("s5_c_re", [512, 64])
    c_im_d = din("s5_c_im", [512, 64])
    s5_d_d = din("s5_d", [4, 128])
    lstep_d = din("s5_log_step", [32])
    w_glu_d = din("s5_w_glu", [512, 512])
    b_glu_d = din("s5_b_glu", [4, 128])
    conv_w_d = din("ssd_conv_w", [80, 128])
    conv_b_d = din("ssd_conv_b", [20, 128])
    dt_bias_d = din("ssd_dt_bias", [24, 1])
    a_log_d = din("ssd_a_log", [24, 1])
    ssd_d_d = din("ssd_d", [24])
    ssd_nw_d = din("ssd_norm_w", [12, 128])
    w_br5_d = din("w_br_s5", [512, 1024])
    w_brs_d = din("w_br_ssd", [1536, 1024])
    w_out_d = din("w_out", [1024, 1024])
    ple_nw_d = din("ple_norm_w", [8, 128])
    w_pg_d = din("w_ple_gate", [1024, 1024])
    w_pp_d = din("w_ple_proj", [256, 1024])
    fnw_d = din("final_norm_w", [1024])
    cd = {}
    for name, arr in _consts().items():
        cd[name] = din(name, arr.shape, BF16 if arr.dtype == ml_dtypes.bfloat16 else F32)

    scr = {
        "w_in": nc.dram_tensor("scr_w_in", [1024, 7192], BF16, kind="ExternalOutput").ap(),
        "w_brs": nc.dram_tensor("scr_w_brs", [1536, 1024], BF16, kind="ExternalOutput").ap(),
        "w_out": nc.dram_tensor("scr_w_out", [1024, 1024], BF16, kind="ExternalOutput").ap(),
        "w_pg": nc.dram_tensor("scr_w_pg", [1024, 1024], BF16, kind="ExternalOutput").ap(),
    }

    with ExitStack() as st:
        def sb(name, shape, dt=F32):
            return st.enter_context(nc.sbuf_tensor(name, list(shape), dt))

        def psum(name, shape, dt):
            return st.enter_context(nc.psum_tensor(name, list(shape), dt))

        P = Prog(nc)
        P.alias["xt1"] = [("xt1", 0), ("xt1", 1)]
        psf = [psum("psf%d" % i, [128, 512], F32) for i in range(4)]
        pst = [psum("pst%d" % i, [128, 1024], BF16) for i in range(2)]
        pscs = [psum("psc%d" % i, [128, 512], F32) for i in range(2)]
        st_ = {"f": 0, "t": 0, "w": 0, "cast": 0, "half": 0}

        def nbf():
            i = st_["f"]
            st_["f"] = (i + 1) % 4
            return psf[i], "psf%d" % i

        def nbt():
            i = st_["t"]
            st_["t"] = (i + 1) % 2
            return pst[i], "pst%d" % i

        def mm(out, lhsT, rhs, start, stop, reads, writes):
            P.op("pe", lambda e: e.matmul(out, lhsT=lhsT, rhs=rhs, start=start, stop=stop), reads, writes)

        def tr(out, in_, ident, reads, writes):
            P.op("pe", lambda e: e.transpose(out=out, in_=in_, identity=ident), reads, writes)

        def act(out, in_, func, reads, writes, bias=None, scale=None, accum=None):
            kw = {}
            if bias is not None:
                kw["bias"] = bias
            if scale is not None:
                kw["scale"] = scale
            if accum is not None:
                kw["accum_out"] = accum
            P.op("act", lambda e: e.activation(out=out, in_=in_, func=func, **kw), reads, writes)

        def eng_of(eng):
            return eng

        def tt(eng, out, in0, in1, op, reads, writes):
            P.op(eng, lambda e: e.tensor_tensor(out=out, in0=in0, in1=in1, op=op), reads, writes)

        def ts(eng, out, in0, s1, s2, op0, op1, reads, writes):
            if op1 is None:
                P.op(eng, lambda e: e.tensor_scalar(out=out, in0=in0, scalar1=s1, scalar2=None, op0=op0), reads, writes)
            else:
                P.op(eng, lambda e: e.tensor_scalar(out=out, in0=in0, scalar1=s1, scalar2=s2, op0=op0, op1=op1), reads, writes)

        def stt(eng, out, in0, scalar, in1, op0, op1, reads, writes):
            P.op(eng, lambda e: e.scalar_tensor_tensor(out=out, in0=in0, scalar=scalar, in1=in1, op0=op0, op1=op1), reads, writes)

        def cp(eng, out, in_, reads, writes):
            if eng == "act":
                P.op("act", lambda e: e.copy(out=out, in_=in_), reads, writes)
            else:
                P.op(eng, lambda e: e.tensor_copy(out=out, in_=in_), reads, writes)

        def mset(eng, out, val, writes):
            P.op(eng, lambda e: e.memset(out, val), (), writes)

        identb = sb("identb", [128, 128], BF16)
        identf = sb("identf", [128, 128], F32)
        onesb = sb("onesb", [128, 128], BF16)
        onesf = sb("onesf", [128, 128], F32)
        sel = sb("sel", [128, 64, 128], BF16)
        negbig = sb("negbig", [128, 128], BF16)
        gtm = sb("gtm", [128, 256], BF16)
        rmask = sb("rmask", [24, TB], F32)
        sgc = sb("sgc", [128, 2], F32)
        for t_, d_ in ((identb, "c_identb"), (identf, "c_identf"), (onesb, "c_onesb"), (onesf, "c_onesf"),
                       (negbig, "c_negbig"), (gtm, "c_gt"), (rmask, "c_rmask"), (sgc, "c_sg")):
            P.dma("sp", t_[:], cd[d_], writes=[d_])
        P.dma("sp", sel[:].rearrange("p a b -> p (a b)"), cd["c_sel"], writes=["c_sel"])

        xt = [sb("xt%d" % i, [128, 2, 1024], F32) for i in range(2)]
        xnb = [sb("xnb%d" % i, [128, 1024], BF16) for i in range(2)]
        hnTs = [sb("hnT%d" % i, [128, 8, TB], BF16) for i in range(2)]
        wsl = [sb("wsl%d" % i, [128, 8, 256], BF16) for i in range(NWSL)]
        wglu = sb("wglu", [128, 4, 512], BF16)
        wbr5 = sb("wbr5", [128, 4, 1024], BF16)
        wpp = sb("wpp", [128, 2, 1024], BF16)
        vecT = sb("vecT", [128, 136], F32)
        fnw = sb("fnw", [128, 1024], F32)
        dtb = sb("dtb", [24, 1], F32)
        aneg = sb("aneg", [24, 1], F32)
        dch = sb("dch", [128, 12], F32)
        dsk = sb("dsk", [128, 12, 128], BF16)
        WA = sb("WA", [128, 32, 128], BF16)
        TOE = sb("TOE", [128, 32, 128], BF16)
        WC = sb("WC", [128, 32, 128], BF16)
        COSM = sb("COSM", [128, 32, 32], F32)
        SGNM = sb("SGNM", [128, 32, 32], F32)
        RHO0 = sb("RHO0", [128, 32, 32], F32)
        L8RE = sb("L8RE", [128, 32], F32)
        L8SG = sb("L8SG", [128, 32], F32)
        XCS = sb("XCS", [128, 32], F32)
        XCW = sb("XCW", [128, 32], F32)
        ssq = sb("ssq", [128, 8], F32)
        rst = sb("rst", [128, 8], F32)
        ARENA_W = 15 * 1024 + 512
        arena = sb("arena", [128, ARENA_W], F32)

        class Arena:
            def __init__(self):
                self.off = 0

            def reset(self):
                self.off = 0

            def get(self, shape, dt, name, parts=None):
                n = 1
                for s_ in shape[1:]:
                    n *= s_
                words = (n * (2 if dt == BF16 else 4) + 3) // 4
                assert self.off + words <= ARENA_W, (name, self.off, words)
                g0 = self.off // 256
                g1 = (self.off + words - 1) // 256
                P.alias[name] = [("ar", k) for k in range(g0, g1 + 1)]
                if parts:
                    for pi in range(parts):
                        w0 = self.off + (words * pi) // parts
                        w1 = self.off + (words * (pi + 1)) // parts - 1
                        P.alias[(name, pi)] = [("ar", k) for k in range(w0 // 256, w1 // 256 + 1)]
                a = arena[0:shape[0], self.off:self.off + words]
                if dt != F32:
                    a = a.bitcast(dt)
                self.off += words
                if len(shape) == 3:
                    a = a.rearrange("p (a b) -> p a b", a=shape[1])
                elif len(shape) == 4:
                    a = a.rearrange("p (a b c) -> p a b c", a=shape[1], b=shape[2])
                return a

        AR = Arena()

        vs0 = AR.get([128, 128], F32, "vs0")
        vs1 = AR.get([128, 128], F32, "vs1")
        mset("dve", vs0, 0.0, ["vs0"])
        mset("dve", vs1, 0.0, ["vs1"])
        P.dma("sp", vs0[0:80, :], conv_w_d, reads=["vs0"], writes=["vs0"])
        P.dma("sp", vs0[80:100, :], conv_b_d, reads=["vs0"], writes=["vs0"])
        P.dma("sp", vs0[100:112, :], ssd_nw_d, reads=["vs0"], writes=["vs0"])
        P.dma("sp", vs0[112:120, :], norm_w_d, reads=["vs0"], writes=["vs0"])
        P.dma("sp", vs0[120:128, :], ple_nw_d, reads=["vs0"], writes=["vs0"])
        P.dma("sp", vs1[0:4, :], s5_d_d, reads=["vs1"], writes=["vs1"])
        P.dma("sp", vs1[4:8, :], b_glu_d, reads=["vs1"], writes=["vs1"])
        bk, bkk = nbf()
        tr(bk[:, 0:128], vs0, identf[:], ["vs0", "c_identf"], [bkk])
        tr(bk[:, 128:136], vs1[0:8, :], identf[0:8, 0:8], ["vs1", "c_identf"], [bkk])
        cp("dve", vecT[:], bk[:, 0:136], [bkk], ["vecT"])
        CW = lambda i, k: vecT[:, k * 20 + i:k * 20 + i + 1]
        CB = lambda i: vecT[:, 80 + i:81 + i]
        SNW = lambda i: vecT[:, 100 + i:101 + i]
        NW = lambda kt: vecT[:, 112 + kt:113 + kt]
        PNW = lambda kt: vecT[:, 120 + kt:121 + kt]
        D5 = lambda ft: vecT[:, 128 + ft:129 + ft]
        BGL = lambda ft: vecT[:, 132 + ft:133 + ft]
        with nc.allow_non_contiguous_dma(reason="tiny param loads"):
            P.dma("sp", fnw[:], fnw_d.partition_broadcast(128), writes=["fnw"])
            P.dma("sp", dtb[:], dt_bias_d, writes=["dtb"])
            P.dma("sp", aneg[:], a_log_d, writes=["aneg"])
            sdv = ssd_d_d.rearrange("(i two) -> two i", two=2)
            P.dma("sp", dch[0:64, :], sdv[0].partition_broadcast(64), writes=["dch"])
            P.dma("sp", dch[64:128, :], sdv[1].partition_broadcast(64), reads=["dch"], writes=["dch"])
        act(aneg[:], aneg[:], AF.Exp, ["aneg"], ["aneg"])
        ts("dve", aneg[:], aneg[:], -1.0, None, ALU.mult, None, ["aneg"], ["aneg"])
        for i in range(12):
            ts("dve", dsk[:, i, :], identb[:], dch[:, i:i + 1], None, ALU.mult, None, ["c_identb", "dch"], ["dsk"])

        AR.reset()
        A2 = lambda n, s, d=F32, parts=None: AR.get(s, d, n, parts)
        SKIP_S5 = stop in ('S0',)
        SKIP_W = stop in ('S0', 'S1')
        bmaskt = sb("bmaskt", [128, 128], F32)
        P.dma("sp", bmaskt[:], cd["c_bmask"], writes=["bmask"])
        y5T = sb("y5T", [128, 4, TB], BF16)
        yssT = sb("yssT", [128, 12, TB], BF16)
        prevS = sb("prevS", [128, 1536], F32)
        hist = sb("hist", [128, 20, 4], BF16)

        def _s5_setup():
            aTre = A2("aTre", [128, 32]); aTim = A2("aTim", [128, 32]); stp = A2("stp", [128, 32])
            ars = A2("ars", [128, 32]); ais = A2("ais", [128, 32]); r8 = A2("r8", [128, 32])
            anat = A2("anat", [32, 2, 128])
            for h in range(2):
                P.dma("sp", anat[:, 0, 64 * h:64 * h + 64], a_re_d, reads=["anat"] if h else [], writes=["anat"])
                P.dma("sp", anat[:, 1, 64 * h:64 * h + 64], a_im_d, reads=["anat"], writes=["anat"])
            bka, bkak = nbf()
            tr(bka[:, 0:32], anat[:, 0, :], identf[0:32, 0:32], ["anat", "c_identf"], [bkak])
            tr(bka[:, 32:64], anat[:, 1, :], identf[0:32, 0:32], ["anat", "c_identf"], [bkak])
            cp("dve", aTre, bka[:, 0:32], [bkak], ["aTre"])
            cp("dve", aTim, bka[:, 32:64], [bkak], ["aTim"])
            P.dma("sp", stp, lstep_d.partition_broadcast(128), writes=["stp"])
            act(stp, stp, AF.Exp, ["stp"], ["stp"])
            tt("dve", ars, aTre, stp, ALU.mult, ["aTre", "stp"], ["ars"])
            tt("dve", ais, aTim, stp, ALU.mult, ["aTim", "stp"], ["ais"])

            def range_reduce(eng, xin, shape, key, tmpname):
                y = AR.get(shape, F32, tmpname + "y")
                yi = AR.get(shape, I32, tmpname + "i")
                ky, ki_ = tmpname + "y", tmpname + "i"
                ts(eng, y, xin, 1.0 / TWO_PI, None, ALU.mult, None, [key], [ky])
                cp(eng, yi, y, [ky], [ki_])
                cp(eng, y, yi, [ki_], [ky])
                stt(eng, xin, y, -TWO_PI, xin, ALU.mult, ALU.add, [ky, key], [key])
                ts(eng, y, xin, PI, None, ALU.is_gt, None, [key], [ky])
                stt(eng, xin, y, -TWO_PI, xin, ALU.mult, ALU.add, [ky, key], [key])
                ts(eng, y, xin, -PI, None, ALU.is_lt, None, [key], [ky])
                stt(eng, xin, y, TWO_PI, xin, ALU.mult, ALU.add, [ky, key], [key])
                ts(eng, xin, xin, -PI, PI, ALU.max, ALU.min, [key], [key])

            mark0 = AR.off
            cvt = A2("cvt", [128, 32, 32]); cpos = A2("cpos", [128, 32, 32])
            AM = A2("AM", [128, 32, 32]); AM2 = A2("AM2", [128, 32, 32])
            P.dma("sp", cvt.rearrange("p a b -> p (a b)"), cd["c_cv"], writes=["cvt"])
            P.dma("sp", cpos.rearrange("p a b -> p (a b)"), cd["c_cpos"], writes=["cpos"])
            bcc = lambda a: a.unsqueeze(2).to_broadcast([128, 32, 32])
            tt("dve", AM, cvt, bcc(ais), ALU.mult, ["cvt", "ais"], ["AM"])
            ts("dve", AM2, AM, PI / 2, None, ALU.add, None, ["AM"], ["AM2"])
            mk = AR.off
            range_reduce("dve", AM, [128, 32, 32], "AM", "rra")
            AR.off = mk
            range_reduce("dve", AM2, [128, 32, 32], "AM2", "rra")
            act(AM, AM, AF.Sin, ["AM"], ["AM"])
            act(COSM[:], AM2, AF.Sin, ["AM2"], ["COSM"])
            ts("dve", SGNM[:], AM, sgc[:, 1:2], None, ALU.mult, None, ["AM", "c_sg"], ["SGNM"])
            act(r8, ars, AF.Exp, ["ars"], ["r8"], scale=8.0)
            tt("dve", RHO0[:], cpos, bcc(r8), ALU.mult, ["cpos", "r8"], ["RHO0"])
            AR.off = mark0
            kvt = A2("kvt", [128, 16, 32])
            P.dma("sp", kvt.rearrange("p a b -> p (a b)"), cd["c_kv"], writes=["kvt"])
            MAG = A2("MAG", [128, 16, 32]); ANG = A2("ANG", [128, 16, 32]); ANG2 = A2("ANG2", [128, 16, 32])
            PRE = A2("PRE", [128, 16, 32]); PIM = A2("PIM", [128, 16, 32]); PIMS = A2("PIMS", [128, 16, 32])
            bc16 = lambda a: a.unsqueeze(1).to_broadcast([128, 16, 32])
            tt("dve", MAG, kvt, bc16(ars), ALU.mult, ["kvt", "ars"], ["MAG"])
            act(MAG, MAG, AF.Exp, ["MAG"], ["MAG"])
            tt("dve", ANG, kvt, bc16(ais), ALU.mult, ["kvt", "ais"], ["ANG"])
            ts("dve", ANG2, ANG, PI / 2, None, ALU.add, None, ["ANG"], ["ANG2"])
            mk = AR.off
            range_reduce("dve", ANG, [128, 16, 32], "ANG", "rrb")
            AR.off = mk
            range_reduce("dve", ANG2, [128, 16, 32], "ANG2", "rrb")
            AR.off = mk
            act(ANG, ANG, AF.Sin, ["ANG"], ["ANG"])
            act(ANG2, ANG2, AF.Sin, ["ANG2"], ["ANG2"])
            tt("dve", PRE, MAG, ANG2, ALU.mult, ["MAG", "ANG2"], ["PRE"])
            tt("dve", PIM, MAG, ANG, ALU.mult, ["MAG", "ANG"], ["PIM"])
            ts("dve", PIMS, PIM, sgc[:, 0:1], None, ALU.mult, None, ["PIM", "c_sg"], ["PIMS"])
            cp("dve", L8RE[:], PRE[:, 15, :], ["PRE"], ["L8RE"])
            cp("dve", L8SG[:], PIMS[:, 15, :], ["PIMS"], ["L8SG"])
            nre = A2("nre", [128, 32]); den = A2("den", [128, 32]); t0 = A2("t0", [128, 32]); t1 = A2("t1", [128, 32])
            fre = A2("fre", [128, 32]); fim = A2("fim", [128, 32])
            ts("dve", nre, PRE[:, 8, :], -1.0, None, ALU.add, None, ["PRE"], ["nre"])
            tt("dve", den, aTre, aTre, ALU.mult, ["aTre"], ["den"])
            tt("dve", t0, aTim, aTim, ALU.mult, ["aTim"], ["t0"])
            tt("dve", den, den, t0, ALU.add, ["den", "t0"], ["den"])
            P.op("dve", lambda e: e.reciprocal(out=den, in_=den), P._expand(["den"]), P._expand(["den"]))
            tt("dve", t0, nre, aTre, ALU.mult, ["nre", "aTre"], ["t0"])
            tt("dve", t1, PIM[:, 8, :], aTim, ALU.mult, ["PIM", "aTim"], ["t1"])
            tt("dve", t0, t0, t1, ALU.add, ["t0", "t1"], ["t0"])
            tt("dve", fre, t0, den, ALU.mult, ["t0", "den"], ["fre"])
            tt("dve", t0, PIM[:, 8, :], aTre, ALU.mult, ["PIM", "aTre"], ["t0"])
            tt("dve", t1, nre, aTim, ALU.mult, ["nre", "aTim"], ["t1"])
            tt("dve", t0, t0, t1, ALU.subtract, ["t0", "t1"], ["t0"])
            tt("dve", fim, t0, den, ALU.mult, ["t0", "den"], ["fim"])
            bre = A2("bre", [128, 32, 16]); bim = A2("bim", [128, 32, 16])
            BBa = A2("BBa", [128, 32, 16]); BBb = A2("BBb", [128, 32, 16])
            u0 = A2("u0", [128, 32, 16]); u1 = A2("u1", [128, 32, 16])
            with nc.allow_non_contiguous_dma(reason="param loads"):
                for h in range(2):
                    for g4 in range(4):
                        gsl = slice(g4 * 8, g4 * 8 + 8)
                        P.dma("sp", bre[64 * h:64 * h + 64, gsl, :], b_re_d[gsl].rearrange("g p h -> p g h"), reads=["bre"], writes=["bre"])
                        P.dma("sp", bim[64 * h:64 * h + 64, gsl, :], b_im_d[gsl].rearrange("g p h -> p g h"), reads=["bim"], writes=["bim"])
            bch = lambda a: a.unsqueeze(2).to_broadcast([128, 32, 16])
            tt("dve", u0, bre, bch(fre), ALU.mult, ["bre", "fre"], ["u0"])
            tt("dve", u1, bim, bch(fim), ALU.mult, ["bim", "fim"], ["u1"])
            tt("dve", u0, u0, u1, ALU.subtract, ["u0", "u1"], ["u0"])
            tt("dve", u1, bim, bch(fre), ALU.mult, ["bim", "fre"], ["u1"])
            tt("dve", BBb, bre, bch(fim), ALU.mult, ["bre", "fim"], ["BBb"])
            tt("dve", u1, u1, BBb, ALU.add, ["u1", "BBb"], ["u1"])
            cp("dve", BBa[0:64], u0[0:64], ["u0"], ["BBa"])
            cp("dve", BBa[64:128], u1[64:128], ["u1", "BBa"], ["BBa"])
            cp("dve", BBb[0:64], u1[0:64], ["u1", "BBb"], ["BBb"])
            cp("dve", BBb[64:128], u0[64:128], ["u0", "BBb"], ["BBb"])
            cnr = A2("cnr", [128, 4, 128]); cni = A2("cni", [128, 4, 128])
            Ca = A2("Ca", [128, 32, 16]); Cb = A2("Cb", [128, 32, 16])
            with nc.allow_non_contiguous_dma(reason="param loads"):
                for h in range(2):
                    P.dma("sp", cnr[:, :, 64 * h:64 * h + 64], c_re_d.rearrange("(t q) p -> q t p", q=128), reads=["cnr"] if h else [], writes=["cnr"])
                    P.dma("sp", cni[:, :, 64 * h:64 * h + 64], c_im_d.rearrange("(t q) p -> q t p", q=128), reads=["cni"] if h else [], writes=["cni"])
            bkr, bkrk = nbf()
            bki, bkik = nbf()
            for t in range(4):
                tr(bkr[:, t * 128:(t + 1) * 128], cnr[:, t, :], identf[:], ["cnr", "c_identf"], [bkrk])
                tr(bki[:, t * 128:(t + 1) * 128], cni[:, t, :], identf[:], ["cni", "c_identf"], [bkik])
            v3 = lambda a: a.rearrange("p (g h) -> p g h", h=16)
            cp("dve", Ca[0:64], v3(bkr[0:64, :]), [bkrk], ["Ca"])
            ts("dve", Ca[64:128], v3(bki[64:128, :]), -1.0, None, ALU.mult, None, [bkik, "Ca"], ["Ca"])
            cp("dve", Cb[0:64], v3(bki[0:64, :]), [bkik], ["Cb"])
            cp("dve", Cb[64:128], v3(bkr[64:128, :]), [bkrk, "Cb"], ["Cb"])
            T1 = xt[0][:].rearrange("p a b -> p (a b)").rearrange("p (g j h) -> p g j h", g=16, j=8)
            T2 = xt[1][:].rearrange("p a b -> p (a b)").rearrange("p (g j h) -> p g j h", g=16, j=8)
            bch16 = lambda a: a.unsqueeze(2).to_broadcast([128, 16, 16])

            def table(dst, dk, koff, X0, X1, IM, opx, gh):
                gs = slice(gh * 16, gh * 16 + 16)
                for j in range(8):
                    kA = koff(j) + 7
                    e1 = "dve" if (j % 2 == 0) else "pool"
                    a0 = u0[:, 0:16, :] if e1 == "dve" else u0[:, 16:32, :]
                    a1 = u1[:, 0:16, :] if e1 == "dve" else u1[:, 16:32, :]
                    k0 = "u0" + e1
                    k1 = "u1" + e1
                    tt(e1, a0, X0[:, gs, :], bch16(PRE[:, kA, gs]), ALU.mult, ["BBa", "Ca", "PRE", "u0"], ["u0", k0])
                    tt(e1, a1, X1[:, gs, :], bch16(IM[:, kA, gs]), ALU.mult, ["BBb", "Cb", "PIM", "PIMS", "u1"], ["u1", k1])
                    tt(e1, dst[:, :, j, :], a0, a1, opx, ["u0", "u1", k0, k1], [dk])

            for gh in range(2):
                table(T1, "xt0", lambda j: j + 1, Ca, Cb, PIM, ALU.subtract, gh)
                cp("dve", WC[:, gh * 16:gh * 16 + 16, :], T1.rearrange("p g j h -> p g (j h)"), ["xt0"], ["WC"])
                table(T2, "xt1", lambda j: 7 - j, BBa, BBb, PIMS, ALU.add, gh)
                for q4 in range(4):
                    bA, bAk = nbf()
                    for gl in range(4):
                        tr(bA[:, gl * 128:(gl + 1) * 128], T2[:, q4 * 4 + gl].rearrange("p j h -> p (j h)"), identf[:], ["xt1", "c_identf"], [bAk])
                    g0 = gh * 16 + q4 * 4
                    cp("dve", WA[:, g0:g0 + 4, :], bA[:].rearrange("p (g m) -> p g m", g=4), [bAk], ["WA"])
                table(T1, "xt0", lambda j: -j, BBa, BBb, PIMS, ALU.add, gh)
                table(T2, "xt1", lambda j: j, Ca, Cb, PIM, ALU.subtract, gh)
                for q4 in range(4):
                    bT, bTk = nbf()
                    for gl in range(4):
                        mm(bT[:, gl * 128:(gl + 1) * 128], T1[:, q4 * 4 + gl].rearrange("p j h -> p (j h)"),
                           T2[:, q4 * 4 + gl].rearrange("p j h -> p (j h)"), True, True, ["xt0", "xt1"], [bTk])
                    g0 = gh * 16 + q4 * 4
                    tt("dve", TOE[:, g0:g0 + 4, :], bT[:].rearrange("p (g m) -> p g m", g=4),
                       bmaskt[:].unsqueeze(1).to_broadcast([128, 4, 128]), ALU.mult, [bTk, "bmask"], ["TOE"])

        if not SKIP_S5:
            _s5_setup()
        NSTG = 4
        PW = 1024
        pend = []
        stg = [A2("stg%d" % i, [128, PW]) for i in range(NSTG)]
        stb = [A2("stb%d" % i, [128, PW], BF16) for i in range(NSTG)]

        def conv_piece(wd, kt, c0, cw, dst_scr=None, dst_sb=None, name=""):
            i = st_["cast"]
            st_["cast"] += 1
            s_ = i % NSTG
            P.dma("pool", stg[s_][:, 0:cw], wd[kt * 128:(kt + 1) * 128, c0:c0 + cw], writes=["stg%d" % s_])
            if dst_sb is not None:
                while pend:
                    d_, s2_, rk_, wk_ = pend.pop(0)
                    P.dma("pool", d_, s2_, reads=[rk_], writes=[wk_])
                cp("act", dst_sb[:, kt, c0:c0 + cw], stg[s_][:, 0:cw], ["stg%d" % s_], [name])
            else:
                cp("act", stb[s_][:, 0:cw], stg[s_][:, 0:cw], ["stg%d" % s_], ["stb%d" % s_])
                key = ("scr", name, kt, c0 // PW)
                pend.append((dst_scr[kt * 128:(kt + 1) * 128, c0:c0 + cw], stb[s_][:, 0:cw], "stb%d" % s_, key))
            while len(pend) > (NSTG - 2 if dst_sb is None else NSTG - 2):
                d_, s2_, rk_, wk_ = pend.pop(0)
                P.dma("pool", d_, s2_, reads=[rk_], writes=[wk_])

        def convert(wd, K, C, dst_scr=None, dst_sb=None, name=""):
            for c0 in range(0, C, PW):
                cw = min(PW, C - c0)
                for kt in range(K // 128):
                    conv_piece(wd, kt, c0, cw, dst_scr, dst_sb, name)

        if not SKIP_W:
            convert(w_glu_d, 512, 512, dst_sb=wglu, name="wglu")
            convert(w_br5_d, 512, 1024, dst_sb=wbr5, name="wbr5")
            convert(w_pp_d, 256, 1024, dst_sb=wpp, name="wpp")

        src32 = {"w_in": w_in_d, "w_brs": w_brs_d, "w_out": w_out_d, "w_pg": w_pg_d}
        cur = {"blk": 0}

        def wload(name, kt0, nkt, c0, ncol):
            i = st_["w"]
            st_["w"] = (i + 1) % NWSL
            key = ("scrw", name, kt0, c0)
            dsc = scr[name][kt0 * 128:(kt0 + nkt) * 128, c0:c0 + ncol].rearrange("(kt p) c -> p kt c", p=128)
            if cur["blk"] == 0:
                for h in range((ncol + 127) // 128):
                    cw = min(128, ncol - h * 128)
                    hs = st_["half"] % 2
                    st_["half"] += 1
                    stg_ = xt[1][:, hs, :].rearrange("p (k c) -> p k c", k=8)[:, 0:nkt, 0:cw]
                    s32 = src32[name][kt0 * 128:(kt0 + nkt) * 128, c0 + h * 128:c0 + h * 128 + cw].rearrange("(kt p) c -> p kt c", p=128)
                    P.dma("sp", stg_, s32, writes=[("xt1", hs)])
                    cp("pool", wsl[i][:, 0:nkt, h * 128:h * 128 + cw], stg_, [("xt1", hs)], ["wsl%d" % i])
                P.dma("pool", dsc, wsl[i][:, 0:nkt, 0:ncol], reads=["wsl%d" % i], writes=[key])
            else:
                P.dma("sp", wsl[i][:, 0:nkt, 0:ncol], dsc, reads=[key], writes=["wsl%d" % i])
            return wsl[i], "wsl%d" % i

        out_stores = []
        dbg_t = {}

        def dump(name, ap, shape, dt, keys, blk):
            if not dbg:
                return
            if name not in dbg_t:
                dbg_t[name] = nc.dram_tensor("dbg_" + name, [NB] + list(shape), dt, kind="ExternalOutput").ap()
            out_stores.append(P.dma("pool", dbg_t[name][blk], ap, reads=keys))

        def early(Xsrc, XKsrc, t0s=0):
            out_stores.append(P.dma("pool", out_d[t0s:t0s + TB, :].rearrange("(t p) d -> p t d", p=128), Xsrc[:], reads=[XKsrc]))
        for blk in range(NB):
            cur["blk"] = blk
            first = (blk % NBLK) == 0
            t0_ = blk * TB
            xs_ = blk % 2
            X = xt[xs_]
            XK = "xt%d" % xs_
            AR.reset()
            H = hnTs[blk % 2]
            hp = blk % 2

            def norm_T(Xs, XKs, wcol, col0, Hd, hpd):
                for t in range(2):
                    act(xnb[t][:], Xs[:, t, :], AF.Square, [XKs], ["xnb%d" % t, "ssq"], accum=ssq[:, col0 + t:col0 + t + 1])
                ts("dve", rst[:, col0:col0 + 2], ssq[:, col0:col0 + 2], 1.0 / 1024, 1e-6, ALU.mult, ALU.add, ["xnb0", "xnb1", "ssq"], ["rst"])
                act(rst[:, col0:col0 + 2], rst[:, col0:col0 + 2], AF.Sqrt, ["rst"], ["rst"])
                P.op("dve", lambda e: e.reciprocal(out=rst[:, col0:col0 + 2], in_=rst[:, col0:col0 + 2]), P._expand(["rst"]), P._expand(["rst"]))
                for t in range(2):
                    act(xnb[t][:], Xs[:, t, :], AF.Copy, [XKs, "rst"], ["xnb%d" % t], scale=rst[:, col0 + t:col0 + t + 1])
                for half in range(2):
                    bt, btk = nbt()
                    for kl in range(4):
                        for t in range(2):
                            kt = half * 4 + kl
                            tr(bt[:, (kl * 2 + t) * 128:(kl * 2 + t + 1) * 128], xnb[t][:, kt * 128:(kt + 1) * 128], identb[:],
                               ["xnb%d" % t, "c_identb"], [btk])
                    for kl in range(4):
                        kt = half * 4 + kl
                        ts("dve", Hd[:, kt, :], bt[:, kl * 256:(kl + 1) * 256], wcol(kt), None, ALU.mult, None, [btk, "vecT"], [("hnT", hpd, kt)])

            def stage_A(b):
                xs2 = b % 2
                P.dma("sp", xt[xs2][:], x_d[b * TB:(b + 1) * TB, :].rearrange("(t p) d -> p t d", p=128), writes=["xt%d" % xs2])
                norm_T(xt[xs2], "xt%d" % xs2, NW, 0, hnTs[xs2], xs2)

            if stop in ("load", "S0", "S1"):
                P.dma("sp", X[:], x_d[t0_:t0_ + TB, :].rearrange("(t p) d -> p t d", p=128), writes=[XK])
                early(X, XK, t0_)
                continue
            if blk == 0:
                stage_A(0)

            def inproj_pair(c0, width, nt):
                wt, wk = wload("w_in", 0, 8, c0, nt * width)
                bk_, bkk_ = nbf()
                for ti in range(nt):
                    for kt in range(8):
                        mm(bk_[0:width, ti * 256:(ti + 1) * 256], wt[:, kt, ti * width:(ti + 1) * width], H[:, kt, :],
                           kt == 0, kt == 7, [wk, ("hnT", hp, kt)], [bkk_])
                return bk_, bkk_

            dump("hnT", H[:], [128, 8, TB], BF16, [("hnT", hp, kt) for kt in range(8)], blk)
            szT = A2("szT", [128, 12, TB], BF16, 6)
            uT = A2("uT", [128, 4, TB], BF16); sz5 = A2("sz5", [128, 4, TB], BF16)
            for pr in range(2):
                bk_, bkk_ = inproj_pair(pr * 256, 128, 2)
                cp("dve", uT[:, pr * 2:pr * 2 + 2, :], bk_[:].rearrange("p (a b) -> p a b", a=2), [bkk_], ["uT"])
            for pr in range(2):
                bk_, bkk_ = inproj_pair(512 + pr * 256, 128, 2)
                act(sz5[:, pr * 2:pr * 2 + 2, :], bk_[:].rearrange("p (a b) -> p a b", a=2), AF.Silu, [bkk_], ["sz5"])
            Ug = A2("Ug", [128, 32, 32], BF16)
            for ft in range(4):
                bk_, bkk_ = nbf()
                uv = uT[:, ft, :].rearrange("p (c j) -> p j c", j=8)
                for g8 in range(8):
                    for j in range(8):
                        mm(bk_[:, g8 * 32:(g8 + 1) * 32], sel[:, g8 * 8 + j, :], uv[:, j, :], j == 0, j == 7, ["c_sel", "uT"], [bkk_])
                cp("act" if ft % 2 else "dve", Ug[:, ft * 8:(ft + 1) * 8, :], bk_[:, 0:256].rearrange("p (a b) -> p a b", a=8), [bkk_], ["Ug"])
            St = A2("St", [128, 32, 32], F32, 2); Sw = A2("Sw", [128, 32, 32], F32, 2)
            m1 = A2("m1", [128, 16, 32]); m2 = A2("m2", [128, 16, 32])
            if first:
                mset("dve", XCS[:], 0.0, ["XCS"])
                mset("dve", XCW[:], 0.0, ["XCW"])
            for hb in range(2):
                bS, bSk = nbf()
                bW, bWk = nbf()
                for gl in range(16):
                    g = hb * 16 + gl
                    mm(bS[:, gl * 32:(gl + 1) * 32], WA[:, g, :], Ug[:, g, :], True, True, ["WA", "Ug"], [bSk])
                    mm(bW[0:64, gl * 32:(gl + 1) * 32], WA[:, g, 64:128], Ug[:, g, :], True, True, ["WA", "Ug"], [bWk])
                    mm(bW[64:128, gl * 32:(gl + 1) * 32], WA[:, g, 0:64], Ug[:, g, :], True, True, ["WA", "Ug"], [bWk])
                gs = slice(hb * 16, hb * 16 + 16)
                S3 = bS[:].rearrange("p (a b) -> p a b", a=16)
                W3 = bW[:].rearrange("p (a b) -> p a b", a=16)
                tt("dve", m1, S3, COSM[:, gs, :], ALU.mult, [bSk, "COSM"], ["m1"])
                tt("dve", m2, W3, SGNM[:, gs, :], ALU.mult, [bWk, "SGNM"], ["m2"])
                tt("pool", St[:, gs, :], m1, m2, ALU.add, ["m1", "m2"], [("St", hb)])
                tt("dve", m1, W3, COSM[:, gs, :], ALU.mult, [bWk, "COSM"], ["m1"])
                tt("dve", m2, S3, SGNM[:, gs, :], ALU.mult, [bSk, "SGNM"], ["m2"])
                tt("pool", Sw[:, gs, :], m1, m2, ALU.subtract, ["m1", "m2"], [("Sw", hb)])
            for pr in range(6):
                bk_, bkk_ = inproj_pair(1024 + pr * 256, 128, 2)
                act(szT[:, pr * 2:pr * 2 + 2, :], bk_[:].rearrange("p (a b) -> p a b", a=2), AF.Silu, [bkk_], [("szT", pr)])
            c1 = A2("c1", [128, 32]); c2 = A2("c2", [128, 32])
            tt("dve", c1, L8RE[:], XCS[:], ALU.mult, ["L8RE", "XCS"], ["c1"])
            tt("dve", c2, L8SG[:], XCW[:], ALU.mult, ["L8SG", "XCW"], ["c2"])
            tt("dve", c1, c1, c2, ALU.add, ["c1", "c2"], ["c1"])
            tt("dve", St[:, :, 0], St[:, :, 0], c1, ALU.add, [("St", 0), ("St", 1), "c1"], [("St", 0), ("St", 1)])
            tt("dve", c1, L8RE[:], XCW[:], ALU.mult, ["L8RE", "XCW"], ["c1"])
            tt("dve", c2, L8SG[:], XCS[:], ALU.mult, ["L8SG", "XCS"], ["c2"])
            tt("dve", c1, c1, c2, ALU.subtract, ["c1", "c2"], ["c1"])
            tt("dve", Sw[:, :, 0], Sw[:, :, 0], c1, ALU.add, [("Sw", 0), ("Sw", 1), "c1"], [("Sw", 0), ("Sw", 1)])
            Vt = A2("Vt", [128, 32, 32]); Vw = A2("Vw", [128, 32, 32])
            fl = lambda a: a.rearrange("p a b -> p (a b)")
            P.op("dve", lambda e: e.tensor_tensor_scan(out=fl(Vt), data0=fl(RHO0[:]), data1=fl(St), initial=0.0, op0=ALU.mult, op1=ALU.add),
                 P._expand(["RHO0", ("St", 0), ("St", 1)]), P._expand(["Vt"]))
            P.op("dve", lambda e: e.tensor_tensor_scan(out=fl(Vw), data0=fl(RHO0[:]), data1=fl(Sw), initial=0.0, op0=ALU.mult, op1=ALU.add),
                 P._expand(["RHO0", ("Sw", 0), ("Sw", 1)]), P._expand(["Vw"]))
            Xp = A2("Xp", [128, 32, 32], BF16)
            tt("dve", St, Vt, COSM[:], ALU.mult, ["Vt", "COSM", ("St", 0), ("St", 1)], [("St", 0), ("St", 1)])
            tt("pool", Sw, Vw, SGNM[:], ALU.mult, ["Vw", "SGNM", ("Sw", 0), ("Sw", 1)], [("Sw", 0), ("Sw", 1)])
            tt("dve", St, St, Sw, ALU.subtract, [("St", 0), ("St", 1), ("Sw", 0), ("Sw", 1)], [("St", 0), ("St", 1)])
            cp("act", Xp[:, :, 1:32], St[:, :, 0:31], [("St", 0), ("St", 1)], ["Xp"])
            cp("act", Xp[:, :, 0], XCS[:], ["XCS", "Xp"], ["Xp"])
            cp("dve", XCS[:], St[:, :, 31], [("St", 0), ("St", 1), "XCS"], ["XCS"])
            tt("dve", c1, Vw[:, :, 31], COSM[:, :, 31], ALU.mult, ["Vw", "COSM"], ["c1"])
            tt("dve", c2, Vt[:, :, 31], SGNM[:, :, 31], ALU.mult, ["Vt", "SGNM"], ["c2"])
            tt("dve", XCW[:], c1, c2, ALU.add, ["c1", "c2", "XCW"], ["XCW"])
            Yg = A2("Yg", [128, 32, 32], BF16)
            for hb in range(2):
                bY, bYk = nbf()
                for gl in range(16):
                    g = hb * 16 + gl
                    mm(bY[:, gl * 32:(gl + 1) * 32], TOE[:, g, :], Ug[:, g, :], True, False, ["TOE", "Ug"], [bYk])
                    mm(bY[:, gl * 32:(gl + 1) * 32], WC[:, g, :], Xp[:, g, :], False, True, ["WC", "Xp"], [bYk])
                cp("act", Yg[:, hb * 16:hb * 16 + 16, :], bY[:].rearrange("p (a b) -> p a b", a=16), [bYk], ["Yg"])
            yT = A2("yT", [128, 4, TB], BF16, 4)
            ytmp = A2("ytmp", [128, TB])
            for ft in range(4):
                bk_, bkk_ = nbf()
                for j in range(8):
                    for g8 in range(8):
                        mm(bk_[:, j * 32:(j + 1) * 32], sel[:, j * 8 + g8, :], Yg[:, ft * 8 + g8, :], g8 == 0, g8 == 7, ["c_sel", "Yg"], [bkk_])
                stt("dve", ytmp.rearrange("p (c j) -> p j c", j=8), uT[:, ft, :].rearrange("p (c j) -> p j c", j=8), D5(ft),
                    bk_[:, 0:256].rearrange("p (j c) -> p j c", j=8), ALU.mult, ALU.add, ["uT", "vecT", bkk_], ["ytmp"])
                act(yT[:, ft, :], ytmp, AF.Gelu, ["ytmp"], [("yT", ft)])
            dump("uT", uT, [128, 4, TB], BF16, ["uT"], blk)
            dump("Ug", Ug, [128, 32, 32], BF16, ["Ug"], blk)
            dump("Xp", Xp, [128, 32, 32], BF16, ["Xp"], blk)
            dump("Yg", Yg, [128, 32, 32], BF16, ["Yg"], blk)
            dump("yT", yT, [128, 4, TB], BF16, [("yT", f_) for f_ in range(4)], blk)
            sgl = A2("sgl", [128, TB], BF16)
            for ft in range(4):
                bk_, bkk_ = nbf()
                for kt in range(4):
                    mm(bk_[:, 0:256], wglu[:, kt, ft * 128:(ft + 1) * 128], yT[:, kt, :], kt == 0, kt == 3, ["wglu", ("yT", kt)], [bkk_])
                act(sgl, bk_[:, 0:256], AF.Sigmoid, [bkk_, "vecT"], ["sgl"], bias=BGL(ft))
                tt("dve", sgl, sgl, yT[:, ft, :], ALU.mult, ["sgl", ("yT", ft)], ["sgl"])
                tt("pool", y5T[:, ft, :], sgl, sz5[:, ft, :], ALU.mult, ["sgl", "sz5"], [("y5T", ft)])

            dump("y5T", y5T[:], [128, 4, TB], BF16, [("y5T", f_) for f_ in range(4)], blk)
            if stop == "C":
                early(X, XK, t0_)
                continue
            AR.reset()
            szT = A2("szT", [128, 12, TB], BF16, 6)
            cvT = A2("cvT", [128, 20, TB], BF16, 10)
            xraws = [A2("xraw%d" % q, [128, 2, 260], BF16) for q in range(2)]
            accs = [A2("acc%d" % q, [128, 2, TB], F32, 2) for q in range(2)]
            if first:
                mset("pool", hist[:], 0.0, ["hist"])
                mset("pool", prevS[:], 0.0, ["prevS"])
            for pr in range(10):
                q = pr % 2
                xraw = xraws[q]; acc = accs[q]; xk = "xraw%d" % q; ak = "acc%d" % q
                bk_, bkk_ = inproj_pair(2560 + pr * 256, 128, 2)
                cp("pool", xraw[:, :, 0:3], hist[:, pr * 2:pr * 2 + 2, 0:3], ["hist"], [xk])
                cp("act", xraw[:, :, 3:259], bk_[:].rearrange("p (a b) -> p a b", a=2), [bkk_, xk], [xk])
                cp("pool", hist[:, pr * 2:pr * 2 + 2, 0:3], xraw[:, :, 256:259], [xk, "hist"], ["hist"])
                for ti in range(2):
                    i = pr * 2 + ti
                    act(acc[:, ti, :], xraw[:, ti, 0:256], AF.Identity, [xk, "vecT"], [(ak, ti)], bias=CB(i), scale=CW(i, 0))
                for k in (1, 2, 3):
                    for ti in range(2):
                        i = pr * 2 + ti
                        stt("dve", acc[:, ti, :], xraw[:, ti, k:k + 256], CW(i, k), acc[:, ti, :], ALU.mult, ALU.add, [xk, "vecT", (ak, ti)], [(ak, ti)])
                if pr > 0:
                    qp = (pr - 1) % 2
                    act(cvT[:, (pr - 1) * 2:(pr - 1) * 2 + 2, :], accs[qp], AF.Silu, [("acc%d" % qp, 0), ("acc%d" % qp, 1)], [("cvT", pr - 1)])
            act(cvT[:, 18:20, :], accs[1], AF.Silu, [("acc1", 0), ("acc1", 1)], [("cvT", 9)])
            CV = lambda i: ("cvT", i // 2)
            dump("cvT", cvT, [128, 20, TB], BF16, [("cvT", f_) for f_ in range(10)], blk)
            wt, wk = wload("w_in", 0, 8, 5120, 24)
            bk_, bkk_ = nbf()
            for kt in range(8):
                mm(bk_[0:24, 0:256], wt[:, kt, 0:24], H[:, kt, :], kt == 0, kt == 7, [wk, ("hnT", hp, kt)], [bkk_])
            dtT = A2("dtT", [24, TB]); daT = A2("daT", [24, TB]); AT = A2("AT", [24, TB])
            AThi = A2("AThi", [24, TB], BF16); ATlo = A2("ATlo", [24, TB], BF16)
            act(dtT, bk_[0:24, 0:256], AF.Exp, [bkk_, "dtb"], ["dtT"], bias=dtb[:, 0:1])
            act(dtT, dtT, AF.Ln, ["dtT"], ["dtT"], bias=1.0)
            ts("dve", daT, dtT, aneg[:, 0:1], None, ALU.mult, None, ["dtT", "aneg"], ["daT"])
            P.op("dve", lambda e: e.tensor_tensor_scan(out=AT, data0=rmask[:], data1=daT, initial=0.0, op0=ALU.mult, op1=ALU.add),
                 P._expand(["c_rmask", "daT"]), P._expand(["AT"]))
            cp("dve", AThi, AT, ["AT"], ["AThi"])
            tt("dve", ATlo, AT, AThi, ALU.subtract, ["AT", "AThi"], ["ATlo"])
            dump("dtT", dtT, [24, TB], F32, ["dtT"], blk)
            dump("AT", AT, [24, TB], F32, ["AT"], blk)
            dtk = A2("dtk", [128, 2, 24]); nAc = A2("nAc", [128, 2, 24])
            dec = A2("dec", [128, 2, 24]); cdec = A2("cdec", [128, 2, 24]); dtd = A2("dtd", [128, 2, 24])
            rhd = A2("rhd", [24, 2, 24])
            bk_, bkk_ = nbf()
            for ck in range(2):
                tr(bk_[:, ck * 24:(ck + 1) * 24], dtT[:, ck * 128:(ck + 1) * 128], identf[0:24, 0:24], ["dtT", "c_identf"], [bkk_])
                tr(bk_[:, 48 + ck * 24:48 + (ck + 1) * 24], AT[:, ck * 128:(ck + 1) * 128], identf[0:24, 0:24], ["AT", "c_identf"], [bkk_])
                ts("dve", rhd[:, ck, :], identf[0:24, 0:24], AT[:, ck * 128 + 127:ck * 128 + 128], None, ALU.mult, None, ["c_identf", "AT"], ["rhd"])
            mm(bk_[:, 96:144], onesf[0:24, :], rhd.rearrange("p a b -> p (a b)"), True, True, ["c_onesf", "rhd"], [bkk_])
            cp("dve", dtk, bk_[:, 0:48].rearrange("p (a b) -> p a b", a=2), [bkk_], ["dtk"])
            ts("dve", nAc, bk_[:, 48:96].rearrange("p (a b) -> p a b", a=2), -1.0, None, ALU.mult, None, [bkk_], ["nAc"])
            act(cdec, bk_[:, 96:144].rearrange("p (a b) -> p a b", a=2), AF.Exp, [bkk_], ["cdec"])
            tt("dve", dec, bk_[:, 96:144].rearrange("p (a b) -> p a b", a=2), nAc, ALU.add, [bkk_, "nAc"], ["dec"])
            act(dec, dec, AF.Exp, ["dec"], ["dec"])
            tt("dve", dtd, dtk, dec, ALU.mult, ["dtk", "dec"], ["dtd"])
            xdt = A2("xdt", [128, 2, 1536], BF16, 2); xdd = A2("xdd", [128, 2, 1536], BF16, 2)
            Btk = A2("Btk", [128, 2, 512], BF16, 2)
            for ck in range(2):
                cs = slice(ck * 128, (ck + 1) * 128)
                for (i0, n) in ((0, 8), (8, 4)):
                    bt, btk = nbt()
                    for ii in range(n):
                        tr(bt[:, ii * 128:(ii + 1) * 128], cvT[:, i0 + ii, cs], identb[:], [CV(i0 + ii), "c_identb"], [btk])
                    h0 = i0 * 2
                    nh = n * 2
                    src3 = bt[:, 0:n * 128].rearrange("p (h d) -> p h d", d=64)
                    tt("dve", xdt[:, ck, h0 * 64:(h0 + nh) * 64].rearrange("p (h d) -> p h d", d=64), src3,
                       dtk[:, ck, h0:h0 + nh].unsqueeze(2).to_broadcast([128, nh, 64]), ALU.mult, [btk, "dtk"], [("xdt", ck)])
                    tt("dve", xdd[:, ck, h0 * 64:(h0 + nh) * 64].rearrange("p (h d) -> p h d", d=64), src3,
                       dtd[:, ck, h0:h0 + nh].unsqueeze(2).to_broadcast([128, nh, 64]), ALU.mult, [btk, "dtd"], [("xdd", ck)])
                bt, btk = nbt()
                for gq in range(4):
                    tr(bt[:, gq * 128:(gq + 1) * 128], cvT[:, 12 + gq, cs], identb[:], [CV(12 + gq), "c_identb"], [btk])
                cp("act", Btk[:, ck, :], bt[:, 0:512], [btk], [("Btk", ck)])
            pvb = A2("pvb", [128, 2, 1536], BF16, 2)
            for ck in range(2):
                cp("act", pvb[:, ck, :], prevS[:], ["prevS"], [("pvb", ck)])
                tt("pool", prevS[:].rearrange("p (h d) -> p h d", d=64), prevS[:].rearrange("p (h d) -> p h d", d=64),
                   cdec[:, ck, :].unsqueeze(2).to_broadcast([128, 24, 64]), ALU.mult, ["prevS", "cdec"], ["prevS"])
                for gq in range(4):
                    bk_, bkk_ = nbf()
                    mm(bk_[:, 0:384], Btk[:, ck, gq * 128:(gq + 1) * 128], xdd[:, ck, gq * 384:(gq + 1) * 384], True, True,
                       [("Btk", ck), ("xdd", ck)], [bkk_])
                    tt("dve", prevS[:, gq * 384:(gq + 1) * 384], prevS[:, gq * 384:(gq + 1) * 384], bk_[:, 0:384], ALU.add, ["prevS", bkk_], ["prevS"])
            NBUF = 4
            ET = [A2("ET%d" % i, [128, TB], BF16, 2) for i in range(NBUF)]
            EA = [A2("EA%d" % i, [128, TB], BF16) for i in range(NBUF)]
            CD = [A2("CD%d" % i, [128, TB], BF16) for i in range(NBUF)]
            WT = [A2("WT%d" % i, [128, TB], BF16) for i in range(NBUF)]
            ygT = A2("ygT", [128, 3, TB], F32, 3); sq = A2("sq", [128, 3, TB], BF16, 3); rsr = A2("rsr", [128, TB])

            def head_decay(gq, hd, bSc, bSck):
                b_ = hd % NBUF
                bA_, bAk_ = nbf()
                mm(bA_[:, 0:256], identb[0:24, hd:hd + 1].to_broadcast([24, 128]), AThi, True, False, ["c_identb", "AThi"], [bAk_])
                mm(bA_[:, 0:256], identb[0:24, hd:hd + 1].to_broadcast([24, 128]), ATlo, False, True, ["c_identb", "ATlo"], [bAk_])
                mm(bA_[:, 256:512], identb[0:24, hd:hd + 1].to_broadcast([24, 128]), AThi, True, False, ["c_identb", "AThi"], [bAk_])
                mm(bA_[:, 256:512], identb[0:24, hd:hd + 1].to_broadcast([24, 128]), ATlo, False, False, ["c_identb", "ATlo"], [bAk_])
                mm(bA_[:, 256:512], negbig[:], gtm[:], False, True, ["c_negbig", "c_gt"], [bAk_])
                act(EA[b_], bA_[:, 0:256], AF.Exp, [bAk_], ["EA%d" % b_])
                for ck in range(2):
                    cs = slice(ck * 128, (ck + 1) * 128)
                    act(ET[b_][:, cs], bA_[:, 256 + ck * 128:256 + (ck + 1) * 128], AF.Exp, [bAk_, "nAc"], [("ET%d" % b_, ck)],
                        bias=nAc[:, ck, hd:hd + 1])
                tt("dve", WT[b_], bSc[:, 0:256], ET[b_], ALU.mult, [bSck, ("ET%d" % b_, 0), ("ET%d" % b_, 1)], ["WT%d" % b_])
                tt("pool", CD[b_], cvT[:, 16 + gq, :], EA[b_], ALU.mult, [CV(16 + gq), "EA%d" % b_], ["CD%d" % b_])

            def head_pair_y(gq, hd):
                i = hd // 2
                bY_, bYk_ = nbf()
                for ck in range(2):
                    cs = slice(ck * 128, (ck + 1) * 128)
                    mm(bY_[:, cs], dsk[:, i, :], cvT[:, i, cs], True, False, ["dsk", CV(i)], [bYk_])
                    for hh in range(2):
                        h2 = hd - 1 + hh
                        b2 = h2 % NBUF
                        mm(bY_[64 * hh:64 * hh + 64, cs], xdt[:, ck, h2 * 64:(h2 + 1) * 64], WT[b2][:, cs], False, False,
                           [("xdt", ck), "WT%d" % b2], [bYk_])
                        mm(bY_[64 * hh:64 * hh + 64, cs], pvb[:, ck, h2 * 64:(h2 + 1) * 64], CD[b2][:, cs], False, hh == 1,
                           [("pvb", ck), "CD%d" % b2], [bYk_])
                il = i % 3
                tt("dve", ygT[:, il, :], bY_[:, 0:256], szT[:, i, :], ALU.mult, [bYk_, ("szT", i // 2)], [("ygT", il)])
                tt("dve", sq[:, il, :], ygT[:, il, :], ygT[:, il, :], ALU.mult, [("ygT", il)], [("sq", il)])

            for gq in range(4):
                bSc, bSck = pscs[gq % 2], "psc%d" % (gq % 2)
                for ck in range(2):
                    cs = slice(ck * 128, (ck + 1) * 128)
                    mm(bSc[:, cs], cvT[:, 12 + gq, cs], cvT[:, 16 + gq, cs], True, True, [CV(12 + gq), CV(16 + gq)], [bSck])
                h0 = gq * 6
                head_decay(gq, h0, bSc, bSck)
                head_decay(gq, h0 + 1, bSc, bSck)
                head_decay(gq, h0 + 2, bSc, bSck)
                head_decay(gq, h0 + 3, bSc, bSck)
                head_pair_y(gq, h0 + 1)
                head_decay(gq, h0 + 4, bSc, bSck)
                head_decay(gq, h0 + 5, bSc, bSck)
                head_pair_y(gq, h0 + 3)
                head_pair_y(gq, h0 + 5)
                bR, bRk = nbf()
                for il in range(3):
                    mm(bR[:, 0:256], onesb[:], sq[:, il, :], il == 0, il == 2, ["c_onesb", ("sq", il)], [bRk])
                ts("dve", rsr, bR[:, 0:256], 1.0 / 384, 1e-6, ALU.mult, ALU.add, [bRk], ["rsr"])
                act(rsr, rsr, AF.Ln, ["rsr"], ["rsr"])
                act(rsr, rsr, AF.Exp, ["rsr"], ["rsr"], scale=-0.5)
                for il in range(3):
                    i = gq * 3 + il
                    stt("dve", yssT[:, i, :], ygT[:, il, :], SNW(i), rsr, ALU.mult, ALU.mult, [("ygT", il), "vecT", "rsr"], [("yssT", i)])

            dump("yssT", yssT[:], [128, 12, TB], BF16, [("yssT", f_) for f_ in range(12)], blk)
            dump("prevS", prevS[:], [128, 1536], F32, ["prevS"], blk)
            if stop == "D":
                early(X, XK, t0_)
                continue
            AR.reset()
            gT = A2("gT", [128, 16, TB], BF16, 8)
            mgT = A2("mgT", [128, 8, TB], BF16, 8)
            for pr in range(8):
                bk_, bkk_ = inproj_pair(5144 + pr * 256, 128, 2)
                act(gT[:, pr * 2:pr * 2 + 2, :], bk_[:].rearrange("p (a b) -> p a b", a=2), AF.Sigmoid, [bkk_], [("gT", pr)])
            if blk + 1 < NB and blk != 0:
                stage_A(blk + 1)
            e1t = A2("e1t", [128, TB]); e2t = A2("e2t", [128, TB])
            for dp in range(4):
                wa, wak = wload("w_brs", 0, 8, dp * 256, 256)
                wb, wbk = wload("w_brs", 8, 4, dp * 256, 256)
                for dl in range(2):
                    d_ = dp * 2 + dl
                    bk_, bkk_ = nbf()
                    for kt in range(4):
                        mm(bk_[:, 0:256], wbr5[:, kt, d_ * 128:(d_ + 1) * 128], y5T[:, kt, :], kt == 0, kt == 3, ["wbr5", ("y5T", kt)], [bkk_])
                    for kt in range(12):
                        w_, wk_ = (wa, wak) if kt < 8 else (wb, wbk)
                        mm(bk_[:, 256:512], w_[:, kt % 8, dl * 128:(dl + 1) * 128], yssT[:, kt, :], kt == 0, kt == 11, [wk_, ("yssT", kt)], [bkk_])
                    tt("dve", e1t, bk_[:, 0:256], gT[:, d_, :], ALU.mult, [bkk_, ("gT", d_ // 2)], ["e1t"])
                    tt("dve", e2t, bk_[:, 256:512], gT[:, 8 + d_, :], ALU.mult, [bkk_, ("gT", 4 + d_ // 2)], ["e2t"])
                    tt("pool", mgT[:, d_, :], e1t, e2t, ALU.add, ["e1t", "e2t"], [("mgT", d_)])
            MG = [("mgT", d_) for d_ in range(8)]
            dump("mgT", mgT, [128, 8, TB], BF16, MG, blk)
            dump("gT", gT, [128, 16, TB], BF16, [("gT", f_) for f_ in range(8)], blk)
            for dc in range(4):
                wt, wk = wload("w_out", 0, 8, dc * 256, 256)
                bk_, bkk_ = nbf()
                for t in range(2):
                    for kt in range(8):
                        mm(bk_[:, t * 256:(t + 1) * 256], mgT[:, kt, t * 128:(t + 1) * 128], wt[:, kt, :], kt == 0, kt == 7, [wk, ("mgT", kt)], [bkk_])
                tt("dve", X[:, :, dc * 256:(dc + 1) * 256], X[:, :, dc * 256:(dc + 1) * 256], bk_[:].rearrange("p (a b) -> p a b", a=2),
                   ALU.add, [XK, bkk_], [XK])
            norm_T(X, XK, PNW, 2, H, hp)
            pf = A2("pf", [128, 2, 256]); pb = A2("pb", [128, 2, 256], BF16); pT = A2("pT", [128, 2, TB], BF16)
            P.dma("pool", pf, p_d[t0_:t0_ + TB, :].rearrange("(t p) d -> p t d", p=128), writes=["pf"])
            cp("act", pb, pf, ["pf"], ["pb"])
            bt, btk = nbt()
            for kt in range(2):
                for t in range(2):
                    tr(bt[:, (kt * 2 + t) * 128:(kt * 2 + t + 1) * 128], pb[:, t, kt * 128:(kt + 1) * 128], identb[:], ["pb", "c_identb"], [btk])
            cp("dve", pT, bt[:, 0:512].rearrange("p (a b) -> p a b", a=2), [btk], ["pT"])
            sgp = A2("sgp", [128, 2, 256]); e3t = A2("e3t", [128, 2, 256])
            for dc in range(4):
                wt, wk = wload("w_pg", 0, 8, dc * 256, 256)
                bG, bGk = nbf()
                bP, bPk = nbf()
                for t in range(2):
                    for kt in range(8):
                        mm(bG[:, t * 256:(t + 1) * 256], H[:, kt, t * 128:(t + 1) * 128], wt[:, kt, :], kt == 0, kt == 7, [wk, ("hnT", hp, kt)], [bGk])
                    for kt in range(2):
                        mm(bP[:, t * 256:(t + 1) * 256], pT[:, kt, t * 128:(t + 1) * 128], wpp[:, kt, dc * 256:(dc + 1) * 256], kt == 0, kt == 1,
                           ["wpp", "pT"], [bPk])
                act(sgp, bG[:].rearrange("p (a b) -> p a b", a=2), AF.Sigmoid, [bGk], ["sgp"])
                tt("dve", e3t, bP[:].rearrange("p (a b) -> p a b", a=2), sgp, ALU.mult, [bPk, "sgp"], ["e3t"])
                tt("pool", X[:, :, dc * 256:(dc + 1) * 256], X[:, :, dc * 256:(dc + 1) * 256], e3t, ALU.add, [XK, "e3t"], [XK])
            ob = A2("ob", [128, 2, 1024], F32, 2)
            for t in range(2):
                act(xnb[t][:], X[:, t, :], AF.Square, [XK], ["xnb%d" % t, "ssq"], accum=ssq[:, 4 + t:5 + t])
            ts("dve", rst[:, 4:6], ssq[:, 4:6], 1.0 / 1024, 1e-6, ALU.mult, ALU.add, ["xnb0", "xnb1", "ssq"], ["rst"])
            act(rst[:, 4:6], rst[:, 4:6], AF.Sqrt, ["rst"], ["rst"])
            P.op("dve", lambda e: e.reciprocal(out=rst[:, 4:6], in_=rst[:, 4:6]), P._expand(["rst"]), P._expand(["rst"]))
            for t in range(2):
                act(ob[:, t, :], X[:, t, :], AF.Copy, [XK, "rst"], [("ob", t)], scale=rst[:, 4 + t:5 + t])
                tt("dve" if t == 0 else "pool", ob[:, t, :], ob[:, t, :], fnw[:], ALU.mult, [("ob", t), "fnw"], [("ob", t)])
            out_stores.append(P.dma("pool", out_d[t0_:t0_ + TB, :].rearrange("(t p) d -> p t d", p=128), ob, reads=[("ob", 0), ("ob", 1)]))
            if blk == 0 and NB > 1:
                stage_A(1)

        P.emit(final_wait_ops=out_stores)
    return nc


_NC_CACHE = {}


def _prep_shared(inp):
    f = lambda a: np.ascontiguousarray(np.asarray(a, dtype=np.float32))
    sh = {
        "norm_w": f(inp["norm_w"]).reshape(8, 128),
        "w_in": f(inp["w_in"]).reshape(1024, 7192),
        "s5_a_re": f(inp["s5_a_re"]).reshape(32, 64),
        "s5_a_im": f(inp["s5_a_im"]).reshape(32, 64),
        "s5_b_re": f(inp["s5_b_re"]).reshape(32, 64, 16),
        "s5_b_im": f(inp["s5_b_im"]).reshape(32, 64, 16),
        "s5_c_re": f(inp["s5_c_re"]).reshape(512, 64),
        "s5_c_im": f(inp["s5_c_im"]).reshape(512, 64),
        "s5_d": f(inp["s5_d"]).reshape(4, 128),
        "s5_log_step": f(inp["s5_log_step"]).reshape(32),
        "s5_w_glu": f(inp["s5_w_glu"]).reshape(512, 512),
        "s5_b_glu": f(inp["s5_b_glu"]).reshape(4, 128),
        "ssd_conv_w": f(inp["ssd_conv_w"]).reshape(80, 128),
        "ssd_conv_b": f(inp["ssd_conv_b"]).reshape(20, 128),
        "ssd_dt_bias": f(inp["ssd_dt_bias"]).reshape(24, 1),
        "ssd_a_log": f(inp["ssd_a_log"]).reshape(24, 1),
        "ssd_d": f(inp["ssd_d"]).reshape(24),
        "ssd_norm_w": f(inp["ssd_norm_w"]).reshape(12, 128),
        "w_br_s5": f(inp["w_br_s5"]).reshape(512, 1024),
        "w_br_ssd": f(inp["w_br_ssd"]).reshape(1536, 1024),
        "w_out": f(inp["w_out"]).reshape(1024, 1024),
        "ple_norm_w": f(inp["ple_norm_w"]).reshape(8, 128),
        "w_ple_gate": f(inp["w_ple_gate"]).reshape(1024, 1024),
        "w_ple_proj": f(inp["w_ple_proj"]).reshape(256, 1024),
        "final_norm_w": f(inp["final_norm_w"]).reshape(1024),
    }
    sh.update(_consts())
    return sh


def run(inputs, n_cores=8):
    x = np.asarray(inputs["x"], dtype=np.float32)
    p = np.asarray(inputs["p"], dtype=np.float32)
    B, L, Dm = x.shape
    assert B % n_cores == 0 and L % TB == 0
    NSEQ = B // n_cores
    NBLK = L // TB
    key = (NSEQ, NBLK)
    import os
    if key not in _NC_CACHE:
        _NC_CACHE[key] = build(NSEQ, NBLK, stop=os.environ.get("KSTOP"), dbg=bool(os.environ.get("KDBG")))
    nc = _NC_CACHE[key]
    sh = _prep_shared(inputs)
    xs = x.reshape(n_cores, NSEQ * L, Dm)
    ps = p.reshape(n_cores, NSEQ * L, 256)
    in_maps = []
    for c in range(n_cores):
        m = dict(sh)
        m["x"] = np.ascontiguousarray(xs[c])
        m["p"] = np.ascontiguousarray(ps[c])
        in_maps.append(m)
    res = run_bass_kernel_spmd(nc, in_maps, core_ids=list(range(n_cores)))
    global LAST_RES
    LAST_RES = res.results
    out = np.stack([np.asarray(r["out"], dtype=np.float32) for r in res.results], axis=0)
    return out.reshape(B, L, Dm)


def kernel(**inputs):
    return run(inputs, 8)
```

```python
import math
from contextlib import ExitStack

import numpy as np
import ml_dtypes
import concourse.bass as bass
import concourse.mybir as mybir
from concourse.bass_utils import run_bass_kernel_spmd

F32 = mybir.dt.float32
BF16 = mybir.dt.bfloat16
I32 = mybir.dt.int32
AF = mybir.ActivationFunctionType
ALU = mybir.AluOpType

TB = 256
NWSL = 4
PI = math.pi
TWO_PI = 2.0 * math.pi


class _Op:
    __slots__ = ("eng", "fn", "reads", "writes", "deps", "sig", "dma_sem", "dma_val", "is_dma", "waits")

    def __init__(self, eng, fn, reads, writes, is_dma):
        self.eng = eng
        self.fn = fn
        self.reads = reads
        self.writes = writes
        self.deps = []
        self.sig = None
        self.is_dma = is_dma
        self.dma_sem = None
        self.dma_val = None
        self.waits = []


class Prog:
    ENGS = ("pe", "act", "dve", "pool", "sp")

    def __init__(self, nc, n_dma_sems=40, same_engine_sync=True):
        self.nc = nc
        self.ops = []
        self.last_writer = {}
        self.readers = {}
        self.n_dma_sems = n_dma_sems
        self.same_engine_sync = same_engine_sync
        self.alias = {}

    def _expand(self, keys):
        out = []
        for k in keys:
            a = self.alias.get(k)
            if a is None:
                out.append(k)
            else:
                out.extend(a)
        return tuple(out)

    def _add(self, op):
        op.reads = self._expand(op.reads)
        op.writes = self._expand(op.writes)
        deps = set()
        for r in op.reads:
            w = self.last_writer.get(r)
            if w is not None:
                deps.add(w)
        for r in op.writes:
            w = self.last_writer.get(r)
            if w is not None and (op.is_dma or self.ops[w].is_dma or self.ops[w].eng != op.eng):
                deps.add(w)
            for rd in self.readers.get(r, {}).values():
                if op.is_dma or self.ops[rd].is_dma or self.ops[rd].eng != op.eng:
                    deps.add(rd)
        idx = len(self.ops)
        op.deps = sorted(deps)
        self.ops.append(op)
        for r in op.writes:
            self.last_writer[r] = idx
            self.readers[r] = {}
        for r in op.reads:
            if r not in op.writes:
                k = ("dma", idx) if op.is_dma else op.eng
                self.readers.setdefault(r, {})[k] = idx
        return idx

    def op(self, eng, fn, reads=(), writes=()):
        return self._add(_Op(eng, fn, tuple(reads), tuple(writes), False))

    def dma(self, queue, out, in_, reads=(), writes=(), **kw):
        def fn(e, out=out, in_=in_, kw=kw):
            return e.dma_start(out=out, in_=in_, allow_slow_non_contiguous=True, **kw)
        return self._add(_Op(queue, fn, tuple(reads), tuple(writes), True))

    def emit(self, final_wait_ops=()):
        nc = self.nc
        ops = self.ops
        needed = [False] * len(ops)
        for o in ops:
            for d in o.deps:
                needed[d] = True
        for i in final_wait_ops:
            needed[i] = True
        sig_cnt = {e: 0 for e in self.ENGS}
        dma_cnt = [0] * self.n_dma_sems
        half = self.n_dma_sems // 2
        pools = {"sp": list(range(0, half)), "pool": list(range(half, self.n_dma_sems))}
        dma_rr = {"sp": 0, "pool": 0}
        for i, o in enumerate(ops):
            if o.is_dma:
                pl = pools[o.eng]
                o.dma_sem = pl[dma_rr[o.eng] % len(pl)]
                dma_rr[o.eng] += 1
                dma_cnt[o.dma_sem] += 1
                o.dma_val = 16 * dma_cnt[o.dma_sem]
            elif needed[i]:
                sig_cnt[o.eng] += 1
                o.sig = sig_cnt[o.eng]
        self.sig_cnt = sig_cnt
        waited = {e: {} for e in self.ENGS}
        for i, o in enumerate(ops):
            need = {}
            if o.is_dma and o.dma_val > 16:
                need[("dma", o.dma_sem)] = o.dma_val - 16
            for d in o.deps:
                p = ops[d]
                if p.is_dma:
                    k = ("dma", p.dma_sem)
                    v = p.dma_val
                else:
                    if p.eng == o.eng and (p.eng == "pe" or not self.same_engine_sync):
                        continue
                    k = ("eng", p.eng)
                    v = p.sig
                if need.get(k, 0) < v:
                    need[k] = v
            w = waited[o.eng]
            for k, v in need.items():
                if w.get(k, 0) < v:
                    w[k] = v
                    o.waits.append((k, v))
        final = []
        for i in final_wait_ops:
            p = ops[i]
            if p.is_dma:
                final.append((("dma", p.dma_sem), p.dma_val))
            else:
                final.append((("eng", p.eng), p.sig))
        with ExitStack() as st:
            sems = {}
            for e in self.ENGS:
                sems[("eng", e)] = st.enter_context(nc.semaphore("s_" + e))
            for j in range(self.n_dma_sems):
                sems[("dma", j)] = st.enter_context(nc.semaphore("s_dma%d" % j))
            block = st.enter_context(nc.Block())
            per = {e: [o for o in ops if o.eng == e] for e in self.ENGS}

            def run(e, engobj):
                for o in per[e]:
                    for k, v in o.waits:
                        engobj.wait_ge(sems[k], v)
                    ins = o.fn(engobj)
                    if o.is_dma:
                        ins.then_inc(sems[("dma", o.dma_sem)], 16)
                    elif o.sig is not None:
                        ins.then_inc(sems[("eng", e)], 1)
                if e == "sp":
                    for k, v in final:
                        engobj.wait_ge(sems[k], v)

            @block.tensor
            def _(eng):
                run("pe", eng)

            @block.scalar
            def _(eng):
                run("act", eng)

            @block.vector
            def _(eng):
                run("dve", eng)

            @block.gpsimd
            def _(eng):
                run("pool", eng)

            @block.sync
            def _(eng):
                run("sp", eng)


def _consts():
    bf = ml_dtypes.bfloat16
    c = {}
    c["c_identb"] = np.eye(128, dtype=np.float32).astype(bf)
    c["c_identf"] = np.eye(128, dtype=np.float32)
    c["c_onesb"] = np.ones((128, 128), np.float32).astype(bf)
    c["c_onesf"] = np.ones((128, 128), np.float32)
    sel = np.zeros((128, 64, 128), np.float32)
    selT = np.zeros((128, 64, 128), np.float32)
    for g8 in range(8):
        for j in range(8):
            for h in range(16):
                sel[g8 * 16 + h, g8 * 8 + j, j * 16 + h] = 1.0
                selT[j * 16 + h, j * 8 + g8, g8 * 16 + h] = 1.0
    c["c_sel"] = sel.reshape(128, 64 * 128).astype(bf)
    c["c_negbig"] = (-30000.0 * np.eye(128, dtype=np.float32)).astype(bf)
    k = np.arange(128)[:, None]
    l = np.arange(128)[None, :]
    gt = (k > l).astype(np.float32)
    c["c_gt"] = np.concatenate([gt, gt], axis=1).astype(bf)
    oh = np.zeros((24, 24, 128), np.float32)
    for hd in range(24):
        oh[hd, hd, :] = 1.0
    rm = np.ones((24, TB), np.float32)
    rm[:, 0::128] = 0.0
    c["c_rmask"] = rm
    jp = (np.arange(128) // 16)[:, None]
    jj = (np.arange(128) // 16)[None, :]
    c["c_bmask"] = (jj >= jp).astype(np.float32)
    kv = np.zeros((128, 16, 32), np.float32)
    for ki in range(16):
        kv[:, ki, :] = ki - 7
    c["c_kv"] = kv.reshape(128, 512)
    cv = np.zeros((128, 32, 32), np.float32)
    cpos = np.ones((128, 32, 32), np.float32)
    for cc in range(32):
        cv[:, :, cc] = 8.0 * cc
    cpos[:, :, 0] = 0.0
    c["c_cv"] = cv.reshape(128, 1024)
    c["c_cpos"] = cpos.reshape(128, 1024)
    sg = np.zeros((128, 2), np.float32)
    sg[:64, 0] = -1.0
    sg[64:, 0] = 1.0
    sg[:64, 1] = 1.0
    sg[64:, 1] = -1.0
    c["c_sg"] = sg
    return c


_CONST_SHAPES = None


def build(NSEQ, NBLK, stop=None, dbg=False):
    NB = NSEQ * NBLK
    NTOK = NB * TB
    nc = bass.Bass("TRN2", target_bir_lowering=False)

    def din(name, shape, dt=F32):
        return nc.dram_tensor(name, list(shape), dt, kind="ExternalInput").ap()

    x_d = din("x", [NTOK, 1024])
    p_d = din("p", [NTOK, 256])
    out_d = nc.dram_tensor("out", [NTOK, 1024], F32, kind="ExternalOutput").ap()
    norm_w_d = din("norm_w", [8, 128])
    w_in_d = din("w_in", [1024, 7192])
    a_re_d = din("s5_a_re", [32, 64])
    a_im_d = din("s5_a_im", [32, 64])
    b_re_d = din("s5_b_re", [32, 64, 16])
    b_im_d = din("s5_b_im", [32, 64, 16])
    c_re_d = din("s5_c_re", [512, 64])
    c_im_d = din("s5_c_im", [512, 64])
    s5_d_d = din("s5_d", [4, 128])
    lstep_d = din("s5_log_step", [32])
    w_glu_d = din("s5_w_glu", [512, 512])
    b_glu_d = din("s5_b_glu", [4, 128])
    conv_w_d = din("ssd_conv_w", [80, 128])
    conv_b_d = din("ssd_conv_b", [20, 128])
    dt_bias_d = din("ssd_dt_bias", [24, 1])
    a_log_d = din("ssd_a_log", [24, 1])
    ssd_d_d = din("ssd_d", [24])
    ssd_nw_d = din("ssd_norm_w", [12, 128])
    w_br5_d = din("w_br_s5", [512, 1024])
    w_brs_d = din("w_br_ssd", [1536, 1024])
    w_out_d = din("w_out", [1024, 1024])
    ple_nw_d = din("ple_norm_w", [8, 128])
    w_pg_d = din("w_ple_gate", [1024, 1024])
    w_pp_d = din("w_ple_proj", [256, 1024])
    fnw_d = din("final_norm_w", [1024])
    cd = {}
    for name, arr in _consts().items():
        cd[name] = din(name, arr.shape, BF16 if arr.dtype == ml_dtypes.bfloat16 else F32)

    scr = {
        "w_in": nc.dram_tensor("scr_w_in", [1024, 7192], BF16, kind="ExternalOutput").ap(),
        "w_brs": nc.dram_tensor("scr_w_brs", [1536, 1024], BF16, kind="ExternalOutput").ap(),
        "w_out": nc.dram_tensor("scr_w_out", [1024, 1024], BF16, kind="ExternalOutput").ap(),
        "w_pg": nc.dram_tensor("scr_w_pg", [1024, 1024], BF16, kind="ExternalOutput").ap(),
    }

    with ExitStack() as st:
        def sb(name, shape, dt=F32):
            return st.enter_context(nc.sbuf_tensor(name, list(shape), dt))

        def psum(name, shape, dt):
            return st.enter_context(nc.psum_tensor(name, list(shape), dt))

        P = Prog(nc)
        P.alias["xt1"] = [("xt1", 0), ("xt1", 1)]
        psf = [psum("psf%d" % i, [128, 512], F32) for i in range(4)]
        pst = [psum("pst%d" % i, [128, 1024], BF16) for i in range(2)]
        pscs = [psum("psc%d" % i, [128, 512], F32) for i in range(2)]
        st_ = {"f": 0, "t": 0, "w": 0, "cast": 0, "half": 0}

        def nbf():
            i = st_["f"]
            st_["f"] = (i + 1) % 4
            return psf[i], "psf%d" % i

        def nbt():
            i = st_["t"]
            st_["t"] = (i + 1) % 2
            return pst[i], "pst%d" % i

        def mm(out, lhsT, rhs, start, stop, reads, writes):
            P.op("pe", lambda e: e.matmul(out, lhsT=lhsT, rhs=rhs, start=start, stop=stop), reads, writes)

        def tr(out, in_, ident, reads, writes):
            P.op("pe", lambda e: e.transpose(out=out, in_=in_, identity=ident), reads, writes)

        def act(out, in_, func, reads, writes, bias=None, scale=None, accum=None):
            kw = {}
            if bias is not None:
                kw["bias"] = bias
            if scale is not None:
                kw["scale"] = scale
            if accum is not None:
                kw["accum_out"] = accum
            P.op("act", lambda e: e.activation(out=out, in_=in_, func=func, **kw), reads, writes)

        def eng_of(eng):
            return eng

        def tt(eng, out, in0, in1, op, reads, writes):
            P.op(eng, lambda e: e.tensor_tensor(out=out, in0=in0, in1=in1, op=op), reads, writes)

        def ts(eng, out, in0, s1, s2, op0, op1, reads, writes):
            if op1 is None:
                P.op(eng, lambda e: e.tensor_scalar(out=out, in0=in0, scalar1=s1, scalar2=None, op0=op0), reads, writes)
            else:
                P.op(eng, lambda e: e.tensor_scalar(out=out, in0=in0, scalar1=s1, scalar2=s2, op0=op0, op1=op1), reads, writes)

        def stt(eng, out, in0, scalar, in1, op0, op1, reads, writes):
            P.op(eng, lambda e: e.scalar_tensor_tensor(out=out, in0=in0, scalar=scalar, in1=in1, op0=op0, op1=op1), reads, writes)

        def cp(eng, out, in_, reads, writes):
            if eng == "act":
                P.op("act", lambda e: e.copy(out=out, in_=in_), reads, writes)
            else:
                P.op(eng, lambda e: e.tensor_copy(out=out, in_=in_), reads, writes)

        def mset(eng, out, val, writes):
            P.op(eng, lambda e: e.memset(out, val), (), writes)

        identb = sb("identb", [128, 128], BF16)
        identf = sb("identf", [128, 128], F32)
        onesb = sb("onesb", [128, 128], BF16)
        onesf = sb("onesf", [128, 128], F32)
        sel = sb("sel", [128, 64, 128], BF16)
        negbig = sb("negbig", [128, 128], BF16)
        gtm = sb("gtm", [128, 256], BF16)
        rmask = sb("rmask", [24, TB], F32)
        sgc = sb("sgc", [128, 2], F32)
        for t_, d_ in ((identb, "c_identb"), (identf, "c_identf"), (onesb, "c_onesb"), (onesf, "c_onesf"),
                       (negbig, "c_negbig"), (gtm, "c_gt"), (rmask, "c_rmask"), (sgc, "c_sg")):
            P.dma("sp", t_[:], cd[d_], writes=[d_])
        P.dma("sp", sel[:].rearrange("p a b -> p (a b)"), cd["c_sel"], writes=["c_sel"])

        xt = [sb("xt%d" % i, [128, 2, 1024], F32) for i in range(2)]
        xnb = [sb("xnb%d" % i, [128, 1024], BF16) for i in range(2)]
        hnTs = [sb("hnT%d" % i, [128, 8, TB], BF16) for i in range(2)]
        wsl = [sb("wsl%d" % i, [128, 8, 256], BF16) for i in range(NWSL)]
        wglu = sb("wglu", [128, 4, 512], BF16)
        wbr5 = sb("wbr5", [128, 4, 1024], BF16)
        wpp = sb("wpp", [128, 2, 1024], BF16)
        vecT = sb("vecT", [128, 136], F32)
        fnw = sb("fnw", [128, 1024], F32)
        dtb = sb("dtb", [24, 1], F32)
        aneg = sb("aneg", [24, 1], F32)
        dch = sb("dch", [128, 12], F32)
        dsk = sb("dsk", [128, 12, 128], BF16)
        WA = sb("WA", [128, 32, 128], BF16)
        TOE = sb("TOE", [128, 32, 128], BF16)
        WC = sb("WC", [128, 32, 128], BF16)
        COSM = sb("COSM", [128, 32, 32], F32)
        SGNM = sb("SGNM", [128, 32, 32], F32)
        RHO0 = sb("RHO0", [128, 32, 32], F32)
        L8RE = sb("L8RE", [128, 32], F32)
        L8SG = sb("L8SG", [128, 32], F32)
        XCS = sb("XCS", [128, 32], F32)
        XCW = sb("XCW", [128, 32], F32)
        ssq = sb("ssq", [128, 8], F32)
        rst = sb("rst", [128, 8], F32)
        ARENA_W = 15 * 1024 + 512
        arena = sb("arena", [128, ARENA_W], F32)

        class Arena:
            def __init__(self):
                self.off = 0

            def reset(self):
                self.off = 0

            def get(self, shape, dt, name, parts=None):
                n = 1
                for s_ in shape[1:]:
                    n *= s_
                words = (n * (2 if dt == BF16 else 4) + 3) // 4
                assert self.off + words <= ARENA_W, (name, self.off, words)
                g0 = self.off // 256
                g1 = (self.off + words - 1) // 256
                P.alias[name] = [("ar", k) for k in range(g0, g1 + 1)]
                if parts:
                    for pi in range(parts):
                        w0 = self.off + (words * pi) // parts
                        w1 = self.off + (words * (pi + 1)) // parts - 1
                        P.alias[(name, pi)] = [("ar", k) for k in range(w0 // 256, w1 // 256 + 1)]
                a = arena[0:shape[0], self.off:self.off + words]
                if dt != F32:
                    a = a.bitcast(dt)
                self.off += words
                if len(shape) == 3:
                    a = a.rearrange("p (a b) -> p a b", a=shape[1])
                elif len(shape) == 4:
                    a = a.rearrange("p (a b c) -> p a b c", a=shape[1], b=shape[2])
                return a

        AR = Arena()

        vs0 = AR.get([128, 128], F32, "vs0")
        vs1 = AR.get([128, 128], F32, "vs1")
        mset("dve", vs0, 0.0, ["vs0"])
        mset("dve", vs1, 0.0, ["vs1"])
        P.dma("sp", vs0[0:80, :], conv_w_d, reads=["vs0"], writes=["vs0"])
        P.dma("sp", vs0[80:100, :], conv_b_d, reads=["vs0"], writes=["vs0"])
        P.dma("sp", vs0[100:112, :], ssd_nw_d, reads=["vs0"], writes=["vs0"])
        P.dma("sp", vs0[112:120, :], norm_w_d, reads=["vs0"], writes=["vs0"])
        P.dma("sp", vs0[120:128, :], ple_nw_d, reads=["vs0"], writes=["vs0"])
        P.dma("sp", vs1[0:4, :], s5_d_d, reads=["vs1"], writes=["vs1"])
        P.dma("sp", vs1[4:8, :], b_glu_d, reads=["vs1"], writes=["vs1"])
        bk, bkk = nbf()
        tr(bk[:, 0:128], vs0, identf[:], ["vs0", "c_identf"], [bkk])
        tr(bk[:, 128:136], vs1[0:8, :], identf[0:8, 0:8], ["vs1", "c_identf"], [bkk])
        cp("dve", vecT[:], bk[:, 0:136], [bkk], ["vecT"])
        CW = lambda i, k: vecT[:, k * 20 + i:k * 20 + i + 1]
        CB = lambda i: vecT[:, 80 + i:81 + i]
        SNW = lambda i: vecT[:, 100 + i:101 + i]
        NW = lambda kt: vecT[:, 112 + kt:113 + kt]
        PNW = lambda kt: vecT[:, 120 + kt:121 + kt]
        D5 = lambda ft: vecT[:, 128 + ft:129 + ft]
        BGL = lambda ft: vecT[:, 132 + ft:133 + ft]
        with nc.allow_non_contiguous_dma(reason="tiny param loads"):
            P.dma("sp", fnw[:], fnw_d.partition_broadcast(128), writes=["fnw"])
            P.dma("sp", dtb[:], dt_bias_d, writes=["dtb"])
            P.dma("sp", aneg[:], a_log_d, writes=["aneg"])
            sdv = ssd_d_d.rearrange("(i two) -> two i", two=2)
            P.dma("sp", dch[0:64, :], sdv[0].partition_broadcast(64), writes=["dch"])
            P.dma("sp", dch[64:128, :], sdv[1].partition_broadcast(64), reads=["dch"], writes=["dch"])
        act(aneg[:], aneg[:], AF.Exp, ["aneg"], ["aneg"])
        ts("dve", aneg[:], aneg[:], -1.0, None, ALU.mult, None, ["aneg"], ["aneg"])
        for i in range(12):
            ts("dve", dsk[:, i, :], identb[:], dch[:, i:i + 1], None, ALU.mult, None, ["c_identb", "dch"], ["dsk"])

        AR.reset()
        A2 = lambda n, s, d=F32, parts=None: AR.get(s, d, n, parts)
        SKIP_S5 = stop in ('S0',)
        SKIP_W = stop in ('S0', 'S1')
        bmaskt = sb("bmaskt", [128, 128], F32)
        P.dma("sp", bmaskt[:], cd["c_bmask"], writes=["bmask"])
        y5T = sb("y5T", [128, 4, TB], BF16)
        yssT = sb("yssT", [128, 12, TB], BF16)
        prevS = sb("prevS", [128, 1536], F32)
        hist = sb("hist", [128, 20, 4], BF16)

        def _s5_setup():
            aTre = A2("aTre", [128, 32]); aTim = A2("aTim", [128, 32]); stp = A2("stp", [128, 32])
            ars = A2("ars", [128, 32]); ais = A2("ais", [128, 32]); r8 = A2("r8", [128, 32])
            anat = A2("anat", [32, 2, 128])
            for h in range(2):
                P.dma("sp", anat[:, 0, 64 * h:64 * h + 64], a_re_d, reads=["anat"] if h else [], writes=["anat"])
                P.dma("sp", anat[:, 1, 64 * h:64 * h + 64], a_im_d, reads=["anat"], writes=["anat"])
            bka, bkak = nbf()
            tr(bka[:, 0:32], anat[:, 0, :], identf[0:32, 0:32], ["anat", "c_identf"], [bkak])
            tr(bka[:, 32:64], anat[:, 1, :], identf[0:32, 0:32], ["anat", "c_identf"], [bkak])
            cp("dve", aTre, bka[:, 0:32], [bkak], ["aTre"])
            cp("dve", aTim, bka[:, 32:64], [bkak], ["aTim"])
            P.dma("sp", stp, lstep_d.partition_broadcast(128), writes=["stp"])
            act(stp, stp, AF.Exp, ["stp"], ["stp"])
            tt("dve", ars, aTre, stp, ALU.mult, ["aTre", "stp"], ["ars"])
            tt("dve", ais, aTim, stp, ALU.mult, ["aTim", "stp"], ["ais"])

            def range_reduce(eng, xin, shape, key, tmpname):
                y = AR.get(shape, F32, tmpname + "y")
                yi = AR.get(shape, I32, tmpname + "i")
                ky, ki_ = tmpname + "y", tmpname + "i"
                ts(eng, y, xin, 1.0 / TWO_PI, None, ALU.mult, None, [key], [ky])
                cp(eng, yi, y, [ky], [ki_])
                cp(eng, y, yi, [ki_], [ky])
                stt(eng, xin, y, -TWO_PI, xin, ALU.mult, ALU.add, [ky, key], [key])
                ts(eng, y, xin, PI, None, ALU.is_gt, None, [key], [ky])
                stt(eng, xin, y, -TWO_PI, xin, ALU.mult, ALU.add, [ky, key], [key])
                ts(eng, y, xin, -PI, None, ALU.is_lt, None, [key], [ky])
                stt(eng, xin, y, TWO_PI, xin, ALU.mult, ALU.add, [ky, key], [key])
                ts(eng, xin, xin, -PI, PI, ALU.max, ALU.min, [key], [key])

            mark0 = AR.off
            cvt = A2("cvt", [128, 32, 32]); cpos = A2("cpos", [128, 32, 32])
            AM = A2("AM", [128, 32, 32]); AM2 = A2("AM2", [128, 32, 32])
            P.dma("sp", cvt.rearrange("p a b -> p (a b)"), cd["c_cv"], writes=["cvt"])
            P.dma("sp", cpos.rearrange("p a b -> p (a b)"), cd["c_cpos"], writes=["cpos"])
            bcc = lambda a: a.unsqueeze(2).to_broadcast([128, 32, 32])
            tt("dve", AM, cvt, bcc(ais), ALU.mult, ["cvt", "ais"], ["AM"])
            ts("dve", AM2, AM, PI / 2, None, ALU.add, None, ["AM"], ["AM2"])
            mk = AR.off
            range_reduce("dve", AM, [128, 32, 32], "AM", "rra")
            AR.off = mk
            range_reduce("dve", AM2, [128, 32, 32], "AM2", "rra")
            act(AM, AM, AF.Sin, ["AM"], ["AM"])
            act(COSM[:], AM2, AF.Sin, ["AM2"], ["COSM"])
            ts("dve", SGNM[:], AM, sgc[:, 1:2], None, ALU.mult, None, ["AM", "c_sg"], ["SGNM"])
            act(r8, ars, AF.Exp, ["ars"], ["r8"], scale=8.0)
            tt("dve", RHO0[:], cpos, bcc(r8), ALU.mult, ["cpos", "r8"], ["RHO0"])
            AR.off = mark0
            kvt = A2("kvt", [128, 16, 32])
            P.dma("sp", kvt.rearrange("p a b -> p (a b)"), cd["c_kv"], writes=["kvt"])
            MAG = A2("MAG", [128, 16, 32]); ANG = A2("ANG", [128, 16, 32]); ANG2 = A2("ANG2", [128, 16, 32])
            PRE = A2("PRE", [128, 16, 32]); PIM = A2("PIM", [128, 16, 32]); PIMS = A2("PIMS", [128, 16, 32])
            bc16 = lambda a: a.unsqueeze(1).to_broadcast([128, 16, 32])
            tt("dve", MAG, kvt, bc16(ars), ALU.mult, ["kvt", "ars"], ["MAG"])
            act(MAG, MAG, AF.Exp, ["MAG"], ["MAG"])
            tt("dve", ANG, kvt, bc16(ais), ALU.mult, ["kvt", "ais"], ["ANG"])
            ts("dve", ANG2, ANG, PI / 2, None, ALU.add, None, ["ANG"], ["ANG2"])
            mk = AR.off
            range_reduce("dve", ANG, [128, 16, 32], "ANG", "rrb")
            AR.off = mk
            range_reduce("dve", ANG2, [128, 16, 32], "ANG2", "rrb")
            AR.off = mk
            act(ANG, ANG, AF.Sin, ["ANG"], ["ANG"])
            act(ANG2, ANG2, AF.Sin, ["ANG2"], ["ANG2"])
            tt("dve", PRE, MAG, ANG2, ALU.mult, ["MAG", "ANG2"], ["PRE"])
            tt("dve", PIM, MAG, ANG, ALU.mult, ["MAG", "ANG"], ["PIM"])
            ts("dve", PIMS, PIM, sgc[:, 0:1], None, ALU.mult, None, ["PIM", "c_sg"], ["PIMS"])
            cp("dve", L8RE[:], PRE[:, 15, :], ["PRE"], ["L8RE"])
            cp("dve", L8SG[:], PIMS[:, 15, :], ["PIMS"], ["L8SG"])
            nre = A2("nre", [128, 32]); den = A2("den", [128, 32]); t0 = A2("t0", [128, 32]); t1 = A2("t1", [128, 32])
            fre = A2("fre", [128, 32]); fim = A2("fim", [128, 32])
            ts("dve", nre, PRE[:, 8, :], -1.0, None, ALU.add, None, ["PRE"], ["nre"])
            tt("dve", den, aTre, aTre, ALU.mult, ["aTre"], ["den"])
            tt("dve", t0, aTim, aTim, ALU.mult, ["aTim"], ["t0"])
            tt("dve", den, den, t0, ALU.add, ["den", "t0"], ["den"])
            P.op("dve", lambda e: e.reciprocal(out=den, in_=den), P._expand(["den"]), P._expand(["den"]))
            tt("dve", t0, nre, aTre, ALU.mult, ["nre", "aTre"], ["t0"])
            tt("dve", t1, PIM[:, 8, :], aTim, ALU.mult, ["PIM", "aTim"], ["t1"])
            tt("dve", t0, t0, t1, ALU.add, ["t0", "t1"], ["t0"])
            tt("dve", fre, t0, den, ALU.mult, ["t0", "den"], ["fre"])
            tt("dve", t0, PIM[:, 8, :], aTre, ALU.mult, ["PIM", "aTre"], ["t0"])
            tt("dve", t1, nre, aTim, ALU.mult, ["nre", "aTim"], ["t1"])
            tt("dve", t0, t0, t1, ALU.subtract, ["t0", "t1"], ["t0"])
            tt("dve", fim, t0, den, ALU.mult, ["t0", "den"], ["fim"])
            bre = A2("bre", [128, 32, 16]); bim = A2("bim", [128, 32, 16])
            BBa = A2("BBa", [128, 32, 16]); BBb = A2("BBb", [128, 32, 16])
            u0 = A2("u0", [128, 32, 16]); u1 = A2("u1", [128, 32, 16])
            with nc.allow_non_contiguous_dma(reason="param loads"):
                for h in range(2):
                    for g4 in range(4):
                        gsl = slice(g4 * 8, g4 * 8 + 8)
                        P.dma("sp", bre[64 * h:64 * h + 64, gsl, :], b_re_d[gsl].rearrange("g p h -> p g h"), reads=["bre"], writes=["bre"])
                        P.dma("sp", bim[64 * h:64 * h + 64, gsl, :], b_im_d[gsl].rearrange("g p h -> p g h"), reads=["bim"], writes=["bim"])
            bch = lambda a: a.unsqueeze(2).to_broadcast([128, 32, 16])
            tt("dve", u0, bre, bch(fre), ALU.mult, ["bre", "fre"], ["u0"])
            tt("dve", u1, bim, bch(fim), ALU.mult, ["bim", "fim"], ["u1"])
            tt("dve", u0, u0, u1, ALU.subtract, ["u0", "u1"], ["u0"])
            tt("dve", u1, bim, bch(fre), ALU.mult, ["bim", "fre"], ["u1"])
            tt("dve", BBb, bre, bch(fim), ALU.mult, ["bre", "fim"], ["BBb"])
            tt("dve", u1, u1, BBb, ALU.add, ["u1", "BBb"], ["u1"])
            cp("dve", BBa[0:64], u0[0:64], ["u0"], ["BBa"])
            cp("dve", BBa[64:128], u1[64:128], ["u1", "BBa"], ["BBa"])
            cp("dve", BBb[0:64], u1[0:64], ["u1", "BBb"], ["BBb"])
            cp("dve", BBb[64:128], u0[64:128], ["u0", "BBb"], ["BBb"])
            cnr = A2("cnr", [128, 4, 128]); cni = A2("cni", [128, 4, 128])
            Ca = A2("Ca", [128, 32, 16]); Cb = A2("Cb", [128, 32, 16])
            with nc.allow_non_contiguous_dma(reason="param loads"):
                for h in range(2):
                    P.dma("sp", cnr[:, :, 64 * h:64 * h + 64], c_re_d.rearrange("(t q) p -> q t p", q=128), reads=["cnr"] if h else [], writes=["cnr"])
                    P.dma("sp", cni[:, :, 64 * h:64 * h + 64], c_im_d.rearrange("(t q) p -> q t p", q=128), reads=["cni"] if h else [], writes=["cni"])
            bkr, bkrk = nbf()
            bki, bkik = nbf()
            for t in range(4):
                tr(bkr[:, t * 128:(t + 1) * 128], cnr[:, t, :], identf[:], ["cnr", "c_identf"], [bkrk])
                tr(bki[:, t * 128:(t + 1) * 128], cni[:, t, :], identf[:], ["cni", "c_identf"], [bkik])
            v3 = lambda a: a.rearrange("p (g h) -> p g h", h=16)
            cp("dve", Ca[0:64], v3(bkr[0:64, :]), [bkrk], ["Ca"])
            ts("dve", Ca[64:128], v3(bki[64:128, :]), -1.0, None, ALU.mult, None, [bkik, "Ca"], ["Ca"])
            cp("dve", Cb[0:64], v3(bki[0:64, :]), [bkik], ["Cb"])
            cp("dve", Cb[64:128], v3(bkr[64:128, :]), [bkrk, "Cb"], ["Cb"])
            T1 = xt[0][:].rearrange("p a b -> p (a b)").rearrange("p (g j h) -> p g j h", g=16, j=8)
            T2 = xt[1][:].rearrange("p a b -> p (a b)").rearrange("p (g j h) -> p g j h", g=16, j=8)
            bch16 = lambda a: a.unsqueeze(2).to_broadcast([128, 16, 16])

            def table(dst, dk, koff, X0, X1, IM, opx, gh):
                gs = slice(gh * 16, gh * 16 + 16)
                for j in range(8):
                    kA = koff(j) + 7
                    e1 = "dve" if (j % 2 == 0) else "pool"
                    a0 = u0[:, 0:16, :] if e1 == "dve" else u0[:, 16:32, :]
                    a1 = u1[:, 0:16, :] if e1 == "dve" else u1[:, 16:32, :]
                    k0 = "u0" + e1
                    k1 = "u1" + e1
                    tt(e1, a0, X0[:, gs, :], bch16(PRE[:, kA, gs]), ALU.mult, ["BBa", "Ca", "PRE", "u0"], ["u0", k0])
                    tt(e1, a1, X1[:, gs, :], bch16(IM[:, kA, gs]), ALU.mult, ["BBb", "Cb", "PIM", "PIMS", "u1"], ["u1", k1])
                    tt(e1, dst[:, :, j, :], a0, a1, opx, ["u0", "u1", k0, k1], [dk])

            for gh in range(2):
                table(T1, "xt0", lambda j: j + 1, Ca, Cb, PIM, ALU.subtract, gh)
                cp("dve", WC[:, gh * 16:gh * 16 + 16, :], T1.rearrange("p g j h -> p g (j h)"), ["xt0"], ["WC"])
                table(T2, "xt1", lambda j: 7 - j, BBa, BBb, PIMS, ALU.add, gh)
                for q4 in range(4):
                    bA, bAk = nbf()
                    for gl in range(4):
                        tr(bA[:, gl * 128:(gl + 1) * 128], T2[:, q4 * 4 + gl].rearrange("p j h -> p (j h)"), identf[:], ["xt1", "c_identf"], [bAk])
                    g0 = gh * 16 + q4 * 4
                    cp("dve", WA[:, g0:g0 + 4, :], bA[:].rearrange("p (g m) -> p g m", g=4), [bAk], ["WA"])
                table(T1, "xt0", lambda j: -j, BBa, BBb, PIMS, ALU.add, gh)
                table(T2, "xt1", lambda j: j, Ca, Cb, PIM, ALU.subtract, gh)
                for q4 in range(4):
                    bT, bTk = nbf()
                    for gl in range(4):
                        mm(bT[:, gl * 128:(gl + 1) * 128], T1[:, q4 * 4 + gl].rearrange("p j h -> p (j h)"),
                           T2[:, q4 * 4 + gl].rearrange("p j h -> p (j h)"), True, True, ["xt0", "xt1"], [bTk])
                    g0 = gh * 16 + q4 * 4
                    tt("dve", TOE[:, g0:g0 + 4, :], bT[:].rearrange("p (g m) -> p g m", g=4),
                       bmaskt[:].unsqueeze(1).to_broadcast([128, 4, 128]), ALU.mult, [bTk, "bmask"], ["TOE"])

        if not SKIP_S5:
            _s5_setup()
        NSTG = 4
        PW = 1024
        pend = []
        stg = [A2("stg%d" % i, [128, PW]) for i in range(NSTG)]
        stb = [A2("stb%d" % i, [128, PW], BF16) for i in range(NSTG)]

        def conv_piece(wd, kt, c0, cw, dst_scr=None, dst_sb=None, name=""):
            i = st_["cast"]
            st_["cast"] += 1
            s_ = i % NSTG
            P.dma("pool", stg[s_][:, 0:cw], wd[kt * 128:(kt + 1) * 128, c0:c0 + cw], writes=["stg%d" % s_])
            if dst_sb is not None:
                while pend:
                    d_, s2_, rk_, wk_ = pend.pop(0)
                    P.dma("pool", d_, s2_, reads=[rk_], writes=[wk_])
                cp("act", dst_sb[:, kt, c0:c0 + cw], stg[s_][:, 0:cw], ["stg%d" % s_], [name])
            else:
                cp("act", stb[s_][:, 0:cw], stg[s_][:, 0:cw], ["stg%d" % s_], ["stb%d" % s_])
                key = ("scr", name, kt, c0 // PW)
                pend.append((dst_scr[kt * 128:(kt + 1) * 128, c0:c0 + cw], stb[s_][:, 0:cw], "stb%d" % s_, key))
            while len(pend) > (NSTG - 2 if dst_sb is None else NSTG - 2):
                d_, s2_, rk_, wk_ = pend.pop(0)
                P.dma("pool", d_, s2_, reads=[rk_], writes=[wk_])

        def convert(wd, K, C, dst_scr=None, dst_sb=None, name=""):
            for c0 in range(0, C, PW):
                cw = min(PW, C - c0)
                for kt in range(K // 128):
                    conv_piece(wd, kt, c0, cw, dst_scr, dst_sb, name)

        if not SKIP_W:
            convert(w_glu_d, 512, 512, dst_sb=wglu, name="wglu")
            convert(w_br5_d, 512, 1024, dst_sb=wbr5, name="wbr5")
            convert(w_pp_d, 256, 1024, dst_sb=wpp, name="wpp")

        src32 = {"w_in": w_in_d, "w_brs": w_brs_d, "w_out": w_out_d, "w_pg": w_pg_d}
        cur = {"blk": 0}

        def wload(name, kt0, nkt, c0, ncol):
            i = st_["w"]
            st_["w"] = (i + 1) % NWSL
            key = ("scrw", name, kt0, c0)
            dsc = scr[name][kt0 * 128:(kt0 + nkt) * 128, c0:c0 + ncol].rearrange("(kt p) c -> p kt c", p=128)
            if cur["blk"] == 0:
                for h in range((ncol + 127) // 128):
                    cw = min(128, ncol - h * 128)
                    hs = st_["half"] % 2
                    st_["half"] += 1
                    stg_ = xt[1][:, hs, :].rearrange("p (k c) -> p k c", k=8)[:, 0:nkt, 0:cw]
                    s32 = src32[name][kt0 * 128:(kt0 + nkt) * 128, c0 + h * 128:c0 + h * 128 + cw].rearrange("(kt p) c -> p kt c", p=128)
                    P.dma("sp", stg_, s32, writes=[("xt1", hs)])
                    ceng = ("act", "pool", "dve", "act")[(st_["half"] - 1) % 4]
                    cp(ceng, wsl[i][:, 0:nkt, h * 128:h * 128 + cw], stg_, [("xt1", hs)], ["wsl%d" % i])
                P.dma("pool", dsc, wsl[i][:, 0:nkt, 0:ncol], reads=["wsl%d" % i], writes=[key])
            else:
                P.dma("sp", wsl[i][:, 0:nkt, 0:ncol], dsc, reads=[key], writes=["wsl%d" % i])
            return wsl[i], "wsl%d" % i

        out_stores = []
        dbg_t = {}

        def dump(name, ap, shape, dt, keys, blk):
            if not dbg:
                return
            if name not in dbg_t:
                dbg_t[name] = nc.dram_tensor("dbg_" + name, [NB] + list(shape), dt, kind="ExternalOutput").ap()
            out_stores.append(P.dma("pool", dbg_t[name][blk], ap, reads=keys))

        def early(Xsrc, XKsrc, t0s=0):
            out_stores.append(P.dma("pool", out_d[t0s:t0s + TB, :].rearrange("(t p) d -> p t d", p=128), Xsrc[:], reads=[XKsrc]))
        for blk in range(NB):
            cur["blk"] = blk
            first = (blk % NBLK) == 0
            t0_ = blk * TB
            xs_ = blk % 2
            X = xt[xs_]
            XK = "xt%d" % xs_
            AR.reset()
            H = hnTs[blk % 2]
            hp = blk % 2

            def norm_T(Xs, XKs, wcol, col0, Hd, hpd):
                for t in range(2):
                    act(xnb[t][:], Xs[:, t, :], AF.Square, [XKs], ["xnb%d" % t, "ssq"], accum=ssq[:, col0 + t:col0 + t + 1])
                ts("dve", rst[:, col0:col0 + 2], ssq[:, col0:col0 + 2], 1.0 / 1024, 1e-6, ALU.mult, ALU.add, ["xnb0", "xnb1", "ssq"], ["rst"])
                act(rst[:, col0:col0 + 2], rst[:, col0:col0 + 2], AF.Sqrt, ["rst"], ["rst"])
                P.op("dve", lambda e: e.reciprocal(out=rst[:, col0:col0 + 2], in_=rst[:, col0:col0 + 2]), P._expand(["rst"]), P._expand(["rst"]))
                for t in range(2):
                    act(xnb[t][:], Xs[:, t, :], AF.Copy, [XKs, "rst"], ["xnb%d" % t], scale=rst[:, col0 + t:col0 + t + 1])
                for half in range(2):
                    bt, btk = nbt()
                    for kl in range(4):
                        for t in range(2):
                            kt = half * 4 + kl
                            tr(bt[:, (kl * 2 + t) * 128:(kl * 2 + t + 1) * 128], xnb[t][:, kt * 128:(kt + 1) * 128], identb[:],
                               ["xnb%d" % t, "c_identb"], [btk])
                    for kl in range(4):
                        kt = half * 4 + kl
                        ts("dve", Hd[:, kt, :], bt[:, kl * 256:(kl + 1) * 256], wcol(kt), None, ALU.mult, None, [btk, "vecT"], [("hnT", hpd, kt)])

            def stage_A(b):
                xs2 = b % 2
                P.dma("sp", xt[xs2][:], x_d[b * TB:(b + 1) * TB, :].rearrange("(t p) d -> p t d", p=128), writes=["xt%d" % xs2])
                norm_T(xt[xs2], "xt%d" % xs2, NW, 0, hnTs[xs2], xs2)

            if stop in ("load", "S0", "S1"):
                P.dma("sp", X[:], x_d[t0_:t0_ + TB, :].rearrange("(t p) d -> p t d", p=128), writes=[XK])
                early(X, XK, t0_)
                continue
            if blk == 0:
                stage_A(0)

            def inproj_pair(c0, width, nt):
                wt, wk = wload("w_in", 0, 8, c0, nt * width)
                bk_, bkk_ = nbf()
                for ti in range(nt):
                    for kt in range(8):
                        mm(bk_[0:width, ti * 256:(ti + 1) * 256], wt[:, kt, ti * width:(ti + 1) * width], H[:, kt, :],
                           kt == 0, kt == 7, [wk, ("hnT", hp, kt)], [bkk_])
                return bk_, bkk_

            dump("hnT", H[:], [128, 8, TB], BF16, [("hnT", hp, kt) for kt in range(8)], blk)
            szT = A2("szT", [128, 12, TB], BF16, 6)
            uT = A2("uT", [128, 4, TB], BF16); sz5 = A2("sz5", [128, 4, TB], BF16)
            for pr in range(2):
                bk_, bkk_ = inproj_pair(pr * 256, 128, 2)
                cp("dve", uT[:, pr * 2:pr * 2 + 2, :], bk_[:].rearrange("p (a b) -> p a b", a=2), [bkk_], ["uT"])
            for pr in range(2):
                bk_, bkk_ = inproj_pair(512 + pr * 256, 128, 2)
                act(sz5[:, pr * 2:pr * 2 + 2, :], bk_[:].rearrange("p (a b) -> p a b", a=2), AF.Silu, [bkk_], ["sz5"])
            Ug = A2("Ug", [128, 32, 32], BF16)
            for ft in range(4):
                bk_, bkk_ = nbf()
                uv = uT[:, ft, :].rearrange("p (c j) -> p j c", j=8)
                for g8 in range(8):
                    for j in range(8):
                        mm(bk_[:, g8 * 32:(g8 + 1) * 32], sel[:, g8 * 8 + j, :], uv[:, j, :], j == 0, j == 7, ["c_sel", "uT"], [bkk_])
                cp("act" if ft % 2 else "dve", Ug[:, ft * 8:(ft + 1) * 8, :], bk_[:, 0:256].rearrange("p (a b) -> p a b", a=8), [bkk_], ["Ug"])
            St = A2("St", [128, 32, 32], F32, 2); Sw = A2("Sw", [128, 32, 32], F32, 2)
            m1 = A2("m1", [128, 16, 32]); m2 = A2("m2", [128, 16, 32])
            if first:
                mset("dve", XCS[:], 0.0, ["XCS"])
                mset("dve", XCW[:], 0.0, ["XCW"])
            for hb in range(2):
                bS, bSk = nbf()
                bW, bWk = nbf()
                for gl in range(16):
                    g = hb * 16 + gl
                    mm(bS[:, gl * 32:(gl + 1) * 32], WA[:, g, :], Ug[:, g, :], True, True, ["WA", "Ug"], [bSk])
                    mm(bW[0:64, gl * 32:(gl + 1) * 32], WA[:, g, 64:128], Ug[:, g, :], True, True, ["WA", "Ug"], [bWk])
                    mm(bW[64:128, gl * 32:(gl + 1) * 32], WA[:, g, 0:64], Ug[:, g, :], True, True, ["WA", "Ug"], [bWk])
                gs = slice(hb * 16, hb * 16 + 16)
                S3 = bS[:].rearrange("p (a b) -> p a b", a=16)
                W3 = bW[:].rearrange("p (a b) -> p a b", a=16)
                tt("dve", m1, S3, COSM[:, gs, :], ALU.mult, [bSk, "COSM"], ["m1"])
                tt("dve", m2, W3, SGNM[:, gs, :], ALU.mult, [bWk, "SGNM"], ["m2"])
                tt("pool", St[:, gs, :], m1, m2, ALU.add, ["m1", "m2"], [("St", hb)])
                tt("dve", m1, W3, COSM[:, gs, :], ALU.mult, [bWk, "COSM"], ["m1"])
                tt("dve", m2, S3, SGNM[:, gs, :], ALU.mult, [bSk, "SGNM"], ["m2"])
                tt("pool", Sw[:, gs, :], m1, m2, ALU.subtract, ["m1", "m2"], [("Sw", hb)])
            for pr in range(6):
                bk_, bkk_ = inproj_pair(1024 + pr * 256, 128, 2)
                act(szT[:, pr * 2:pr * 2 + 2, :], bk_[:].rearrange("p (a b) -> p a b", a=2), AF.Silu, [bkk_], [("szT", pr)])
            c1 = A2("c1", [128, 32]); c2 = A2("c2", [128, 32])
            tt("dve", c1, L8RE[:], XCS[:], ALU.mult, ["L8RE", "XCS"], ["c1"])
            tt("dve", c2, L8SG[:], XCW[:], ALU.mult, ["L8SG", "XCW"], ["c2"])
            tt("dve", c1, c1, c2, ALU.add, ["c1", "c2"], ["c1"])
            tt("dve", St[:, :, 0], St[:, :, 0], c1, ALU.add, [("St", 0), ("St", 1), "c1"], [("St", 0), ("St", 1)])
            tt("dve", c1, L8RE[:], XCW[:], ALU.mult, ["L8RE", "XCW"], ["c1"])
            tt("dve", c2, L8SG[:], XCS[:], ALU.mult, ["L8SG", "XCS"], ["c2"])
            tt("dve", c1, c1, c2, ALU.subtract, ["c1", "c2"], ["c1"])
            tt("dve", Sw[:, :, 0], Sw[:, :, 0], c1, ALU.add, [("Sw", 0), ("Sw", 1), "c1"], [("Sw", 0), ("Sw", 1)])
            Vt = A2("Vt", [128, 32, 32]); Vw = A2("Vw", [128, 32, 32])
            fl = lambda a: a.rearrange("p a b -> p (a b)")
            P.op("dve", lambda e: e.tensor_tensor_scan(out=fl(Vt), data0=fl(RHO0[:]), data1=fl(St), initial=0.0, op0=ALU.mult, op1=ALU.add),
                 P._expand(["RHO0", ("St", 0), ("St", 1)]), P._expand(["Vt"]))
            P.op("dve", lambda e: e.tensor_tensor_scan(out=fl(Vw), data0=fl(RHO0[:]), data1=fl(Sw), initial=0.0, op0=ALU.mult, op1=ALU.add),
                 P._expand(["RHO0", ("Sw", 0), ("Sw", 1)]), P._expand(["Vw"]))
            Xp = A2("Xp", [128, 32, 32], BF16)
            tt("dve", St, Vt, COSM[:], ALU.mult, ["Vt", "COSM", ("St", 0), ("St", 1)], [("St", 0), ("St", 1)])
            tt("pool", Sw, Vw, SGNM[:], ALU.mult, ["Vw", "SGNM", ("Sw", 0), ("Sw", 1)], [("Sw", 0), ("Sw", 1)])
            tt("dve", St, St, Sw, ALU.subtract, [("St", 0), ("St", 1), ("Sw", 0), ("Sw", 1)], [("St", 0), ("St", 1)])
            cp("act", Xp[:, :, 1:32], St[:, :, 0:31], [("St", 0), ("St", 1)], ["Xp"])
            cp("act", Xp[:, :, 0], XCS[:], ["XCS", "Xp"], ["Xp"])
            cp("dve", XCS[:], St[:, :, 31], [("St", 0), ("St", 1), "XCS"], ["XCS"])
            tt("dve", c1, Vw[:, :, 31], COSM[:, :, 31], ALU.mult, ["Vw", "COSM"], ["c1"])
            tt("dve", c2, Vt[:, :, 31], SGNM[:, :, 31], ALU.mult, ["Vt", "SGNM"], ["c2"])
            tt("dve", XCW[:], c1, c2, ALU.add, ["c1", "c2", "XCW"], ["XCW"])
            Yg = A2("Yg", [128, 32, 32], BF16)
            for hb in range(2):
                bY, bYk = nbf()
                for gl in range(16):
                    g = hb * 16 + gl
                    mm(bY[:, gl * 32:(gl + 1) * 32], TOE[:, g, :], Ug[:, g, :], True, False, ["TOE", "Ug"], [bYk])
                    mm(bY[:, gl * 32:(gl + 1) * 32], WC[:, g, :], Xp[:, g, :], False, True, ["WC", "Xp"], [bYk])
                cp("act", Yg[:, hb * 16:hb * 16 + 16, :], bY[:].rearrange("p (a b) -> p a b", a=16), [bYk], ["Yg"])
            yT = A2("yT", [128, 4, TB], BF16, 4)
            ytmp = A2("ytmp", [128, TB])
            for ft in range(4):
                bk_, bkk_ = nbf()
                for j in range(8):
                    for g8 in range(8):
                        mm(bk_[:, j * 32:(j + 1) * 32], sel[:, j * 8 + g8, :], Yg[:, ft * 8 + g8, :], g8 == 0, g8 == 7, ["c_sel", "Yg"], [bkk_])
                stt("dve", ytmp.rearrange("p (c j) -> p j c", j=8), uT[:, ft, :].rearrange("p (c j) -> p j c", j=8), D5(ft),
                    bk_[:, 0:256].rearrange("p (j c) -> p j c", j=8), ALU.mult, ALU.add, ["uT", "vecT", bkk_], ["ytmp"])
                act(yT[:, ft, :], ytmp, AF.Gelu, ["ytmp"], [("yT", ft)])
            dump("uT", uT, [128, 4, TB], BF16, ["uT"], blk)
            dump("Ug", Ug, [128, 32, 32], BF16, ["Ug"], blk)
            dump("Xp", Xp, [128, 32, 32], BF16, ["Xp"], blk)
            dump("Yg", Yg, [128, 32, 32], BF16, ["Yg"], blk)
            dump("yT", yT, [128, 4, TB], BF16, [("yT", f_) for f_ in range(4)], blk)
            sgl = A2("sgl", [128, TB], BF16)
            for ft in range(4):
                bk_, bkk_ = nbf()
                for kt in range(4):
                    mm(bk_[:, 0:256], wglu[:, kt, ft * 128:(ft + 1) * 128], yT[:, kt, :], kt == 0, kt == 3, ["wglu", ("yT", kt)], [bkk_])
                act(sgl, bk_[:, 0:256], AF.Sigmoid, [bkk_, "vecT"], ["sgl"], bias=BGL(ft))
                tt("dve", sgl, sgl, yT[:, ft, :], ALU.mult, ["sgl", ("yT", ft)], ["sgl"])
                tt("pool", y5T[:, ft, :], sgl, sz5[:, ft, :], ALU.mult, ["sgl", "sz5"], [("y5T", ft)])

            dump("y5T", y5T[:], [128, 4, TB], BF16, [("y5T", f_) for f_ in range(4)], blk)
            if stop == "C":
                early(X, XK, t0_)
                continue
            AR.reset()
            szT = A2("szT", [128, 12, TB], BF16, 6)
            cvT = A2("cvT", [128, 20, TB], BF16, 10)
            xraws = [A2("xraw%d" % q, [128, 2, 260], BF16) for q in range(2)]
            accs = [A2("acc%d" % q, [128, 2, TB], F32, 2) for q in range(2)]
            if first:
                mset("pool", hist[:], 0.0, ["hist"])
                mset("pool", prevS[:], 0.0, ["prevS"])
            for pr in range(10):
                q = pr % 2
                xraw = xraws[q]; acc = accs[q]; xk = "xraw%d" % q; ak = "acc%d" % q
                bk_, bkk_ = inproj_pair(2560 + pr * 256, 128, 2)
                cp("pool", xraw[:, :, 0:3], hist[:, pr * 2:pr * 2 + 2, 0:3], ["hist"], [xk])
                cp("act", xraw[:, :, 3:259], bk_[:].rearrange("p (a b) -> p a b", a=2), [bkk_, xk], [xk])
                cp("pool", hist[:, pr * 2:pr * 2 + 2, 0:3], xraw[:, :, 256:259], [xk, "hist"], ["hist"])
                for ti in range(2):
                    i = pr * 2 + ti
                    act(acc[:, ti, :], xraw[:, ti, 0:256], AF.Identity, [xk, "vecT"], [(ak, ti)], bias=CB(i), scale=CW(i, 0))
                for k in (1, 2, 3):
                    for ti in range(2):
                        i = pr * 2 + ti
                        stt("dve", acc[:, ti, :], xraw[:, ti, k:k + 256], CW(i, k), acc[:, ti, :], ALU.mult, ALU.add, [xk, "vecT", (ak, ti)], [(ak, ti)])
                if pr > 0:
                    qp = (pr - 1) % 2
                    act(cvT[:, (pr - 1) * 2:(pr - 1) * 2 + 2, :], accs[qp], AF.Silu, [("acc%d" % qp, 0), ("acc%d" % qp, 1)], [("cvT", pr - 1)])
            act(cvT[:, 18:20, :], accs[1], AF.Silu, [("acc1", 0), ("acc1", 1)], [("cvT", 9)])
            CV = lambda i: ("cvT", i // 2)
            dump("cvT", cvT, [128, 20, TB], BF16, [("cvT", f_) for f_ in range(10)], blk)
            wt, wk = wload("w_in", 0, 8, 5120, 24)
            bk_, bkk_ = nbf()
            for kt in range(8):
                mm(bk_[0:24, 0:256], wt[:, kt, 0:24], H[:, kt, :], kt == 0, kt == 7, [wk, ("hnT", hp, kt)], [bkk_])
            dtT = A2("dtT", [24, TB]); daT = A2("daT", [24, TB]); AT = A2("AT", [24, TB])
            AThi = A2("AThi", [24, TB], BF16); ATlo = A2("ATlo", [24, TB], BF16)
            act(dtT, bk_[0:24, 0:256], AF.Exp, [bkk_, "dtb"], ["dtT"], bias=dtb[:, 0:1])
            act(dtT, dtT, AF.Ln, ["dtT"], ["dtT"], bias=1.0)
            ts("dve", daT, dtT, aneg[:, 0:1], None, ALU.mult, None, ["dtT", "aneg"], ["daT"])
            P.op("dve", lambda e: e.tensor_tensor_scan(out=AT, data0=rmask[:], data1=daT, initial=0.0, op0=ALU.mult, op1=ALU.add),
                 P._expand(["c_rmask", "daT"]), P._expand(["AT"]))
            cp("dve", AThi, AT, ["AT"], ["AThi"])
            tt("dve", ATlo, AT, AThi, ALU.subtract, ["AT", "AThi"], ["ATlo"])
            dump("dtT", dtT, [24, TB], F32, ["dtT"], blk)
            dump("AT", AT, [24, TB], F32, ["AT"], blk)
            dtk = A2("dtk", [128, 2, 24]); nAc = A2("nAc", [128, 2, 24])
            dec = A2("dec", [128, 2, 24]); cdec = A2("cdec", [128, 2, 24]); dtd = A2("dtd", [128, 2, 24])
            rhd = A2("rhd", [24, 2, 24])
            bk_, bkk_ = nbf()
            for ck in range(2):
                tr(bk_[:, ck * 24:(ck + 1) * 24], dtT[:, ck * 128:(ck + 1) * 128], identf[0:24, 0:24], ["dtT", "c_identf"], [bkk_])
                tr(bk_[:, 48 + ck * 24:48 + (ck + 1) * 24], AT[:, ck * 128:(ck + 1) * 128], identf[0:24, 0:24], ["AT", "c_identf"], [bkk_])
                ts("dve", rhd[:, ck, :], identf[0:24, 0:24], AT[:, ck * 128 + 127:ck * 128 + 128], None, ALU.mult, None, ["c_identf", "AT"], ["rhd"])
            mm(bk_[:, 96:144], onesf[0:24, :], rhd.rearrange("p a b -> p (a b)"), True, True, ["c_onesf", "rhd"], [bkk_])
            cp("dve", dtk, bk_[:, 0:48].rearrange("p (a b) -> p a b", a=2), [bkk_], ["dtk"])
            ts("dve", nAc, bk_[:, 48:96].rearrange("p (a b) -> p a b", a=2), -1.0, None, ALU.mult, None, [bkk_], ["nAc"])
            act(cdec, bk_[:, 96:144].rearrange("p (a b) -> p a b", a=2), AF.Exp, [bkk_], ["cdec"])
            tt("dve", dec, bk_[:, 96:144].rearrange("p (a b) -> p a b", a=2), nAc, ALU.add, [bkk_, "nAc"], ["dec"])
            act(dec, dec, AF.Exp, ["dec"], ["dec"])
            tt("dve", dtd, dtk, dec, ALU.mult, ["dtk", "dec"], ["dtd"])
            xdt = A2("xdt", [128, 2, 1536], BF16, 2); xdd = A2("xdd", [128, 2, 1536], BF16, 2)
            Btk = A2("Btk", [128, 2, 512], BF16, 2)
            for ck in range(2):
                cs = slice(ck * 128, (ck + 1) * 128)
                for (i0, n) in ((0, 8), (8, 4)):
                    bt, btk = nbt()
                    for ii in range(n):
                        tr(bt[:, ii * 128:(ii + 1) * 128], cvT[:, i0 + ii, cs], identb[:], [CV(i0 + ii), "c_identb"], [btk])
                    h0 = i0 * 2
                    nh = n * 2
                    src3 = bt[:, 0:n * 128].rearrange("p (h d) -> p h d", d=64)
                    tt("dve", xdt[:, ck, h0 * 64:(h0 + nh) * 64].rearrange("p (h d) -> p h d", d=64), src3,
                       dtk[:, ck, h0:h0 + nh].unsqueeze(2).to_broadcast([128, nh, 64]), ALU.mult, [btk, "dtk"], [("xdt", ck)])
                    tt("dve", xdd[:, ck, h0 * 64:(h0 + nh) * 64].rearrange("p (h d) -> p h d", d=64), src3,
                       dtd[:, ck, h0:h0 + nh].unsqueeze(2).to_broadcast([128, nh, 64]), ALU.mult, [btk, "dtd"], [("xdd", ck)])
                bt, btk = nbt()
                for gq in range(4):
                    tr(bt[:, gq * 128:(gq + 1) * 128], cvT[:, 12 + gq, cs], identb[:], [CV(12 + gq), "c_identb"], [btk])
                cp("act", Btk[:, ck, :], bt[:, 0:512], [btk], [("Btk", ck)])
            pvb = A2("pvb", [128, 2, 1536], BF16, 2)
            for ck in range(2):
                cp("act", pvb[:, ck, :], prevS[:], ["prevS"], [("pvb", ck)])
                tt("pool", prevS[:].rearrange("p (h d) -> p h d", d=64), prevS[:].rearrange("p (h d) -> p h d", d=64),
                   cdec[:, ck, :].unsqueeze(2).to_broadcast([128, 24, 64]), ALU.mult, ["prevS", "cdec"], ["prevS"])
                for gq in range(4):
                    bk_, bkk_ = nbf()
                    mm(bk_[:, 0:384], Btk[:, ck, gq * 128:(gq + 1) * 128], xdd[:, ck, gq * 384:(gq + 1) * 384], True, True,
                       [("Btk", ck), ("xdd", ck)], [bkk_])
                    tt("dve", prevS[:, gq * 384:(gq + 1) * 384], prevS[:, gq * 384:(gq + 1) * 384], bk_[:, 0:384], ALU.add, ["prevS", bkk_], ["prevS"])
            NBUF = 4
            ET = [A2("ET%d" % i, [128, TB], BF16, 2) for i in range(NBUF)]
            EA = [A2("EA%d" % i, [128, TB], BF16) for i in range(NBUF)]
            CD = [A2("CD%d" % i, [128, TB], BF16) for i in range(NBUF)]
            WT = [A2("WT%d" % i, [128, TB], BF16) for i in range(NBUF)]
            ygT = A2("ygT", [128, 3, TB], F32, 3); sq = A2("sq", [128, 3, TB], BF16, 3); rsr = A2("rsr", [128, TB])

            def head_decay(gq, hd, bSc, bSck):
                b_ = hd % NBUF
                bA_, bAk_ = nbf()
                mm(bA_[:, 0:256], identb[0:24, hd:hd + 1].to_broadcast([24, 128]), AThi, True, False, ["c_identb", "AThi"], [bAk_])
                mm(bA_[:, 0:256], identb[0:24, hd:hd + 1].to_broadcast([24, 128]), ATlo, False, True, ["c_identb", "ATlo"], [bAk_])
                mm(bA_[:, 256:512], identb[0:24, hd:hd + 1].to_broadcast([24, 128]), AThi, True, False, ["c_identb", "AThi"], [bAk_])
                mm(bA_[:, 256:512], identb[0:24, hd:hd + 1].to_broadcast([24, 128]), ATlo, False, False, ["c_identb", "ATlo"], [bAk_])
                mm(bA_[:, 256:512], negbig[:], gtm[:], False, True, ["c_negbig", "c_gt"], [bAk_])
                act(EA[b_], bA_[:, 0:256], AF.Exp, [bAk_], ["EA%d" % b_])
                for ck in range(2):
                    cs = slice(ck * 128, (ck + 1) * 128)
                    act(ET[b_][:, cs], bA_[:, 256 + ck * 128:256 + (ck + 1) * 128], AF.Exp, [bAk_, "nAc"], [("ET%d" % b_, ck)],
                        bias=nAc[:, ck, hd:hd + 1])
                tt("dve", WT[b_], bSc[:, 0:256], ET[b_], ALU.mult, [bSck, ("ET%d" % b_, 0), ("ET%d" % b_, 1)], ["WT%d" % b_])
                tt("pool", CD[b_], cvT[:, 16 + gq, :], EA[b_], ALU.mult, [CV(16 + gq), "EA%d" % b_], ["CD%d" % b_])

            def head_pair_y(gq, hd):
                i = hd // 2
                bY_, bYk_ = nbf()
                for ck in range(2):
                    cs = slice(ck * 128, (ck + 1) * 128)
                    mm(bY_[:, cs], dsk[:, i, :], cvT[:, i, cs], True, False, ["dsk", CV(i)], [bYk_])
                    for hh in range(2):
                        h2 = hd - 1 + hh
                        b2 = h2 % NBUF
                        mm(bY_[64 * hh:64 * hh + 64, cs], xdt[:, ck, h2 * 64:(h2 + 1) * 64], WT[b2][:, cs], False, False,
                           [("xdt", ck), "WT%d" % b2], [bYk_])
                        mm(bY_[64 * hh:64 * hh + 64, cs], pvb[:, ck, h2 * 64:(h2 + 1) * 64], CD[b2][:, cs], False, hh == 1,
                           [("pvb", ck), "CD%d" % b2], [bYk_])
                il = i % 3
                tt("dve", ygT[:, il, :], bY_[:, 0:256], szT[:, i, :], ALU.mult, [bYk_, ("szT", i // 2)], [("ygT", il)])
                tt("dve", sq[:, il, :], ygT[:, il, :], ygT[:, il, :], ALU.mult, [("ygT", il)], [("sq", il)])

            for gq in range(4):
                bSc, bSck = pscs[gq % 2], "psc%d" % (gq % 2)
                for ck in range(2):
                    cs = slice(ck * 128, (ck + 1) * 128)
                    mm(bSc[:, cs], cvT[:, 12 + gq, cs], cvT[:, 16 + gq, cs], True, True, [CV(12 + gq), CV(16 + gq)], [bSck])
                h0 = gq * 6
                head_decay(gq, h0, bSc, bSck)
                head_decay(gq, h0 + 1, bSc, bSck)
                head_decay(gq, h0 + 2, bSc, bSck)
                head_decay(gq, h0 + 3, bSc, bSck)
                head_pair_y(gq, h0 + 1)
                head_decay(gq, h0 + 4, bSc, bSck)
                head_decay(gq, h0 + 5, bSc, bSck)
                head_pair_y(gq, h0 + 3)
                head_pair_y(gq, h0 + 5)
                bR, bRk = nbf()
                for il in range(3):
                    mm(bR[:, 0:256], onesb[:], sq[:, il, :], il == 0, il == 2, ["c_onesb", ("sq", il)], [bRk])
                ts("dve", rsr, bR[:, 0:256], 1.0 / 384, 1e-6, ALU.mult, ALU.add, [bRk], ["rsr"])
                act(rsr, rsr, AF.Ln, ["rsr"], ["rsr"])
                act(rsr, rsr, AF.Exp, ["rsr"], ["rsr"], scale=-0.5)
                for il in range(3):
                    i = gq * 3 + il
                    stt("dve", yssT[:, i, :], ygT[:, il, :], SNW(i), rsr, ALU.mult, ALU.mult, [("ygT", il), "vecT", "rsr"], [("yssT", i)])

            dump("yssT", yssT[:], [128, 12, TB], BF16, [("yssT", f_) for f_ in range(12)], blk)
            dump("prevS", prevS[:], [128, 1536], F32, ["prevS"], blk)
            if stop == "D":
                early(X, XK, t0_)
                continue
            AR.reset()
            gT = A2("gT", [128, 16, TB], BF16, 8)
            mgT = A2("mgT", [128, 8, TB], BF16, 8)
            for pr in range(8):
                bk_, bkk_ = inproj_pair(5144 + pr * 256, 128, 2)
                act(gT[:, pr * 2:pr * 2 + 2, :], bk_[:].rearrange("p (a b) -> p a b", a=2), AF.Sigmoid, [bkk_], [("gT", pr)])
            if blk + 1 < NB and blk != 0:
                stage_A(blk + 1)
            e1t = A2("e1t", [128, TB]); e2t = A2("e2t", [128, TB])
            for dp in range(4):
                wa, wak = wload("w_brs", 0, 8, dp * 256, 256)
                wb, wbk = wload("w_brs", 8, 4, dp * 256, 256)
                for dl in range(2):
                    d_ = dp * 2 + dl
                    bk_, bkk_ = nbf()
                    for kt in range(4):
                        mm(bk_[:, 0:256], wbr5[:, kt, d_ * 128:(d_ + 1) * 128], y5T[:, kt, :], kt == 0, kt == 3, ["wbr5", ("y5T", kt)], [bkk_])
                    for kt in range(12):
                        w_, wk_ = (wa, wak) if kt < 8 else (wb, wbk)
                        mm(bk_[:, 256:512], w_[:, kt % 8, dl * 128:(dl + 1) * 128], yssT[:, kt, :], kt == 0, kt == 11, [wk_, ("yssT", kt)], [bkk_])
                    tt("dve", e1t, bk_[:, 0:256], gT[:, d_, :], ALU.mult, [bkk_, ("gT", d_ // 2)], ["e1t"])
                    tt("dve", e2t, bk_[:, 256:512], gT[:, 8 + d_, :], ALU.mult, [bkk_, ("gT", 4 + d_ // 2)], ["e2t"])
                    tt("pool", mgT[:, d_, :], e1t, e2t, ALU.add, ["e1t", "e2t"], [("mgT", d_)])
            MG = [("mgT", d_) for d_ in range(8)]
            dump("mgT", mgT, [128, 8, TB], BF16, MG, blk)
            dump("gT", gT, [128, 16, TB], BF16, [("gT", f_) for f_ in range(8)], blk)
            for dc in range(4):
                wt, wk = wload("w_out", 0, 8, dc * 256, 256)
                bk_, bkk_ = nbf()
                for t in range(2):
                    for kt in range(8):
                        mm(bk_[:, t * 256:(t + 1) * 256], mgT[:, kt, t * 128:(t + 1) * 128], wt[:, kt, :], kt == 0, kt == 7, [wk, ("mgT", kt)], [bkk_])
                tt("dve", X[:, :, dc * 256:(dc + 1) * 256], X[:, :, dc * 256:(dc + 1) * 256], bk_[:].rearrange("p (a b) -> p a b", a=2),
                   ALU.add, [XK, bkk_], [XK])
            norm_T(X, XK, PNW, 2, H, hp)
            pf = A2("pf", [128, 2, 256]); pb = A2("pb", [128, 2, 256], BF16); pT = A2("pT", [128, 2, TB], BF16)
            P.dma("pool", pf, p_d[t0_:t0_ + TB, :].rearrange("(t p) d -> p t d", p=128), writes=["pf"])
            cp("act", pb, pf, ["pf"], ["pb"])
            bt, btk = nbt()
            for kt in range(2):
                for t in range(2):
                    tr(bt[:, (kt * 2 + t) * 128:(kt * 2 + t + 1) * 128], pb[:, t, kt * 128:(kt + 1) * 128], identb[:], ["pb", "c_identb"], [btk])
            cp("dve", pT, bt[:, 0:512].rearrange("p (a b) -> p a b", a=2), [btk], ["pT"])
            sgp = A2("sgp", [128, 2, 256]); e3t = A2("e3t", [128, 2, 256])
            for dc in range(4):
                wt, wk = wload("w_pg", 0, 8, dc * 256, 256)
                bG, bGk = nbf()
                bP, bPk = nbf()
                for t in range(2):
                    for kt in range(8):
                        mm(bG[:, t * 256:(t + 1) * 256], H[:, kt, t * 128:(t + 1) * 128], wt[:, kt, :], kt == 0, kt == 7, [wk, ("hnT", hp, kt)], [bGk])
                    for kt in range(2):
                        mm(bP[:, t * 256:(t + 1) * 256], pT[:, kt, t * 128:(t + 1) * 128], wpp[:, kt, dc * 256:(dc + 1) * 256], kt == 0, kt == 1,
                           ["wpp", "pT"], [bPk])
                act(sgp, bG[:].rearrange("p (a b) -> p a b", a=2), AF.Sigmoid, [bGk], ["sgp"])
                tt("dve", e3t, bP[:].rearrange("p (a b) -> p a b", a=2), sgp, ALU.mult, [bPk, "sgp"], ["e3t"])
                tt("pool", X[:, :, dc * 256:(dc + 1) * 256], X[:, :, dc * 256:(dc + 1) * 256], e3t, ALU.add, [XK, "e3t"], [XK])
            ob = A2("ob", [128, 2, 1024], F32, 2)
            for t in range(2):
                act(xnb[t][:], X[:, t, :], AF.Square, [XK], ["xnb%d" % t, "ssq"], accum=ssq[:, 4 + t:5 + t])
            ts("dve", rst[:, 4:6], ssq[:, 4:6], 1.0 / 1024, 1e-6, ALU.mult, ALU.add, ["xnb0", "xnb1", "ssq"], ["rst"])
            act(rst[:, 4:6], rst[:, 4:6], AF.Sqrt, ["rst"], ["rst"])
            P.op("dve", lambda e: e.reciprocal(out=rst[:, 4:6], in_=rst[:, 4:6]), P._expand(["rst"]), P._expand(["rst"]))
            for t in range(2):
                act(ob[:, t, :], X[:, t, :], AF.Copy, [XK, "rst"], [("ob", t)], scale=rst[:, 4 + t:5 + t])
                tt("dve" if t == 0 else "pool", ob[:, t, :], ob[:, t, :], fnw[:], ALU.mult, [("ob", t), "fnw"], [("ob", t)])
            out_stores.append(P.dma("pool", out_d[t0_:t0_ + TB, :].rearrange("(t p) d -> p t d", p=128), ob, reads=[("ob", 0), ("ob", 1)]))
            if blk == 0 and NB > 1:
                stage_A(1)

        P.emit(final_wait_ops=out_stores)
    return nc


_NC_CACHE = {}


def _prep_shared(inp):
    f = lambda a: np.ascontiguousarray(np.asarray(a, dtype=np.float32))
    sh = {
        "norm_w": f(inp["norm_w"]).reshape(8, 128),
        "w_in": f(inp["w_in"]).reshape(1024, 7192),
        "s5_a_re": f(inp["s5_a_re"]).reshape(32, 64),
        "s5_a_im": f(inp["s5_a_im"]).reshape(32, 64),
        "s5_b_re": f(inp["s5_b_re"]).reshape(32, 64, 16),
        "s5_b_im": f(inp["s5_b_im"]).reshape(32, 64, 16),
        "s5_c_re": f(inp["s5_c_re"]).reshape(512, 64),
        "s5_c_im": f(inp["s5_c_im"]).reshape(512, 64),
        "s5_d": f(inp["s5_d"]).reshape(4, 128),
        "s5_log_step": f(inp["s5_log_step"]).reshape(32),
        "s5_w_glu": f(inp["s5_w_glu"]).reshape(512, 512),
        "s5_b_glu": f(inp["s5_b_glu"]).reshape(4, 128),
        "ssd_conv_w": f(inp["ssd_conv_w"]).reshape(80, 128),
        "ssd_conv_b": f(inp["ssd_conv_b"]).reshape(20, 128),
        "ssd_dt_bias": f(inp["ssd_dt_bias"]).reshape(24, 1),
        "ssd_a_log": f(inp["ssd_a_log"]).reshape(24, 1),
        "ssd_d": f(inp["ssd_d"]).reshape(24),
        "ssd_norm_w": f(inp["ssd_norm_w"]).reshape(12, 128),
        "w_br_s5": f(inp["w_br_s5"]).reshape(512, 1024),
        "w_br_ssd": f(inp["w_br_ssd"]).reshape(1536, 1024),
        "w_out": f(inp["w_out"]).reshape(1024, 1024),
        "ple_norm_w": f(inp["ple_norm_w"]).reshape(8, 128),
        "w_ple_gate": f(inp["w_ple_gate"]).reshape(1024, 1024),
        "w_ple_proj": f(inp["w_ple_proj"]).reshape(256, 1024),
        "final_norm_w": f(inp["final_norm_w"]).reshape(1024),
    }
    sh.update(_consts())
    return sh


def run(inputs, n_cores=8):
    x = np.asarray(inputs["x"], dtype=np.float32)
    p = np.asarray(inputs["p"], dtype=np.float32)
    B, L, Dm = x.shape
    assert B % n_cores == 0 and L % TB == 0
    NSEQ = B // n_cores
    NBLK = L // TB
    key = (NSEQ, NBLK)
    import os
    if key not in _NC_CACHE:
        _NC_CACHE[key] = build(NSEQ, NBLK, stop=os.environ.get("KSTOP"), dbg=bool(os.environ.get("KDBG")))
    nc = _NC_CACHE[key]
    sh = _prep_shared(inputs)
    xs = x.reshape(n_cores, NSEQ * L, Dm)
    ps = p.reshape(n_cores, NSEQ * L, 256)
    in_maps = []
    for c in range(n_cores):
        m = dict(sh)
        m["x"] = np.ascontiguousarray(xs[c])
        m["p"] = np.ascontiguousarray(ps[c])
        in_maps.append(m)
    res = run_bass_kernel_spmd(nc, in_maps, core_ids=list(range(n_cores)))
    global LAST_RES
    LAST_RES = res.results
    out = np.stack([np.asarray(r["out"], dtype=np.float32) for r in res.results], axis=0)
    return out.reshape(B, L, Dm)


def kernel(**inputs):
    return run(inputs, 8)
```
